# Optimizing a Trainium2 kernel written in Bass

```python
import math
import jax
import jax.numpy as jnp
from jax import lax
import numpy as np

D_MODEL = 1024
BATCH = 8
SEQ = 2048
DEPTH = 2
DEC_BATCH = 128
DEC_SEQ = 1
PAST_LEN = 16384
PAGE_SIZE = 128

N_EVEN = (DEPTH + 1) // 2
N_ODD = DEPTH // 2
SSM_D_INNER = D_MODEL
SSM_HEAD_DIM = 64
SSM_HEADS = SSM_D_INNER // SSM_HEAD_DIM
SSM_GROUPS = 2
SSM_STATE = 64
SSM_CONV = 4
SSM_CHUNK = 128
SSM_CONV_DIM = SSM_D_INNER + 2 * SSM_GROUPS * SSM_STATE
HG_HEADS = 8
HG_DIM = 128
HG_WIDTH = HG_HEADS * HG_DIM
HG_CHUNK = 16
AB_IN = SSM_D_INNER + SSM_CONV_DIM + SSM_HEADS + 4 * HG_WIDTH
AB_OUT = SSM_D_INNER + HG_WIDTH
S5_GROUP = 16
S5_GROUPS = D_MODEL // S5_GROUP
S5_STATE = 64
FFN_DIM = 2816
FFN_CONV = 3
EPS = 1e-6
F32 = jnp.float32

kernel_name = 'hybrid_ssd_hgrn2_s5_convffn_step'


def rmsnorm(x, w):
    xf = x.astype(F32)
    y = xf * lax.rsqrt(jnp.mean(xf * xf, axis=-1, keepdims=True) + EPS)
    return (y * w.astype(F32)).astype(x.dtype)


def causal_dwconv(x, buf, w, b):
    k_w = w.shape[0]
    n = x.shape[1]
    xp = jnp.concatenate([buf.astype(x.dtype), x], axis=1)
    y = b + xp[:, 0:n] * w[0]
    for k in range(1, k_w):
        y = y + xp[:, k:k + n] * w[k]
    return y, xp[:, xp.shape[1] - (k_w - 1):]


def to_chunks(a, c):
    b, n = a.shape[0], a.shape[1]
    return jnp.swapaxes(a.reshape((b, n // c, c) + a.shape[2:]), 0, 1)


def from_chunks(a):
    a = jnp.swapaxes(a, 0, 1)
    return a.reshape((a.shape[0], a.shape[1] * a.shape[2]) + a.shape[3:])


def ssd_scan(x, dt, a, bm, cm, h0):
    b, n, nh, p = x.shape
    g, ns = bm.shape[2], bm.shape[3]
    r = nh // g
    c = math.gcd(n, SSM_CHUNK)
    mask = jnp.tril(jnp.ones((c, c), dtype=bool))[None, :, :, None, None]
    a = a.reshape(g, r)

    def step(h, inp):
        xc, dtc, bc, cc = inp
        cum = jnp.cumsum(dtc * a, axis=1)
        seg = jnp.exp(jnp.where(mask, cum[:, :, None] - cum[:, None], -jnp.inf))
        cb = jnp.einsum('btgn,bsgn->bgts', cc, bc)
        wts = jnp.einsum('bgts,btsgr,bsgr->bgrts', cb, seg, dtc)
        y = jnp.einsum('bgrts,bsgrp->btgrp', wts, xc)
        y = y + jnp.einsum('btgn,bgrpn,btgr->btgrp', cc, h, jnp.exp(cum))
        last = cum[:, -1]
        to_end = jnp.exp(last[:, None] - cum) * dtc
        h = jnp.exp(last)[..., None, None] * h + jnp.einsum('bsgr,bsgn,bsgrp->bgrpn', to_end, bc, xc)
        return h, y

    xs = to_chunks(x.reshape(b, n, g, r, p), c)
    dts = to_chunks(dt.reshape(b, n, g, r), c)
    h, ys = lax.scan(step, h0.reshape(b, g, r, p, ns), (xs, dts, to_chunks(bm, c), to_chunks(cm, c)))
    return from_chunks(ys).reshape(b, n, nh, p), h.reshape(b, nh, p, ns)


def hgrn2_scan(q, logf, k, v, s0):
    n = q.shape[1]
    c = math.gcd(n, HG_CHUNK)
    mask = jnp.tril(jnp.ones((c, c), dtype=bool))[None, :, :, None, None]

    def step(s, inp):
        qc, gc, kc, vc = inp
        cum = jnp.cumsum(gc, axis=1)
        dec = jnp.exp(jnp.where(mask, cum[:, :, None] - cum[:, None], -jnp.inf))
        att = jnp.einsum('bthk,btshk,bshk->bhts', qc, dec, kc)
        o = jnp.einsum('bhts,bshv->bthv', att, vc) + jnp.einsum('bthk,bhkv->bthv', qc * jnp.exp(cum), s)
        last = cum[:, -1]
        s = jnp.exp(last)[..., None] * s + jnp.einsum('bshk,bshv->bhkv', jnp.exp(last[:, None] - cum) * kc, vc)
        return s, o

    s, o = lax.scan(step, s0, (to_chunks(q, c), to_chunks(logf, c), to_chunks(k, c), to_chunks(v, c)))
    return from_chunks(o), s


def complex_affine_combine(e1, e2):
    a1r, a1i, b1r, b1i = e1
    a2r, a2i, b2r, b2i = e2
    return (a2r * a1r - a2i * a1i,
            a2r * a1i + a2i * a1r,
            a2r * b1r - a2i * b1i + b2r,
            a2r * b1i + a2i * b1r + b2i)


def mixer_ab(h, conv_buf, ssm0, hg0, lb, in_w, conv_w, conv_b, dt_bias, a_log, d_skip, ssm_norm_w, hg_norm_w, out_w):
    b, n, _ = h.shape
    proj = h @ in_w
    cuts = np.cumsum([SSM_D_INNER, SSM_CONV_DIM, SSM_HEADS, HG_WIDTH, HG_WIDTH, HG_WIDTH]).tolist()
    z, xbc, dt_raw, q, f_raw, i_in, g = jnp.split(proj, cuts, axis=-1)
    xbc, conv_new = causal_dwconv(xbc, conv_buf, conv_w, conv_b)
    xbc = jax.nn.silu(xbc.astype(F32))
    xs, bm, cm = jnp.split(xbc, [SSM_D_INNER, SSM_D_INNER + SSM_GROUPS * SSM_STATE], axis=-1)
    xs = xs.reshape(b, n, SSM_HEADS, SSM_HEAD_DIM)
    bm = bm.reshape(b, n, SSM_GROUPS, SSM_STATE)
    cm = cm.reshape(b, n, SSM_GROUPS, SSM_STATE)
    dt = jax.nn.softplus(dt_raw.astype(F32) + dt_bias.astype(F32))
    a = -jnp.exp(a_log.astype(F32))
    y, ssm_new = ssd_scan(xs, dt, a, bm, cm, ssm0.astype(F32))
    y = y + d_skip.astype(F32)[:, None] * xs
    y = rmsnorm(y.reshape(b, n, SSM_D_INNER) * jax.nn.silu(z.astype(F32)), ssm_norm_w)
    f_raw = f_raw.astype(F32)
    logf = jnp.log(lb + (1.0 - lb) * jax.nn.sigmoid(f_raw))
    k = (1.0 - lb) * jax.nn.sigmoid(-f_raw)
    hs = (b, n, HG_HEADS, HG_DIM)
    qh = jax.nn.silu(q.astype(F32)).reshape(hs)
    o, hg_new = hgrn2_scan(qh, logf.reshape(hs), k.reshape(hs), i_in.astype(F32).reshape(hs), hg0.astype(F32))
    o = rmsnorm(o, hg_norm_w) * jax.nn.silu(g.astype(F32).reshape(hs))
    mixed = jnp.concatenate([y, o.reshape(b, n, HG_WIDTH)], axis=-1).astype(h.dtype)
    return mixed @ out_w, conv_new, ssm_new, hg_new


def mixer_c(h, s_re0, s_im0, lam_re, lam_im, log_step, b_re, b_im, c_re, c_im, d_skip, glu_w):
    b, n, _ = h.shape
    u = h.astype(F32).reshape(b, n, S5_GROUPS, S5_GROUP)
    lam_re = lam_re.astype(F32)
    lam_im = lam_im.astype(F32)
    step = jnp.exp(log_step.astype(F32))[:, None]
    mag = jnp.exp(lam_re * step)
    ab_re = mag * jnp.cos(lam_im * step)
    ab_im = mag * jnp.sin(lam_im * step)
    den = lam_re * lam_re + lam_im * lam_im
    nr = ab_re - 1.0
    coef_re = (nr * lam_re + ab_im * lam_im) / den
    coef_im = (ab_im * lam_re - nr * lam_im) / den
    b_re = b_re.astype(F32)
    b_im = b_im.astype(F32)
    bb_re = coef_re[..., None] * b_re - coef_im[..., None] * b_im
    bb_im = coef_re[..., None] * b_im + coef_im[..., None] * b_re
    bu_re = jnp.einsum('blgc,gnc->blgn', u, bb_re)
    bu_im = jnp.einsum('blgc,gnc->blgn', u, bb_im)
    s_re0 = s_re0.astype(F32)
    s_im0 = s_im0.astype(F32)
    bu_re = bu_re.at[:, 0].add(ab_re * s_re0 - ab_im * s_im0)
    bu_im = bu_im.at[:, 0].add(ab_re * s_im0 + ab_im * s_re0)
    a_re = jnp.broadcast_to(ab_re, bu_re.shape)
    a_im = jnp.broadcast_to(ab_im, bu_im.shape)
    _, _, x_re, x_im = lax.associative_scan(complex_affine_combine, (a_re, a_im, bu_re, bu_im), axis=1)
    y = (jnp.einsum('gcn,blgn->blgc', c_re.astype(F32), x_re)
         - jnp.einsum('gcn,blgn->blgc', c_im.astype(F32), x_im)
         + d_skip.astype(F32).reshape(S5_GROUPS, S5_GROUP) * u)
    zg = jax.nn.gelu(y.reshape(b, n, D_MODEL)).astype(h.dtype) @ glu_w
    val, gate = jnp.split(zg, 2, axis=-1)
    return val * jax.nn.sigmoid(gate), x_re[:, -1], x_im[:, -1]


def conv_ffn(h, buf, up_w, conv_w, conv_b, down_w):
    up = h @ up_w
    up, buf_new = causal_dwconv(up, buf, conv_w, conv_b)
    val, gate = jnp.split(up, 2, axis=-1)
    return (val * jax.nn.silu(gate)) @ down_w, buf_new


def trunk(x, ssm0, sconv0, hg0, s5r0, s5i0, fconv0, w):
    lbs = jnp.cumsum(jax.nn.softmax(w['hgrn_lb'].astype(F32), axis=0), axis=0)
    n_ssm, n_sconv, n_hg, n_s5r, n_s5i, n_fconv = [], [], [], [], [], []
    for l in range(DEPTH):
        e = l // 2
        hn = rmsnorm(x, w['norm_mix_pre'][l])
        if l % 2 == 0:
            m, sc, ss, sh = mixer_ab(hn, sconv0[e], ssm0[e], hg0[e], lbs[e],
                                     w['ab_in_w'][e], w['ssm_conv_w'][e], w['ssm_conv_b'][e],
                                     w['ssm_dt_bias'][e], w['ssm_a_log'][e], w['ssm_d'][e],
                                     w['ssm_norm_w'][e], w['hgrn_norm_w'][e], w['ab_out_w'][e])
            n_sconv.append(sc.astype(sconv0.dtype))
            n_ssm.append(ss.astype(ssm0.dtype))
            n_hg.append(sh.astype(hg0.dtype))
        else:
            m, sr, si = mixer_c(hn, s5r0[e], s5i0[e], w['s5_lam_re'][e], w['s5_lam_im'][e],
                                w['s5_log_step'][e], w['s5_b_re'][e], w['s5_b_im'][e],
                                w['s5_c_re'][e], w['s5_c_im'][e], w['s5_d'][e], w['s5_glu_w'][e])
            n_s5r.append(sr.astype(s5r0.dtype))
            n_s5i.append(si.astype(s5i0.dtype))
        x = x + rmsnorm(m, w['norm_mix_post'][l])
        hn = rmsnorm(x, w['norm_ffn_pre'][l])
        f, fc = conv_ffn(hn, fconv0[l], w['ffn_up_w'][l], w['ffn_conv_w'][l], w['ffn_conv_b'][l], w['ffn_down_w'][l])
        n_fconv.append(fc.astype(fconv0.dtype))
        x = x + rmsnorm(f, w['norm_ffn_post'][l])
    return x, (jnp.stack(n_ssm), jnp.stack(n_sconv), jnp.stack(n_hg),
               jnp.stack(n_s5r), jnp.stack(n_s5i), jnp.stack(n_fconv))


def setup_inputs(seed: int = 0) -> dict:
    key = jax.random.key(seed)
    ks = iter(jax.random.split(key, 64))

    def nrm(shape, scale):
        return jax.random.normal(next(ks), shape, F32) * scale

    def gain(shape):
        return 1.0 + nrm(shape, 0.05)

    def log_uniform(shape, lo, hi):
        return jax.random.uniform(next(ks), shape, F32, minval=math.log(lo), maxval=math.log(hi))

    x_prompt = nrm((BATCH, SEQ, D_MODEL), 1.0)
    x_sample = nrm((DEC_BATCH, DEC_SEQ, D_MODEL), 1.0)
    state_ssm = nrm((N_EVEN, DEC_BATCH, SSM_HEADS, SSM_HEAD_DIM, SSM_STATE), 0.5)
    state_ssm_conv = nrm((N_EVEN, DEC_BATCH, SSM_CONV - 1, SSM_CONV_DIM), 1.0)
    state_hgrn = nrm((N_EVEN, DEC_BATCH, HG_HEADS, HG_DIM, HG_DIM), 0.5)
    state_s5_re = nrm((N_ODD, DEC_BATCH, S5_GROUPS, S5_STATE), 0.3)
    state_s5_im = nrm((N_ODD, DEC_BATCH, S5_GROUPS, S5_STATE), 0.3)
    state_ffn_conv = nrm((DEPTH, DEC_BATCH, FFN_CONV - 1, 2 * FFN_DIM), 1.0)
    norm_mix_pre = gain((DEPTH, D_MODEL))
    norm_mix_post = gain((DEPTH, D_MODEL))
    norm_ffn_pre = gain((DEPTH, D_MODEL))
    norm_ffn_post = gain((DEPTH, D_MODEL))
    ab_in_w = nrm((N_EVEN, D_MODEL, AB_IN), D_MODEL ** -0.5)
    ssm_conv_w = nrm((N_EVEN, SSM_CONV, SSM_CONV_DIM), SSM_CONV ** -0.5)
    ssm_conv_b = nrm((N_EVEN, SSM_CONV_DIM), 0.02)
    dt0 = jnp.exp(log_uniform((N_EVEN, SSM_HEADS), 1e-3, 1e-1))
    ssm_dt_bias = dt0 + jnp.log(-jnp.expm1(-dt0))
    ssm_a_log = jnp.log(jax.random.uniform(next(ks), (N_EVEN, SSM_HEADS), F32, minval=1.0, maxval=16.0))
    ssm_d = gain((N_EVEN, SSM_HEADS))
    ssm_norm_w = gain((N_EVEN, SSM_D_INNER))
    hgrn_lb = nrm((N_EVEN + 1, HG_WIDTH), 1.0)
    hgrn_norm_w = gain((N_EVEN, HG_DIM))
    ab_out_w = nrm((N_EVEN, AB_OUT, D_MODEL), AB_OUT ** -0.5)
    s5_lam_re = -0.5 + nrm((N_ODD, S5_GROUPS, S5_STATE), 0.01)
    s5_lam_im = jnp.pi * jnp.arange(S5_STATE, dtype=F32) + nrm((N_ODD, S5_GROUPS, S5_STATE), 0.01)
    s5_log_step = log_uniform((N_ODD, S5_GROUPS), 1e-3, 1e-1)
    s5_b_re = nrm((N_ODD, S5_GROUPS, S5_STATE, S5_GROUP), (2 * S5_GROUP) ** -0.5)
    s5_b_im = nrm((N_ODD, S5_GROUPS, S5_STATE, S5_GROUP), (2 * S5_GROUP) ** -0.5)
    s5_c_re = nrm((N_ODD, S5_GROUPS, S5_GROUP, S5_STATE), (2 * S5_STATE) ** -0.5)
    s5_c_im = nrm((N_ODD, S5_GROUPS, S5_GROUP, S5_STATE), (2 * S5_STATE) ** -0.5)
    s5_d = nrm((N_ODD, D_MODEL), 1.0)
    s5_glu_w = nrm((N_ODD, D_MODEL, 2 * D_MODEL), D_MODEL ** -0.5)
    ffn_up_w = nrm((DEPTH, D_MODEL, 2 * FFN_DIM), D_MODEL ** -0.5)
    ffn_conv_w = nrm((DEPTH, FFN_CONV, 2 * FFN_DIM), FFN_CONV ** -0.5)
    ffn_conv_b = nrm((DEPTH, 2 * FFN_DIM), 0.02)
    ffn_down_w = nrm((DEPTH, FFN_DIM, D_MODEL), FFN_DIM ** -0.5)
    return {'x_prompt': x_prompt, 'x_sample': x_sample,
            'state_ssm': state_ssm, 'state_ssm_conv': state_ssm_conv, 'state_hgrn': state_hgrn,
            'state_s5_re': state_s5_re, 'state_s5_im': state_s5_im, 'state_ffn_conv': state_ffn_conv,
            'norm_mix_pre': norm_mix_pre, 'norm_mix_post': norm_mix_post,
            'norm_ffn_pre': norm_ffn_pre, 'norm_ffn_post': norm_ffn_post,
            'ab_in_w': ab_in_w, 'ssm_conv_w': ssm_conv_w, 'ssm_conv_b': ssm_conv_b,
            'ssm_dt_bias': ssm_dt_bias, 'ssm_a_log': ssm_a_log, 'ssm_d': ssm_d, 'ssm_norm_w': ssm_norm_w,
            'hgrn_lb': hgrn_lb, 'hgrn_norm_w': hgrn_norm_w, 'ab_out_w': ab_out_w,
            's5_lam_re': s5_lam_re, 's5_lam_im': s5_lam_im, 's5_log_step': s5_log_step,
            's5_b_re': s5_b_re, 's5_b_im': s5_b_im, 's5_c_re': s5_c_re, 's5_c_im': s5_c_im,
            's5_d': s5_d, 's5_glu_w': s5_glu_w,
            'ffn_up_w': ffn_up_w, 'ffn_conv_w': ffn_conv_w, 'ffn_conv_b': ffn_conv_b, 'ffn_down_w': ffn_down_w}


def reference(x_prompt, x_sample, state_ssm, state_ssm_conv, state_hgrn, state_s5_re, state_s5_im, state_ffn_conv,
              norm_mix_pre, norm_mix_post, norm_ffn_pre, norm_ffn_post,
              ab_in_w, ssm_conv_w, ssm_conv_b, ssm_dt_bias, ssm_a_log, ssm_d, ssm_norm_w,
              hgrn_lb, hgrn_norm_w, ab_out_w,
              s5_lam_re, s5_lam_im, s5_log_step, s5_b_re, s5_b_im, s5_c_re, s5_c_im, s5_d, s5_glu_w,
              ffn_up_w, ffn_conv_w, ffn_conv_b, ffn_down_w):
    w = dict(norm_mix_pre=norm_mix_pre, norm_mix_post=norm_mix_post,
             norm_ffn_pre=norm_ffn_pre, norm_ffn_post=norm_ffn_post,
             ab_in_w=ab_in_w, ssm_conv_w=ssm_conv_w, ssm_conv_b=ssm_conv_b, ssm_dt_bias=ssm_dt_bias,
             ssm_a_log=ssm_a_log, ssm_d=ssm_d, ssm_norm_w=ssm_norm_w,
             hgrn_lb=hgrn_lb, hgrn_norm_w=hgrn_norm_w, ab_out_w=ab_out_w,
             s5_lam_re=s5_lam_re, s5_lam_im=s5_lam_im, s5_log_step=s5_log_step,
             s5_b_re=s5_b_re, s5_b_im=s5_b_im, s5_c_re=s5_c_re, s5_c_im=s5_c_im, s5_d=s5_d, s5_glu_w=s5_glu_w,
             ffn_up_w=ffn_up_w, ffn_conv_w=ffn_conv_w, ffn_conv_b=ffn_conv_b, ffn_down_w=ffn_down_w)
    pd = x_prompt.dtype
    y_prompt, (p_ssm, p_sconv, p_hg, p_s5r, p_s5i, p_fconv) = trunk(
        x_prompt,
        jnp.zeros((N_EVEN, BATCH, SSM_HEADS, SSM_HEAD_DIM, SSM_STATE), pd),
        jnp.zeros((N_EVEN, BATCH, SSM_CONV - 1, SSM_CONV_DIM), pd),
        jnp.zeros((N_EVEN, BATCH, HG_HEADS, HG_DIM, HG_DIM), pd),
        jnp.zeros((N_ODD, BATCH, S5_GROUPS, S5_STATE), pd),
        jnp.zeros((N_ODD, BATCH, S5_GROUPS, S5_STATE), pd),
        jnp.zeros((DEPTH, BATCH, FFN_CONV - 1, 2 * FFN_DIM), pd),
        w)
    y_sample, (s_ssm, s_sconv, s_hg, s_s5r, s_s5i, s_fconv) = trunk(
        x_sample, state_ssm, state_ssm_conv, state_hgrn, state_s5_re, state_s5_im, state_ffn_conv, w)
    return (y_prompt, y_sample, p_ssm, p_sconv, p_hg, p_s5r, p_s5i, p_fconv,
            s_ssm, s_sconv, s_hg, s_s5r, s_s5i, s_fconv)
```

```python
import numpy as np
from contextlib import ExitStack
import concourse.bass as bass
import concourse.mybir as mybir
from concourse.bass_utils import run_bass_kernel_spmd

F32 = mybir.dt.float32
BF16 = mybir.dt.bfloat16
AF = mybir.ActivationFunctionType
ALU = mybir.AluOpType
AX = mybir.AxisListType

D = 1024
SEQ = 2048
SEG = 1024
NSEG = 2
NS = 16
FFN = 2816
NFB = 22
AB_IN = 6416
EPS = 1e-6


class Sched:
    NDMA = 24

    def __init__(self, nc, es, same_engine_sync=True):
        self.nc = nc
        self.E = {'pe': nc.tensor, 'act': nc.scalar, 'dve': nc.vector, 'pool': nc.gpsimd, 'sp': nc.sync}
        self.sem = {e: es.enter_context(nc.semaphore("s_" + e)) for e in ('pe', 'act', 'dve', 'pool')}
        self.cnt = {e: 0 for e in self.sem}
        self.dsem = [es.enter_context(nc.semaphore("d%d" % i)) for i in range(self.NDMA)]
        self.dcnt = [0] * self.NDMA
        self.dnext = 0
        self.seen = {e: {} for e in self.E}
        self.lw = {}
        self.rd = {}
        self.semobj = {}
        self.same = same_engine_sync
        self.n_wait = 0
        self.n_inst = 0
        for e, s in self.sem.items():
            self.semobj[e] = s
        for i, s in enumerate(self.dsem):
            self.semobj[('d', i)] = s

    def _wait(self, eng, tokens):
        need = {}
        for t in tokens:
            if t is None:
                continue
            sid, val = t
            if sid == eng and (not self.same or eng == 'pe'):
                continue
            if val > need.get(sid, 0):
                need[sid] = val
        for sid, val in need.items():
            if self.seen[eng].get(sid, 0) >= val:
                continue
            self.E[eng].wait_ge(self.semobj[sid], val)
            self.seen[eng][sid] = val
            self.n_wait += 1

    def deps(self, eng, reads, writes):
        toks = []
        for k in reads:
            toks.append(self.lw.get(k))
        for k in writes:
            toks.append(self.lw.get(k))
            toks.extend(self.rd.get(k, ()))
        self._wait(eng, toks)

    def commit(self, tok, reads, writes):
        for k in reads:
            self.rd.setdefault(k, []).append(tok)
        for k in writes:
            self.lw[k] = tok
            self.rd[k] = []

    def op(self, eng, fn, reads=(), writes=()):
        self.deps(eng, reads, writes)
        ins = fn()
        self.n_inst += 1
        self.cnt[eng] += 1
        ins.then_inc(self.sem[eng], 1)
        self.commit((eng, self.cnt[eng]), reads, writes)
        return ins

    def group_end(self, eng, ins, reads, writes):
        self.cnt[eng] += 1
        ins.then_inc(self.sem[eng], 1)
        self.commit((eng, self.cnt[eng]), reads, writes)

    def dma(self, q, out, in_, reads=(), writes=(), **kw):
        i = self.dnext
        self.dnext = (self.dnext + 1) % self.NDMA
        sid = ('d', i)
        if self.dcnt[i]:
            self._wait(q, [(sid, self.dcnt[i])])
        self.deps(q, reads, writes)
        self.dcnt[i] += 16
        ins = self.E[q].dma_start(out=out, in_=in_, **kw)
        ins.then_inc(self.dsem[i], 16)
        self.n_inst += 1
        self.commit((sid, self.dcnt[i]), reads, writes)
        return ins

    def all_tokens(self):
        toks = [(('d', i), self.dcnt[i]) for i in range(self.NDMA) if self.dcnt[i]]
        toks += [(e, c) for e, c in self.cnt.items() if c]
        return toks

    def barrier(self):
        toks = self.all_tokens()
        for e in ('pe', 'act', 'dve', 'pool', 'sp'):
            self._wait(e, toks)
        self.lw = {}
        self.rd = {}

    def finish(self):
        self._wait('sp', self.all_tokens())


class KB:
    def __init__(self, nc, es, dbg=None):
        self.nc = nc
        self.es = es
        import os
        self.S = Sched(nc, es, same_engine_sync=os.environ.get('K_SAME', '1') == '1')
        self.dbg = dbg or {}
        self.din = {}
        self.dout = {}
        self.uid = 0
        self.rr = 0
        self.evac_rr = 0

    def inp(self, name, shape, dt=F32):
        self.din[name] = self.nc.dram_tensor(name, list(shape), dt, kind="ExternalInput").ap()
        return self.din[name]

    def outp(self, name, shape, dt=F32):
        self.dout[name] = self.nc.dram_tensor(name, list(shape), dt, kind="ExternalOutput").ap()
        return self.dout[name]

    def sb(self, es, name, shape, dt=F32):
        self.nalloc = getattr(self, 'nalloc', 0) + 1
        return es.enter_context(self.nc.sbuf_tensor("%s_%d" % (name, self.nalloc), list(shape), dt))

    def dump(self, name, ap, reads, dt=F32):
        if not self.dbg.get('dump'):
            return
        o = self.outp("dbg_" + name, list(ap.shape), dt)
        self.S.dma('sp', o, ap, reads=reads)

    def init_psum(self):
        self.PB = [self.es.enter_context(self.nc.psum_tensor("pb%d" % i, [128, 512], F32)) for i in range(8)]

    def bank(self):
        n = getattr(self, 'nrot', 7)
        i = self.rr % n
        self.rr = (i + 1) % n
        return i

    def mm(self, out_ap, out_keys, pairs, read_keys, **kw):
        S = self.S
        S.deps('pe', read_keys, out_keys)
        n = len(pairs)
        ins = None
        for i, (l, r) in enumerate(pairs):
            ins = self.nc.tensor.matmul(out_ap, lhsT=l, rhs=r, start=(i == 0), stop=(i == n - 1), **kw)
            S.n_inst += 1
        S.group_end('pe', ins, read_keys, out_keys)

    def transpose(self, out_ap, out_keys, in_ap, ident_ap, read_keys):
        S = self.S
        S.deps('pe', read_keys, out_keys)
        ins = self.nc.tensor.transpose(out_ap, in_ap, ident_ap)
        S.n_inst += 1
        S.group_end('pe', ins, read_keys, out_keys)

    def evac_eng(self):
        self.evac_rr ^= 1
        return 'act' if self.evac_rr else 'dve'

    def copy(self, eng, out, in_, reads, writes):
        nc = self.nc
        if eng == 'act':
            return self.S.op('act', lambda: nc.scalar.copy(out=out, in_=in_), reads, writes)
        if eng == 'pool':
            return self.S.op('pool', lambda: nc.gpsimd.tensor_copy(out=out, in_=in_), reads, writes)
        return self.S.op('dve', lambda: nc.vector.tensor_copy(out=out, in_=in_), reads, writes)

    def init_wbufs(self, es, n=4, width=NFB * 128):
        self.WB = [self.sb(es, "wb%d" % i, [128, width], BF16) for i in range(n)]
        self.wrr = 0

    def load_w(self, w2d, kc, ncol=128):
        i = self.wrr
        self.wrr = (self.wrr + 1) % len(self.WB)
        key = ('wb', i)
        view = self.WB[i][:, 0:kc * ncol].rearrange("p (k n) -> p k n", n=ncol)
        self.S.dma('pool', view, w2d.rearrange("(k p) n -> p k n", p=128), writes=[key])
        return view, key


def wstream(kb, specs, depth=3):
    specs = list(specs)
    q = []
    nxt = 0
    while nxt < len(specs) and len(q) < depth:
        q.append(kb.load_w(*specs[nxt]))
        nxt += 1
    for _ in range(len(specs)):
        cur = q.pop(0)
        if nxt < len(specs):
            q.append(kb.load_w(*specs[nxt]))
            nxt += 1
        yield cur


def seg_cols(seg):
    return SEG + (NS if seg == 0 else 0)


def ttiles(seg):
    t = [(0, 512), (512, 512)]
    if seg == 0:
        t.append((SEG, NS))
    return t


def tkeys(name, cs, tt):
    return [(name, c, tt) for c in cs]


class Prog(KB):
    def setup(self):
        nc, S, es = self.nc, self.S, self.es
        self.init_psum()
        c_ident = self.inp("c_ident", [128, 128])
        self.ident = self.sb(es, "ident", [128, 128])
        self.identb = self.sb(es, "identb", [128, 128], BF16)
        self.onesb = self.sb(es, "onesb", [128, 128], BF16)
        self.epsc = self.sb(es, "epsc", [128, 1])
        S.dma('sp', self.ident[:], c_ident[:, :], writes=['ident'])
        S.op('dve', lambda: nc.vector.tensor_copy(out=self.identb[:], in_=self.ident[:]), ['ident'], ['identb'])
        S.op('dve', lambda: nc.vector.memset(self.onesb[:], 1.0), [], ['onesb'])
        S.op('dve', lambda: nc.vector.memset(self.epsc[:], EPS), [], ['epsc'])
        nw = self.inp("nw", [128, 4 * 2 * 8])
        self.NW = self.sb(es, "NW", [128, 4, 2, 8])
        S.dma('sp', self.NW[:].rearrange("p a l c -> p (a l c)"), nw[:, :], writes=['NW'])
        fcw = self.inp("fcw", [128, 2 * 44 * 3])
        fcb = self.inp("fcb", [128, 2 * 44])
        self.FCW = self.sb(es, "FCW", [128, 2, 44, 3])
        self.FCB = self.sb(es, "FCB", [128, 2, 44])
        S.dma('sp', self.FCW[:].rearrange("p l b k -> p (l b k)"), fcw[:, :], writes=['FCW'])
        S.dma('sp', self.FCB[:].rearrange("p l b -> p (l b)"), fcb[:, :], writes=['FCB'])
        self.XR = self.sb(es, "XR", [128, 8, SEG + NS])
        self.FH = self.sb(es, "FH", [128, 2, 44, 2])
        S.op('dve', lambda: nc.vector.memset(self.FH[:].rearrange("p l b k -> p (l b k)"), 0.0), [], ['FH'])
        self.init_wbufs(es)
        self.x_p = self.inp("x_p", [SEQ, D])
        self.x_s = self.inp("x_s", [NS, D])
        self.y_p = self.outp("y_p", [SEQ, D])
        self.y_s = self.outp("y_s", [NS, D])
        self.ffn_up_w = self.inp("ffn_up_w", [2, D, 2 * FFN])
        self.ffn_down_w = self.inp("ffn_down_w", [2, FFN, D])
        self.st_fconv = self.inp("st_fconv", [2, NS * 2, 2 * FFN])
        self.p_fconv = self.outp("p_fconv", [2, 2, 2 * FFN])
        self.s_fconv = self.outp("s_fconv", [2, NS, 2, 2 * FFN])

    def load_x(self, seg):
        nc, S = self.nc, self.S
        with ExitStack() as es:
            XT = [self.sb(es, "xt%d" % i, [128, D]) for i in range(2)]
            for tb in range(8):
                xt = XT[tb % 2]
                k = ('xt', tb % 2)
                r0 = seg * SEG + tb * 128
                S.dma('sp', xt[:], self.x_p[r0:r0 + 128, :], writes=[k])
                for half in range(2):
                    b = self.bank()
                    for j in range(4):
                        c = half * 4 + j
                        self.transpose(self.PB[b][:, j * 128:(j + 1) * 128], [('pb', b)],
                                       xt[:, c * 128:(c + 1) * 128], self.ident[:], [k, 'ident'])
                    self.copy(self.evac_eng(), self.XR[:, half * 4:(half + 1) * 4, tb * 128:(tb + 1) * 128],
                              self.PB[b][:, :].rearrange("p (c t) -> p c t", t=128),
                              [('pb', b)], tkeys('xr', range(half * 4, half * 4 + 4), tb // 4))
            if seg == 0:
                xs = self.sb(es, "xs_in", [NS, D])
                S.dma('sp', xs[:], self.x_s[:, :], writes=['xs_in'])
                b = self.bank()
                for c in range(8):
                    self.transpose(self.PB[b][:, c * NS:(c + 1) * NS], [('pb', b)],
                                   xs[:, c * 128:(c + 1) * 128], self.ident[0:NS, 0:NS], ['xs_in', 'ident'])
                self.copy('dve', self.XR[:, :, SEG:SEG + NS],
                          self.PB[b][:, 0:8 * NS].rearrange("p (c t) -> p c t", t=NS),
                          [('pb', b)], tkeys('xr', range(8), 2))
            S.barrier()

    def store_x(self, seg):
        nc, S = self.nc, self.S
        with ExitStack() as es:
            YT = [self.sb(es, "yt%d" % i, [128, D]) for i in range(2)]
            for tb in range(8):
                yt = YT[tb % 2]
                k = ('yt', tb % 2)
                for half in range(2):
                    b = self.bank()
                    for j in range(4):
                        c = half * 4 + j
                        self.transpose(self.PB[b][:, j * 128:(j + 1) * 128], [('pb', b)],
                                       self.XR[:, c, tb * 128:(tb + 1) * 128], self.ident[:],
                                       [('xr', c, tb // 4), 'ident'])
                    self.copy(self.evac_eng(), yt[:, half * 512:(half + 1) * 512], self.PB[b][:, :],
                              [('pb', b)], [k])
                r0 = seg * SEG + tb * 128
                S.dma('sp', self.y_p[r0:r0 + 128, :], yt[:], reads=[k])
            if seg == 0:
                ys = self.sb(es, "ys_out", [NS, D])
                for half in range(2):
                    b = self.bank()
                    for j in range(4):
                        c = half * 4 + j
                        self.transpose(self.PB[b][0:NS, j * 128:(j + 1) * 128], [('pb', b)],
                                       self.XR[:, c, SEG:SEG + NS], self.ident[:], [('xr', c, 2), 'ident'])
                    self.copy('dve', ys[:, half * 512:(half + 1) * 512], self.PB[b][0:NS, :], [('pb', b)], ['ys_out'])
                S.dma('sp', self.y_s[:, :], ys[:], reads=['ys_out'])
            S.barrier()

    def rmsnorm_fm(self, es, seg, src, sname, wcol, dst, dname, nch=8, residual=False, ncols_div=None):
        nc, S = self.nc, self.S
        nfeat = float(nch * 128) if ncols_div is None else float(ncols_div)
        SQ = [self.sb(es, "sq%d_%d" % (self.uid, i), [128, nch, 512], BF16) for i in range(2)]
        RS = [self.sb(es, "rs%d_%d" % (self.uid, i), [128, 512]) for i in range(2)]
        TMP = [self.sb(es, "nt%d_%d" % (self.uid, i), [128, 512]) for i in range(2)] if residual else None
        self.uid += 1
        for ti, (c0, n) in enumerate(ttiles(seg)):
            sq, rs = SQ[ti % 2], RS[ti % 2]
            ksq, krs = ('sq', self.uid, ti % 2), ('rs', self.uid, ti % 2)
            S.op('act', lambda: nc.scalar.activation(out=sq[:, :, 0:n], in_=src[:, 0:nch, c0:c0 + n], func=AF.Square),
                 tkeys(sname, range(nch), ti), [ksq])
            b = self.bank()
            self.mm(self.PB[b][:, 0:n], [('pb', b)],
                    [(self.onesb[:], sq[:, c, 0:n]) for c in range(nch)], [ksq, 'onesb'])
            S.op('act', lambda: nc.scalar.activation(out=rs[:, 0:n], in_=self.PB[b][:, 0:n], func=AF.Sqrt,
                                                     bias=self.epsc[:, 0:1], scale=1.0 / nfeat),
                 [('pb', b), 'epsc'], [krs])
            if self.dbg.get('dump') and seg == 1 and ti == 0 and not residual and not self.dbg.get('d1'):
                self.dbg['d1'] = 1
                self.dump("rs_sqrt", rs[:, 0:16], [krs])
                self.dump("sq", sq[:, :, 0:16], [ksq], BF16)
                self.dump("xr", src[:, :, 0:16], tkeys(sname, range(nch), ti))
            S.op('dve', lambda: nc.vector.reciprocal(out=rs[:, 0:n], in_=rs[:, 0:n]), [krs], [krs])
            for c in range(nch):
                if not residual:
                    S.op('dve', lambda: nc.vector.scalar_tensor_tensor(
                        out=dst[:, c, c0:c0 + n], in0=src[:, c, c0:c0 + n], scalar=wcol[:, c:c + 1],
                        in1=rs[:, 0:n], op0=ALU.mult, op1=ALU.mult),
                        [(sname, c, ti), krs, 'NW'], [(dname, c, ti)])
                else:
                    tmp = TMP[c % 2]
                    kt = ('ntmp', self.uid, c % 2)
                    S.op('dve', lambda: nc.vector.scalar_tensor_tensor(
                        out=tmp[:, 0:n], in0=src[:, c, c0:c0 + n], scalar=wcol[:, c:c + 1],
                        in1=rs[:, 0:n], op0=ALU.mult, op1=ALU.mult),
                        [(sname, c, ti), krs, 'NW'], [kt])
                    S.op('pool', lambda: nc.gpsimd.tensor_tensor(
                        out=dst[:, c, c0:c0 + n], in0=dst[:, c, c0:c0 + n], in1=tmp[:, 0:n], op=ALU.add),
                        [kt, (dname, c, ti)], [(dname, c, ti)])

    def ffn(self, l, seg):
        nc, S = self.nc, self.S
        ncol = seg_cols(seg)
        tts = ttiles(seg)
        with ExitStack() as es0:
          H = self.sb(es0, "f_h", [128, NFB, ncol], BF16)
          NDW = 2
          DW = [self.sb(es0, "f_dw%d" % i, [128, NFB * 128], BF16) for i in range(NDW)]
          with ExitStack() as es:
            HN = self.sb(es, "f_hn", [128, 8, ncol], BF16)
            with ExitStack() as es2:
                self.rmsnorm_fm(es2, seg, self.XR, 'xr', self.NW[:, 2, l, :], HN, 'hn')
                S.barrier()
            PRE = [[self.sb(es, "pre%d_%d" % (i, vg), [128, 2 + ncol]) for vg in range(2)] for i in range(2)]
            ACC = [[self.sb(es, "acc%d_%d" % (i, vg), [128, ncol]) for vg in range(2)] for i in range(2)]
            if seg == 0:
                CS = self.sb(es, "f_cs", [128, 44, 2 * NS])
                SNEW = self.sb(es, "f_snew", [128, 44, NS])
                with ExitStack() as es2:
                    cst = self.sb(es2, "f_cst", [2 * NS, 24 * 128])
                    for (g0, g1) in ((0, 6), (6, 11)):
                        nb = (g1 - g0) * 4
                        S.dma('sp', cst[:, 0:nb * 128], self.st_fconv[l, :, g0 * 512:g0 * 512 + nb * 128], writes=['cst'])
                        for g in range(g0, g1):
                            b = self.bank()
                            for j in range(4):
                                lb = (g - g0) * 4 + j
                                self.transpose(self.PB[b][:, j * 32:(j + 1) * 32], [('pb', b)],
                                               cst[:, lb * 128:(lb + 1) * 128], self.ident[0:32, 0:32],
                                               ['cst', 'ident'])
                            self.copy(self.evac_eng(), CS[:, g * 4:(g + 1) * 4, :],
                                      self.PB[b][:, 0:128].rearrange("p (j r) -> p j r", r=32),
                                      [('pb', b)], ['f_cs'])
                    S.barrier()
            ws_up = wstream(self, [(self.ffn_up_w[l, :, (vg * NFB + j) * 128:(vg * NFB + j + 1) * 128], 8)
                                   for j in range(NFB) for vg in range(2)])
            for j in range(NFB):
                pre, acc = PRE[j % 2], ACC[j % 2]
                if NDW and j in (4, 12):
                    i = 0 if j == 4 else 1
                    S.dma('pool', DW[i][:].rearrange("p (k n) -> p k n", n=128),
                          self.ffn_down_w[l, :, i * 128:(i + 1) * 128].rearrange("(k p) n -> p k n", p=128), writes=[('dw', i)])
                for vg in range(2):
                    blk = vg * NFB + j
                    kpre, kacc = ('pre', j % 2, vg), ('acc', j % 2, vg)
                    wv, wk = next(ws_up)
                    S.op('pool', lambda: nc.gpsimd.tensor_copy(out=pre[vg][:, 0:2], in_=self.FH[:, l, blk, :]),
                         ['FH'], [kpre])
                    for ti, (c0, n) in enumerate(tts):
                        b = self.bank()
                        self.mm(self.PB[b][:, 0:n], [('pb', b)],
                                [(wv[:, kc, :], HN[:, kc, c0:c0 + n]) for kc in range(8)],
                                [wk] + tkeys('hn', range(8), ti))
                        if ti == 2:
                            self.copy('act', SNEW[:, blk, :], self.PB[b][:, 0:n], [('pb', b)], [('snew', blk)])
                        else:
                            self.copy('act', pre[vg][:, 2 + c0:2 + c0 + n], self.PB[b][:, 0:n],
                                      [('pb', b)], [kpre])
                    S.op('pool', lambda: nc.gpsimd.tensor_copy(out=self.FH[:, l, blk, :], in_=pre[vg][:, SEG:SEG + 2]),
                         [kpre], ['FH'])
                    w = self.FCW[:, l, blk, :]
                    S.op('act', lambda: nc.scalar.activation(out=acc[vg][:, 0:SEG], in_=pre[vg][:, 2:2 + SEG],
                                                             func=AF.Identity, bias=self.FCB[:, l, blk:blk + 1],
                                                             scale=w[:, 2:3]),
                         [kpre, 'FCW', 'FCB'], [kacc])
                    S.op('dve', lambda: nc.vector.scalar_tensor_tensor(
                        out=acc[vg][:, 0:SEG], in0=pre[vg][:, 1:1 + SEG], scalar=w[:, 1:2], in1=acc[vg][:, 0:SEG],
                        op0=ALU.mult, op1=ALU.add), [kpre, kacc, 'FCW'], [kacc])
                    S.op('dve', lambda: nc.vector.scalar_tensor_tensor(
                        out=acc[vg][:, 0:SEG], in0=pre[vg][:, 0:SEG], scalar=w[:, 0:1], in1=acc[vg][:, 0:SEG],
                        op0=ALU.mult, op1=ALU.add), [kpre, kacc, 'FCW'], [kacc])
                if seg == 1 and j == 0 and l == 0:
                    self.dump("pre_v", pre[0][:, 0:16], [('pre', 0, 0)])
                    self.dump("acc_v", acc[0][:, 0:16], [('acc', 0, 0)])
                    self.dump("acc_g", acc[1][:, 0:16], [('acc', 0, 1)])
                    self.dump("hn", HN[:, 0, 0:16], tkeys('hn', [0], 0), BF16)
                kv, kg = ('acc', j % 2, 0), ('acc', j % 2, 1)
                S.op('act', lambda: nc.scalar.activation(out=acc[1][:, 0:SEG], in_=acc[1][:, 0:SEG], func=AF.Silu),
                     [kg], [kg])
                S.op('dve', lambda: nc.vector.tensor_tensor(out=H[:, j, 0:SEG], in0=acc[0][:, 0:SEG], in1=acc[1][:, 0:SEG],
                                                            op=ALU.mult), [kv, kg], tkeys('h', [j], 0) + tkeys('h', [j], 1))
            if seg == 0:
                es3 = ExitStack()
                ACS = self.sb(es3, "f_acs", [128, 44, NS])
                TMS = self.sb(es3, "f_tms", [128, 44, NS])
                ksn = [('snew', blk) for blk in range(44)]
                cs4 = CS[:].rearrange("p a (b k) -> p a b k", k=2)
                wb = lambda k: self.FCW[:, l, :, k:k + 1].to_broadcast([128, 44, NS])
                S.op('dve', lambda: nc.vector.tensor_tensor(out=ACS[:], in0=SNEW[:], in1=wb(2), op=ALU.mult), ksn + ['FCW'], ['f_acs'])
                for k in (1, 0):
                    S.op('dve', lambda: nc.vector.tensor_tensor(out=TMS[:], in0=cs4[:, :, :, k], in1=wb(k), op=ALU.mult),
                         ['f_cs', 'FCW'], ['f_tms'])
                    S.op('dve', lambda: nc.vector.tensor_tensor(out=ACS[:], in0=ACS[:], in1=TMS[:], op=ALU.add), ['f_acs', 'f_tms'], ['f_acs'])
                S.op('dve', lambda: nc.vector.tensor_tensor(out=ACS[:], in0=ACS[:], in1=self.FCB[:, l, :].unsqueeze(2).to_broadcast([128, 44, NS]),
                                                            op=ALU.add), ['f_acs', 'FCB'], ['f_acs'])
                S.op('act', lambda: nc.scalar.activation(out=ACS[:, NFB:2 * NFB, :], in_=ACS[:, NFB:2 * NFB, :], func=AF.Silu), ['f_acs'], ['f_acs'])
                S.op('dve', lambda: nc.vector.tensor_tensor(out=H[:, :, SEG:SEG + NS], in0=ACS[:, 0:NFB, :], in1=ACS[:, NFB:2 * NFB, :],
                                                            op=ALU.mult), ['f_acs'], tkeys('h', range(NFB), 2))
                S.barrier()
                es3.close()
            if seg == NSEG - 1:
                with nc.allow_non_contiguous_dma(reason="tiny conv-state scatter"):
                    for k in range(2):
                        S.dma('sp', self.p_fconv[l, k].rearrange("(b p) -> p b", p=128), self.FH[:, l, :, k], reads=['FH'])
            if seg == 0:
                with ExitStack() as es2:
                    so = self.sb(es2, "f_so", [NS, 24 * 128])
                    for (g0, g1) in ((0, 6), (6, 11)):
                        nb = (g1 - g0) * 4
                        for g in range(g0, g1):
                            b = self.bank()
                            for jj in range(4):
                                blk = g * 4 + jj
                                self.transpose(self.PB[b][0:NS, jj * 128:(jj + 1) * 128], [('pb', b)],
                                               SNEW[:, blk, :], self.ident[:], [('snew', blk), 'ident'])
                            self.copy(self.evac_eng(), so[:, (g - g0) * 512:(g - g0 + 1) * 512], self.PB[b][0:NS, :],
                                      [('pb', b)], ['f_so'])
                        S.dma('sp', self.s_fconv[l, :, 1, g0 * 512:g0 * 512 + nb * 128], so[:, 0:nb * 128], reads=['f_so'])
                    S.dma('sp', self.s_fconv[l, :, 0, :],
                          self.st_fconv[l].rearrange("(b k) c -> b k c", k=2)[:, 1, :])
                    S.barrier()
            S.barrier()
          with ExitStack() as es:
            Fo = self.sb(es, "f_out", [128, 8, ncol])
            ws_dn = wstream(self, [(self.ffn_down_w[l, :, ob * 128:(ob + 1) * 128], NFB) for ob in range(NDW, 8)])
            for ob in range(8):
                if ob < NDW:
                    wv, wk = DW[ob][:].rearrange("p (k n) -> p k n", n=128), ('dw', ob)
                else:
                    wv, wk = next(ws_dn)
                for ti, (c0, n) in enumerate(tts):
                    b = self.bank()
                    self.mm(self.PB[b][:, 0:n], [('pb', b)],
                            [(wv[:, kc, :], H[:, kc, c0:c0 + n]) for kc in range(NFB)],
                            [wk] + tkeys('h', range(NFB), ti))
                    self.copy(self.evac_eng(), Fo[:, ob, c0:c0 + n], self.PB[b][:, 0:n], [('pb', b)], [('fo', ob, ti)])
            with ExitStack() as es2:
                self.rmsnorm_fm(es2, seg, Fo, 'fo', self.NW[:, 3, l, :], self.XR, 'xr', residual=True)
                S.barrier()
            S.barrier()


def build_program(stage="full", dbg=None):
    nc = bass.Bass("TRN2", target_bir_lowering=False)
    es = ExitStack()
    with es:
        P = ProgC2(nc, es, dbg)
        P.setup()
        if stage in ('ssd', 'hg', 'ab', 'full'):
            P.setup_ab()
            P.setup_hg()
            P.setup_ab_sample()
        if stage in ('s5', 'full'):
            if stage == 's5':
                P.stg = [P.sb(es, 'stg%d' % i, [NS, 128]) for i in range(4)]
                P.stg_rr = 0
                P.SC = {}
            P.setup_s5()
        if stage == "copy":
            for seg in range(NSEG):
                P.load_x(seg)
                P.store_x(seg)
        if stage == "ssd":
            ydbg = P.outp("dbg_y", [128, 8, SEQ], BF16)
            for seg in range(NSEG):
                P.load_x(seg)
                with ExitStack() as es0:
                    MIX = P.sb(es0, "mix", [128, 16, seg_cols(seg)], BF16)
                    HN = P.sb(es0, "hn", [128, 8, seg_cols(seg)], BF16)
                    with ExitStack() as es2:
                        P.rmsnorm_fm(es2, seg, P.XR, 'xr', P.NW[:, 0, 0, :], HN, 'hn')
                        P.S.barrier()
                    P.ssd_phase(seg, HN, MIX)
                    P.S.dma('sp', ydbg[:, :, seg * SEG:(seg + 1) * SEG], MIX[:, 0:8, 0:SEG],
                            reads=[('mix', c, t) for c in range(8) for t in range(2)])
                    P.S.barrier()
        if stage == "hg":
            ydbg = P.outp("dbg_o", [128, 8, SEQ], BF16)
            for seg in range(NSEG):
                P.load_x(seg)
                with ExitStack() as es0:
                    MIX = P.sb(es0, "mix", [128, 16, seg_cols(seg)], BF16)
                    HN = P.sb(es0, "hn", [128, 8, seg_cols(seg)], BF16)
                    with ExitStack() as es2:
                        P.rmsnorm_fm(es2, seg, P.XR, 'xr', P.NW[:, 0, 0, :], HN, 'hn')
                        P.S.barrier()
                    P.hgrn_phase(seg, HN, MIX)
                    P.S.dma('sp', ydbg[:, :, seg * SEG:(seg + 1) * SEG], MIX[:, 8:16, 0:SEG],
                            reads=[('mix', c, t) for c in range(8, 16) for t in range(2)])
                    P.S.barrier()
        if stage == "ab":
            for seg in range(NSEG):
                P.load_x(seg)
                P.mixer_ab(seg)
                P.store_x(seg)
        if stage == "s5":
            for seg in range(NSEG):
                P.load_x(seg)
                P.mixer_c(seg)
                P.store_x(seg)
        if stage == "full":
            for seg in range(NSEG):
                P.load_x(seg)
                P.mixer_ab(seg)
                P.ffn(0, seg)
                P.mixer_c(seg)
                P.ffn(1, seg)
                P.store_x(seg)
        if stage == "ffn":
            for seg in range(NSEG):
                P.load_x(seg)
                P.ffn(0, seg)
                P.store_x(seg)
        P.S.finish()
        print("program: inst=%d waits=%d" % (P.S.n_inst, P.S.n_wait), flush=True)
    return nc, P


def _fm(v, nch):
    return np.ascontiguousarray(np.asarray(v, np.float32).reshape(nch, 128).T)


def host_inputs(inp, core):
    m = {}
    m["c_ident"] = np.eye(128, dtype=np.float32)
    m["x_p"] = np.ascontiguousarray(inp["x_prompt"][core])
    m["x_s"] = np.ascontiguousarray(inp["x_sample"][core * NS:(core + 1) * NS, 0, :])
    nw = np.zeros((128, 4, 2, 8), np.float32)
    for a, k in enumerate(["norm_mix_pre", "norm_mix_post", "norm_ffn_pre", "norm_ffn_post"]):
        for l in range(2):
            nw[:, a, l, :] = _fm(inp[k][l], 8)
    m["nw"] = nw.reshape(128, -1)
    fcw = np.zeros((128, 2, 44, 3), np.float32)
    fcb = np.zeros((128, 2, 44), np.float32)
    for l in range(2):
        for k in range(3):
            fcw[:, l, :, k] = _fm(inp["ffn_conv_w"][l, k], 44)
        fcb[:, l, :] = _fm(inp["ffn_conv_b"][l], 44)
    m["fcw"] = fcw.reshape(128, -1)
    m["fcb"] = fcb.reshape(128, -1)
    m["ffn_up_w"] = np.asarray(inp["ffn_up_w"], np.float32)
    m["ffn_down_w"] = np.asarray(inp["ffn_down_w"], np.float32)
    m["ab_in_w"] = np.asarray(inp["ab_in_w"][0], np.float32)
    m["ab_out_w"] = np.asarray(inp["ab_out_w"][0], np.float32)
    t = np.arange(128)
    m["c_tri"] = (t[:, None] <= t[None, :]).astype(np.float32)
    m["c_neg"] = np.where(t[:, None] <= t[None, :], 0.0, -30000.0).astype(np.float32)
    m["c_bmask"] = ((t[:, None] <= t[None, :]) & (t[:, None] // 32 == t[None, :] // 32)).astype(np.float32)
    rst = np.ones((128, SEG), np.float32)
    rst[:, 0::32] = 0.0
    m["c_rst"] = rst
    scw = np.zeros((128, 10, 4), np.float32)
    for k in range(4):
        scw[:, :, k] = _fm(inp["ssm_conv_w"][0, k], 10)
    m["scw"] = scw.reshape(128, 40)
    m["scb"] = _fm(inp["ssm_conv_b"][0], 10)
    m["ssm_v16"] = np.stack([inp["ssm_dt_bias"][0], inp["ssm_a_log"][0], inp["ssm_d"][0]]).astype(np.float32)
    m["snw"] = _fm(inp["ssm_norm_w"][0], 8)
    lbf = np.zeros((128, 2, 8), np.float32)
    for r in range(2):
        lbf[:, r, :] = _fm(inp["hgrn_lb"][r], 8)
    m["hgrn_lb_fm"] = lbf.reshape(128, 16)
    m["hgrn_nw"] = np.asarray(inp["hgrn_norm_w"][0], np.float32).reshape(128, 1)
    m["c_cmask"] = (t[:, None] // 32 == np.arange(4)[None, :]).astype(np.float32)
    sl = slice(core * NS, (core + 1) * NS)
    m["st_sconv"] = np.ascontiguousarray(inp["state_ssm_conv"][0, sl])
    m["st_ssm"] = np.ascontiguousarray(inp["state_ssm"][0, sl].reshape(NS * 16, 4096))
    m["st_hg"] = np.ascontiguousarray(inp["state_hgrn"][0, sl].reshape(NS * 8, 128 * 128))
    m["ssm_conv_wb"] = np.concatenate([inp["ssm_conv_w"][0], inp["ssm_conv_b"][0][None]], 0).astype(np.float32)
    v16 = m["ssm_v16"].T.reshape(2, 1, 8, 3)
    m["v16bh"] = np.ascontiguousarray(np.broadcast_to(v16, (2, 8, 8, 3))).reshape(128, 3)
    m["snw_row"] = np.asarray(inp["ssm_norm_w"][0], np.float32).reshape(1, D)
    m["hnw_row"] = np.asarray(inp["hgrn_norm_w"][0], np.float32).reshape(1, 128)
    m["hgrn_lb_bh"] = np.ascontiguousarray(np.broadcast_to(
        np.asarray(inp["hgrn_lb"], np.float32).reshape(2, 1, 8, 128), (2, NS, 8, 128))).reshape(2, 128, 128)
    lam = np.stack([inp["s5_lam_re"][0], inp["s5_lam_im"][0]]).astype(np.float32)
    ls = np.asarray(inp["s5_log_step"][0], np.float32)
    bb = np.stack([inp["s5_b_re"][0], inp["s5_b_im"][0]]).astype(np.float32)
    cc = np.stack([inp["s5_c_re"][0], inp["s5_c_im"][0]]).astype(np.float32)
    lamA = np.broadcast_to(lam.reshape(2, 8, 8, 1, 64).transpose(0, 2, 3, 1, 4), (2, 8, 16, 8, 64))
    m["s5_lamA"] = np.ascontiguousarray(lamA).reshape(2, 128, 512)
    m["s5_lsA"] = np.ascontiguousarray(np.broadcast_to(ls.reshape(8, 8).T[:, None, :], (8, 16, 8))).reshape(128, 8)
    m["s5_bA"] = np.ascontiguousarray(bb.reshape(2, 8, 8, 64, 16).transpose(0, 2, 4, 1, 3)).reshape(2, 128, 512)
    m["s5_lamC"] = np.ascontiguousarray(lam.reshape(2, 32, 2, 64).transpose(0, 2, 3, 1)).reshape(2, 128, 32)
    m["s5_lsC"] = np.ascontiguousarray(np.broadcast_to(ls.reshape(32, 2).T[:, None, :], (2, 64, 32))).reshape(128, 32)
    m["s5_cC"] = np.ascontiguousarray(cc.reshape(2, 32, 2, 16, 64).transpose(0, 2, 4, 1, 3)).reshape(2, 128, 512)
    m["s5_lam_row"] = lam.reshape(2, 4096)
    m["s5_ls_row"] = np.ascontiguousarray(np.broadcast_to(ls[:, None], (64, 64))).reshape(1, 4096)
    m["s5_glu_w"] = np.asarray(inp["s5_glu_w"][0], np.float32)
    m["s5_d_fm"] = _fm(inp["s5_d"][0], 8)
    q = np.arange(128)
    s5m = np.zeros((128, 8), np.float32)
    s5m[:, 0] = ((q // 16) % 2 == 0); s5m[:, 1] = ((q // 16) % 2 == 1)
    for p4 in range(4):
        s5m[:, 2 + p4] = (q // 32 == p4)
    s5m[:, 6] = (q // 64 == 0); s5m[:, 7] = (q // 64 == 1)
    m["c_s5masks"] = s5m
    m["c_iota"] = np.tile(np.arange(1, 129, dtype=np.float32)[None, :], (128, 1))
    r8 = np.ones((128, 512), np.float32); r8[:, 0::8] = 0.0
    m["c_rst8"] = r8
    rj = np.ones((128, 128), np.float32); rj[:, 0] = 0.0
    m["c_rstj"] = rj
    m["st_s5_re"] = np.ascontiguousarray(inp["state_s5_re"][0, sl].reshape(NS, 4096))
    m["st_s5_im"] = np.ascontiguousarray(inp["state_s5_im"][0, sl].reshape(NS, 4096))
    m["st_fconv"] = np.ascontiguousarray(
        inp["state_ffn_conv"][:, core * NS:(core + 1) * NS].reshape(2, NS * 2, 2 * FFN))
    return m


_CACHE = {}


def run_stage(inp, stage, cores=8, dbg=None):
    if stage not in _CACHE:
        _CACHE[stage] = build_program(stage, dbg)
    nc, P = _CACHE[stage]
    in_maps = []
    for c in range(cores):
        hm = host_inputs(inp, c)
        in_maps.append({k: hm[k] for k in P.din})
    res = run_bass_kernel_spmd(nc, in_maps, core_ids=list(range(cores)))
    return res.results


OFF_Z, OFF_X, OFF_B, OFF_C, OFF_DT, OFF_Q, OFF_F, OFF_I, OFF_G = 0, 1024, 2048, 2176, 2304, 2320, 3344, 4368, 5392
NTB = SEG // 128


class ProgAB(Prog):
    def setup_ab(self):
        nc, S, es = self.nc, self.S, self.es
        self.ab_in_w = self.inp("ab_in_w", [D, AB_IN])
        self.ab_out_w = self.inp("ab_out_w", [2 * D, D])
        c_tri = self.inp("c_tri", [128, 128])
        c_neg = self.inp("c_neg", [128, 128])
        c_bmask = self.inp("c_bmask", [128, 128])
        c_rst = self.inp("c_rst", [128, SEG])
        self.TRI = self.sb(es, "TRI", [128, 128])
        self.NEG = self.sb(es, "NEG", [128, 128])
        self.BMASK = self.sb(es, "BMASK", [128, 128])
        self.RST = self.sb(es, "RST", [128, SEG])
        self.ONESF = self.sb(es, "ONESF", [128, 128])
        self.ONEC = self.sb(es, "ONEC", [128, 1])
        S.dma('sp', self.TRI[:], c_tri[:, :], writes=['TRI'])
        S.dma('sp', self.NEG[:], c_neg[:, :], writes=['NEG'])
        S.dma('sp', self.BMASK[:], c_bmask[:, :], writes=['BMASK'])
        S.dma('sp', self.RST[:], c_rst[:, :], writes=['RST'])
        S.op('dve', lambda: nc.vector.memset(self.ONESF[:], 1.0), [], ['ONESF'])
        S.op('dve', lambda: nc.vector.memset(self.ONEC[:], 1.0), [], ['ONEC'])
        scw = self.inp("scw", [128, 40])
        scb = self.inp("scb", [128, 10])
        self.SCW = self.sb(es, "SCW", [128, 10, 4])
        self.SCB = self.sb(es, "SCB", [128, 10])
        S.dma('sp', self.SCW[:].rearrange("p b k -> p (b k)"), scw[:, :], writes=['SCW'])
        S.dma('sp', self.SCB[:], scb[:, :], writes=['SCB'])
        v16 = self.inp("ssm_v16", [3, 16])
        self.V16 = self.sb(es, "V16", [128, 3, 16])
        for r in range(3):
            S.dma('sp', self.V16[:, r, :], v16[r:r + 1, :].partition_broadcast(128), writes=['V16'])
        self.ABC = self.sb(es, "ABC", [128, 16])
        S.op('act', lambda: nc.scalar.activation(out=self.ABC[:], in_=self.V16[:, 1, :], func=AF.Exp), ['V16'], ['ABC'])
        S.op('dve', lambda: nc.vector.tensor_scalar(out=self.ABC[:], in0=self.ABC[:], scalar1=-1.0, scalar2=None,
                                                    op0=ALU.mult), ['ABC'], ['ABC'])
        snw = self.inp("snw", [128, 8])
        self.SNW = self.sb(es, "SNW", [128, 8])
        S.dma('sp', self.SNW[:], snw[:, :], writes=['NW'])
        self.XH = self.sb(es, "XH", [128, 10, 3])
        self.HS = self.sb(es, "HS", [128, 512])
        self.HSB = self.sb(es, "HSB", [128, 512], BF16)
        S.op('dve', lambda: nc.vector.memset(self.XH[:].rearrange("p b k -> p (b k)"), 0.0), [], ['XH'])
        S.op('dve', lambda: nc.vector.memset(self.HS[:], 0.0), [], ['HS'])
        S.op('dve', lambda: nc.vector.memset(self.HSB[:], 0.0), [], ['HSB'])
        self.DIH = self.sb(es, "DIH", [128, 16, 128], BF16)
        for h in range(16):
            S.op('dve', lambda: nc.vector.tensor_scalar(out=self.DIH[:, h, :], in0=self.ident[:], scalar1=self.V16[:, 2, h:h + 1],
                                                        scalar2=None, op0=ALU.mult), ['ident', 'V16'], ['DIH'])
        self.p_sconv = self.outp("p_sconv", [3, 1280])
        self.p_ssm = self.outp("p_ssm", [16, 64, 64])

    def bfbank(self, b):
        return self.PB[b][:, :].bitcast(BF16)

    def ssd_phase(self, seg, HN, MIX):
        nc, S = self.nc, self.S
        W = self.ab_in_w
        ncol = seg_cols(seg)
        with ExitStack() as es:
            XTOK = self.sb(es, "xtok", [128, NTB, D], BF16)
            BFM = self.sb(es, "bfm", [128, SEG], BF16)
            BFMG = [self.sb(es, "bfmg%d" % g, [128, SEG], BF16) for g in range(2)]
            for g in range(2):
                S.op('pool', lambda: nc.gpsimd.memset(BFMG[g][:], 0.0), [], [('bfmg', g)])
            CFM = self.sb(es, "cfm", [128, SEG], BF16)
            BTOK = self.sb(es, "btok", [128, NTB, 128], BF16)
            DT = self.sb(es, "dt", [128, NTB, 16])
            DTA = self.sb(es, "dta", [128, NTB, 16])
            CUMT = self.sb(es, "cumt", [128, NTB, 16])
            wdt, wk = self.load_w(W[:, OFF_DT:OFF_DT + 16], 8, ncol=16)
            for tb in range(NTB):
                b = self.bank()
                self.mm(self.PB[b][:, 0:16], [('pb', b)],
                        [(HN[:, kc, tb * 128:(tb + 1) * 128], wdt[:, kc, :]) for kc in range(8)],
                        [wk] + tkeys('hn', range(8), tb // 4))
                S.op('dve', lambda: nc.vector.tensor_tensor(out=DT[:, tb, :], in0=self.PB[b][:, 0:16],
                                                            in1=self.V16[:, 0, :], op=ALU.add),
                     [('pb', b), 'V16'], ['dt'])
            if seg == 0:
                self.sample_proj('dt', 0, wdt, wk, HN, ncols=16)
            dt2 = DT[:].rearrange("p a h -> p (a h)")
            S.op('act', lambda: nc.scalar.activation(out=dt2, in_=dt2, func=AF.Exp), ['dt'], ['dt'])
            S.op('act', lambda: nc.scalar.activation(out=dt2, in_=dt2, func=AF.Ln, bias=self.ONEC[:, 0:1], scale=1.0),
                 ['dt', 'ONEC'], ['dt'])
            S.op('dve', lambda: nc.vector.tensor_tensor(out=DTA[:], in0=DT[:],
                                                        in1=self.ABC[:].unsqueeze(1).to_broadcast([128, NTB, 16]),
                                                        op=ALU.mult), ['dt', 'ABC'], ['dta'])
            if self.dbg.get('ssd_stop') == 1:
                S.barrier()
                return
            with ExitStack() as es1:
                PRE = [self.sb(es1, "spre%d" % i, [128, 3 + SEG]) for i in range(2)]
                ACC = [self.sb(es1, "sacc%d" % i, [128, SEG]) for i in range(2)]
                XS = [self.sb(es1, "sxs%d" % i, [128, SEG], BF16) for i in range(2)]
                blks = [8, 9, 0, 1, 2, 3, 4, 5, 6, 7]
                ws_x = wstream(self, [(W[:, OFF_X + blk * 128:OFF_X + (blk + 1) * 128], 8) for blk in blks])
                for it, blk in enumerate(blks):
                    pre, acc, xs = PRE[it % 2], ACC[it % 2], XS[it % 2]
                    kpre, kacc, kxs = ('spre', it % 2), ('sacc', it % 2), ('sxs', it % 2)
                    wv, wk = next(ws_x)
                    S.op('pool', lambda: nc.gpsimd.tensor_copy(out=pre[:, 0:3], in_=self.XH[:, blk, :]), ['XH'], [kpre])
                    for ti in range(2):
                        c0 = ti * 512
                        b = self.bank()
                        self.mm(self.PB[b][:, :], [('pb', b)],
                                [(wv[:, kc, :], HN[:, kc, c0:c0 + 512]) for kc in range(8)],
                                [wk] + tkeys('hn', range(8), ti))
                        self.copy(self.evac_eng(), pre[:, 3 + c0:3 + c0 + 512], self.PB[b][:, :], [('pb', b)], [kpre])
                    if seg == 0:
                        self.ssd_sample_proj(blk, wv, wk, HN)
                    S.op('pool', lambda: nc.gpsimd.tensor_copy(out=self.XH[:, blk, :], in_=pre[:, SEG:SEG + 3]),
                         [kpre], ['XH'])
                    w = self.SCW[:, blk, :]
                    S.op('act', lambda: nc.scalar.activation(out=acc[:], in_=pre[:, 3:3 + SEG], func=AF.Identity,
                                                             bias=self.SCB[:, blk:blk + 1], scale=w[:, 3:4]),
                         [kpre, 'SCW', 'SCB'], [kacc])
                    for k in (2, 1, 0):
                        S.op('dve', lambda: nc.vector.scalar_tensor_tensor(
                            out=acc[:], in0=pre[:, k:k + SEG], scalar=w[:, k:k + 1], in1=acc[:],
                            op0=ALU.mult, op1=ALU.add), [kpre, kacc, 'SCW'], [kacc])
                    dst, kd = (BFM, 'bfm') if blk == 8 else (CFM, 'cfm') if blk == 9 else (xs, kxs)
                    S.op('act', lambda: nc.scalar.activation(out=dst[:], in_=acc[:], func=AF.Silu), [kacc], [kd])
                    if blk == 9:
                        continue
                    if blk == 8:
                        for g in range(2):
                            ps = slice(g * 64, (g + 1) * 64)
                            self.copy('pool', BFMG[g][ps, :], BFM[ps, :], ['bfm'], [('bfmg', g)])
                    for half in range(2):
                        b = self.bank()
                        pbf = self.bfbank(b)
                        for j in range(4):
                            tb = half * 4 + j
                            self.transpose(pbf[:, j * 128:(j + 1) * 128], [('pb', b)],
                                           dst[:, tb * 128:(tb + 1) * 128], self.identb[:], [kd, 'identb'])
                        src = pbf[:, 0:512].rearrange("p (j f) -> p j f", f=128)
                        if blk == 8:
                            self.copy(self.evac_eng(), BTOK[:, half * 4:(half + 1) * 4, :], src, [('pb', b)], ['btok'])
                        else:
                            self.copy(self.evac_eng(), XTOK[:, half * 4:(half + 1) * 4, blk * 128:(blk + 1) * 128], src,
                                      [('pb', b)], ['xtok'])
                S.barrier()
            if seg == NSEG - 1:
                with nc.allow_non_contiguous_dma(reason="tiny conv-state scatter"):
                    for k in range(3):
                        S.dma('sp', self.p_sconv[k].rearrange("(b p) -> p b", p=128), self.XH[:, :, k], reads=['XH'])
            if self.dbg.get('ssd_stop') == 2:
                S.barrier()
                return
            with ExitStack() as es1:
                R = self.sb(es1, "ssd_r", [128, 16, 128])
                CB = self.sb(es1, "ssd_cb", [128, 16, 128])
                ECB = self.sb(es1, "ssd_ecb", [128, 16, 128])
                CPG = [self.sb(es1, "ssd_cp%d" % g, [128, 8, 128], BF16) for g in range(2)]
                for g in range(2):
                    S.op('pool', lambda: nc.gpsimd.memset(CPG[g][:].rearrange("p h t -> p (h t)"), 0.0), [], [('ssd_cp', g)])
                XW = self.sb(es1, "ssd_xw", [128, D], BF16)
                XDT = self.sb(es1, "ssd_xdt", [128, D], BF16)
                NCUM = self.sb(es1, "ssd_ncum", [128, 16])
                WE = self.sb(es1, "ssd_we", [128, 16])
                LL = [self.sb(es1, "ssd_l%d" % i, [128, 128]) for i in range(3)]
                WT = [self.sb(es1, "ssd_wt%d" % i, [128, 128], BF16) for i in range(3)]
                for tb in range(NTB):
                    cs = slice(tb * 128, (tb + 1) * 128)
                    b = self.bank()
                    self.mm(self.PB[b][:, 0:16], [('pb', b)], [(self.TRI[:], DTA[:, tb, :])], ['TRI', 'dta'])
                    self.copy('act', CUMT[:, tb, :], self.PB[b][:, 0:16], [('pb', b)], [('cumt', tb)])
                    if self.dbg.get('ssd_stop') == 3:
                        S.barrier()
                        return
                    S.op('dve', lambda: nc.vector.tensor_tensor(
                        out=R[:], in0=self.TRI[:].unsqueeze(1).to_broadcast([128, 16, 128]),
                        in1=DTA[:, tb, :].unsqueeze(2).to_broadcast([128, 16, 128]), op=ALU.mult),
                        ['TRI', 'dta'], ['ssd_r'])
                    for hg in range(4):
                        b = self.bank()
                        self.mm(self.PB[b][:, :], [('pb', b)],
                                [(self.ONESF[:], R[:, hg * 4:(hg + 1) * 4, :].rearrange("p h t -> p (h t)"))],
                                ['ONESF', 'ssd_r'])
                        self.copy('act', CB[:, hg * 4:(hg + 1) * 4, :].rearrange("p h t -> p (h t)"), self.PB[b][:, :],
                                  [('pb', b)], ['ssd_cb'])
                    S.op('act', lambda: nc.scalar.activation(out=ECB[:].rearrange("p h t -> p (h t)"),
                                                             in_=CB[:].rearrange("p h t -> p (h t)"), func=AF.Exp),
                         ['ssd_cb'], ['ssd_ecb'])
                    if self.dbg.get('ssd_stop') == 4:
                        S.barrier()
                        return
                    for g in range(2):
                        ps = slice(g * 64, (g + 1) * 64)
                        S.op('dve', lambda: nc.vector.tensor_tensor(
                            out=CPG[g][ps, :, :], in0=CFM[ps, cs].unsqueeze(1).to_broadcast([64, 8, 128]),
                            in1=ECB[ps, g * 8:(g + 1) * 8, :], op=ALU.mult), ['cfm', 'ssd_ecb'], [('ssd_cp', g)])
                    S.op('dve', lambda: nc.vector.tensor_tensor(out=WE[:], in0=CB[:, :, 127], in1=CUMT[:, tb, :],
                                                                op=ALU.subtract), ['ssd_cb', ('cumt', tb)], ['ssd_we'])
                    S.op('act', lambda: nc.scalar.activation(out=WE[:], in_=WE[:], func=AF.Exp), ['ssd_we'], ['ssd_we'])
                    S.op('dve', lambda: nc.vector.tensor_tensor(
                        out=XDT[:].rearrange("p (h q) -> p h q", q=64),
                        in0=XTOK[:, tb, :].rearrange("p (h q) -> p h q", q=64),
                        in1=DT[:, tb, :].unsqueeze(2).to_broadcast([128, 16, 64]), op=ALU.mult),
                        ['xtok', 'dt'], ['ssd_xdt'])
                    S.op('dve', lambda: nc.vector.tensor_tensor(
                        out=XW[:].rearrange("p (h q) -> p h q", q=64),
                        in0=XDT[:].rearrange("p (h q) -> p h q", q=64),
                        in1=WE[:].unsqueeze(2).to_broadcast([128, 16, 64]), op=ALU.mult),
                        ['ssd_xdt', 'ssd_we'], ['ssd_xw'])
                    S.op('dve', lambda: nc.vector.tensor_tensor(out=CB[:], in0=CB[:], in1=self.NEG[:].unsqueeze(1).to_broadcast([128, 16, 128]),
                                                                op=ALU.add), ['ssd_cb', 'NEG', 'ssd_ecb', 'ssd_we'], ['ssd_cb'])
                    S.op('dve', lambda: nc.vector.tensor_scalar(out=NCUM[:], in0=CUMT[:, tb, :], scalar1=-1.0, scalar2=None, op0=ALU.mult),
                         [('cumt', tb)], ['ssd_ncum'])
                    if self.dbg.get('ssd_stop') == 5:
                        S.barrier()
                        return
                    bcb = self.bank()
                    for g in range(2):
                        ps = slice(g * 64, (g + 1) * 64)
                        self.mm(self.PB[bcb][:, g * 128:(g + 1) * 128], [('pb', bcb)],
                                [(BFMG[g][:, cs], CFM[:, cs])], [('bfmg', g), 'cfm'])
                    if self.dbg.get('ssd_stop') == 51:
                        S.barrier()
                        return
                    for q in range(4):
                        by = self.bank()
                        for hh in range(4):
                            h = q * 4 + hh
                            g = h // 8
                            i2 = h % 3
                            ll, wt = LL[i2], WT[i2]
                            S.op('act', lambda: nc.scalar.activation(out=ll[:], in_=CB[:, h, :], func=AF.Exp,
                                                                     bias=NCUM[:, h:h + 1], scale=1.0),
                                 ['ssd_cb', 'ssd_ncum'], [('ll', i2)])
                            S.op('dve', lambda: nc.vector.tensor_tensor(
                                out=wt[:], in0=ll[:], in1=self.PB[bcb][:, g * 128:(g + 1) * 128], op=ALU.mult),
                                [('ll', i2), ('pb', bcb)], [('wt', i2)])
                            pr = h // 2
                            gs = slice(g * 64, (g + 1) * 64)
                            hp = (pr % 4) * 128
                            if self.dbg.get('ssd_stop') == 52:
                                continue
                            prs = [(XDT[:, pr * 128:(pr + 1) * 128], wt[:]),
                                   (XTOK[:, tb, pr * 128:(pr + 1) * 128], self.DIH[:, h, :]),
                                   (self.HSB[:, hp:hp + 128], CPG[g][:, h % 8, :])]
                            if self.dbg.get('ssd_stop') == 53:
                                prs = prs[:1]
                            self.mm(self.PB[by][:, hh * 128:(hh + 1) * 128], [('pb', by)], prs,
                                    ['xtok', 'ssd_xdt', 'DIH', ('wt', i2), 'HSB', ('ssd_cp', g)])
                        if self.dbg.get('ssd_stop') in (52, 54):
                            continue
                        v = self.PB[by][:, :].rearrange("p (h t) -> p h t", t=128)
                        for par in range(2):
                            ps = slice(par * 64, (par + 1) * 64)
                            self.copy(self.evac_eng(), MIX[ps, 2 * q:2 * q + 2, cs], v[ps, par::2, :],
                                      [('pb', by)], [('mix', 2 * q, tb // 4), ('mix', 2 * q + 1, tb // 4)])
                    if self.dbg.get('ssd_stop') == 6:
                        S.barrier()
                        return
                    if self.dbg.get('ssd_stop') in (52, 53, 54):
                        continue
                    for g in range(2):
                        ps = slice(g * 64, (g + 1) * 64)
                        b = self.bank()
                        self.mm(self.PB[b][:, :], [('pb', b)], [(BTOK[:, tb, :], XW[:, g * 512:(g + 1) * 512])],
                                ['btok', 'ssd_xw'])
                        S.op('dve', lambda: nc.vector.tensor_tensor(
                            out=self.HS[ps, :].rearrange("p (h q) -> p h q", q=64),
                            in0=self.HS[ps, :].rearrange("p (h q) -> p h q", q=64),
                            in1=ECB[ps, g * 8:(g + 1) * 8, 127:128].to_broadcast([64, 8, 64]), op=ALU.mult),
                            ['HS', 'ssd_ecb', 'HSB'], ['HS'])
                        S.op('dve', lambda: nc.vector.tensor_tensor(out=self.HS[ps, :], in0=self.HS[ps, :],
                                                                    in1=self.PB[b][ps, :], op=ALU.add),
                             ['HS', ('pb', b)], ['HS'])
                        self.copy('act', self.HSB[ps, :], self.HS[ps, :], ['HS'], ['HSB'])
                S.barrier()
            if seg == NSEG - 1:
                self.ssd_final_state()
            S.barrier()

    def ssd_final_state(self):
        nc, S = self.nc, self.S
        with ExitStack() as es:
            ot = self.sb(es, "ssd_fs", [128, 8, 64])
            b = self.bank()
            for j in range(4):
                self.transpose(self.PB[b][:, j * 128:(j + 1) * 128], [('pb', b)],
                               self.HS[:, j * 128:(j + 1) * 128], self.ident[:], ['HS', 'ident'])
            self.copy('dve', ot[:].rearrange("p (g j) n -> p j g n", g=2),
                      self.PB[b][:, :].rearrange("p (j g n) -> p j g n", g=2, n=64), [('pb', b)], ['ssd_fs'])
            S.dma('sp', self.p_ssm.rearrange("(j a) p n -> (a p) j n", a=2), ot[:], reads=['ssd_fs'])
            S.barrier()

    def ssd_sample_proj(self, blk, wv, wk, HN):
        pass


class ProgAB2(ProgAB):
    NH = 2

    def setup_hg(self):
        nc, S, es = self.nc, self.S, self.es
        lbin = self.inp("hgrn_lb_fm", [128, 16])
        hnw = self.inp("hgrn_nw", [128, 1])
        c_cmask = self.inp("c_cmask", [128, 4])
        t = self.sb(es, "lbtmp", [128, 2, 8])
        self.LB = self.sb(es, "LB", [128, 8])
        self.OML = self.sb(es, "OML", [128, 8])
        self.NOML = self.sb(es, "NOML", [128, 8])
        self.HNW = self.sb(es, "HNW", [128, 1])
        self.CMASK = self.sb(es, "CMASK", [128, 4])
        S.dma('sp', t[:].rearrange("p r h -> p (r h)"), lbin[:, :], writes=['lbtmp'])
        S.dma('sp', self.HNW[:], hnw[:, :], writes=['HNW'])
        S.dma('sp', self.CMASK[:], c_cmask[:, :], writes=['CMASK'])
        S.op('dve', lambda: nc.vector.tensor_tensor(out=self.LB[:], in0=t[:, 0, :], in1=t[:, 1, :], op=ALU.subtract),
             ['lbtmp'], ['LB'])
        S.op('act', lambda: nc.scalar.activation(out=self.LB[:], in_=self.LB[:], func=AF.Sigmoid), ['LB'], ['LB'])
        S.op('dve', lambda: nc.vector.tensor_scalar(out=self.OML[:], in0=self.LB[:], scalar1=-1.0, scalar2=1.0,
                                                    op0=ALU.mult, op1=ALU.add), ['LB'], ['OML'])
        S.op('dve', lambda: nc.vector.tensor_scalar(out=self.NOML[:], in0=self.OML[:], scalar1=-1.0, scalar2=None,
                                                    op0=ALU.mult), ['OML'], ['NOML'])
        self.HGS = self.sb(es, "HGS", [128, 8, 128])
        self.HGSB = self.sb(es, "HGSB", [128, 8, 128], BF16)
        S.op('dve', lambda: nc.vector.memset(self.HGS[:].rearrange("p h v -> p (h v)"), 0.0), [], ['HGS'])
        S.op('dve', lambda: nc.vector.memset(self.HGSB[:].rearrange("p h v -> p (h v)"), 0.0), [], ['HGSB'])
        self.p_hg = self.outp("p_hg", [8, 128, 128])

    def hgrn_phase(self, seg, HN, MIX):
        nc, S = self.nc, self.S
        W = self.ab_in_w
        NH = self.NH
        NCH = SEG // 32
        with ExitStack() as es:
            T0 = self.sb(es, "hg_t0", [128, SEG])
            T1 = self.sb(es, "hg_t1", [128, SEG])
            T2 = self.sb(es, "hg_t2", [128, SEG])
            KH = self.sb(es, "hg_kh", [128, SEG], BF16)
            EM = self.sb(es, "hg_em", [128, NCH])
            RS = self.sb(es, "hg_rs", [128, SEG])
            SQ = self.sb(es, "hg_sq", [128, SEG], BF16)
            QT = [self.sb(es, "hg_qt%d" % j, [128, SEG], BF16) for j in range(NH)]
            KT = [self.sb(es, "hg_kt%d" % j, [128, SEG], BF16) for j in range(NH)]
            QH = [self.sb(es, "hg_qh%d" % j, [128, SEG], BF16) for j in range(NH)]
            KM = [self.sb(es, "hg_km%d" % j, [128, NTB, 4, 128], BF16) for j in range(NH)]
            ITOK = [self.sb(es, "hg_it%d" % j, [128, NTB, 128], BF16) for j in range(NH)]
            SG = [self.sb(es, "hg_sg%d" % j, [128, SEG], BF16) for j in range(NH)]
            OSB = [self.sb(es, "hg_o%d" % j, [128, SEG]) for j in range(NH)]
            DL = [self.sb(es, "hg_dl%d" % j, [128, NCH]) for j in range(NH)]
            AM = [self.sb(es, "hg_am%d" % j, [128, 128], BF16) for j in range(NH)]
            SPP = [[self.sb(es, "hg_spp%d_%d" % (j, i), [128, 128]) for i in range(3)] for j in range(NH)]
            SNAP = [[self.sb(es, "hg_snap%d_%d" % (i, j), [128, 4, 128], BF16) for j in range(NH)] for i in range(2)]
            ws_h = wstream(self, [(W[:, off + h * 128:off + (h + 1) * 128], 8)
                                  for h in range(8) for off in (OFF_Q, OFF_F, OFF_G, OFF_I, OFF_Z)])
            SZ = [self.sb(es, "sz%d" % i, [128, 512]) for i in range(2)]
            for hg in range(8 // NH):
                for j in range(NH):
                    h = hg * NH + j
                    for which, off in (('q', OFF_Q), ('f', OFF_F), ('g', OFF_G)):
                        wv, wk = next(ws_h)
                        for ti in range(2):
                            c0 = ti * 512
                            b = self.bank()
                            self.mm(self.PB[b][:, :], [('pb', b)],
                                    [(wv[:, kc, :], HN[:, kc, c0:c0 + 512]) for kc in range(8)],
                                    [wk] + tkeys('hn', range(8), ti))
                            if which == 'q':
                                S.op('act', lambda: nc.scalar.activation(out=QT[j][:, c0:c0 + 512], in_=self.PB[b][:, :],
                                                                         func=AF.Silu), [('pb', b)], [('qt', j)])
                            elif which == 'f':
                                S.op('act', lambda: nc.scalar.activation(out=T0[:, c0:c0 + 512], in_=self.PB[b][:, :],
                                                                         func=AF.Sigmoid), [('pb', b)], ['t0'])
                            else:
                                S.op('act', lambda: nc.scalar.activation(out=SG[j][:, c0:c0 + 512], in_=self.PB[b][:, :],
                                                                         func=AF.Silu), [('pb', b)], [('sg', j)])
                        if seg == 0:
                            self.hg_sample_proj(which, h, wv, wk, HN)
                    wv, wk = next(ws_h)
                    for half in range(2):
                        b = self.bank()
                        for jj in range(4):
                            tb = half * 4 + jj
                            self.mm(self.PB[b][:, jj * 128:(jj + 1) * 128], [('pb', b)],
                                    [(HN[:, kc, tb * 128:(tb + 1) * 128], wv[:, kc, :]) for kc in range(8)],
                                    [wk] + tkeys('hn', range(8), tb // 4))
                        self.copy(self.evac_eng(), ITOK[j][:, half * 4:(half + 1) * 4, :],
                                  self.PB[b][:, :].rearrange("p (a v) -> p a v", v=128), [('pb', b)], [('itok', j)])
                    if seg == 0:
                        self.hg_sample_proj('i', h, wv, wk, HN)
                    S.op('dve', lambda: nc.vector.tensor_scalar(out=T1[:], in0=T0[:], scalar1=self.NOML[:, h:h + 1],
                                                                scalar2=self.OML[:, h:h + 1], op0=ALU.mult, op1=ALU.add),
                         ['t0', 'NOML', 'OML'], ['t1'])
                    S.op('dve', lambda: nc.vector.tensor_scalar(out=T0[:], in0=T0[:], scalar1=self.OML[:, h:h + 1],
                                                                scalar2=self.LB[:, h:h + 1], op0=ALU.mult, op1=ALU.add),
                         ['t0', 'LB', 'OML'], ['t0'])
                    S.op('act', lambda: nc.scalar.activation(out=T0[:], in_=T0[:], func=AF.Ln), ['t0'], ['t0'])
                    S.op('dve', lambda: nc.vector.tensor_tensor_scan(out=T2[:], data0=self.RST[:], data1=T0[:], initial=0.0,
                                                                     op0=ALU.mult, op1=ALU.add), ['t0', 'RST'], ['t2'])
                    c3 = T2[:].rearrange("p (c t) -> p c t", t=32)
                    S.op('act', lambda: nc.scalar.activation(out=DL[j][:], in_=c3[:, :, 31], func=AF.Exp), ['t2'], [('dl', j)])
                    S.op('act', lambda: nc.scalar.activation(out=EM[:], in_=c3[:, :, 15], func=AF.Exp), ['t2'], ['em'])
                    S.op('dve', lambda: nc.vector.tensor_tensor(out=c3, in0=c3, in1=c3[:, :, 15:16].to_broadcast([128, NCH, 32]),
                                                                op=ALU.subtract), ['t2'], ['t2'])
                    S.op('act', lambda: nc.scalar.activation(out=T0[:], in_=T2[:], func=AF.Exp), ['t2', 't0'], ['t0'])
                    S.op('act', lambda: nc.scalar.activation(out=T2[:], in_=T2[:], func=AF.Exp, scale=-1.0), ['t2'], ['t2'])
                    S.op('dve', lambda: nc.vector.tensor_tensor(out=QT[j][:], in0=QT[j][:], in1=T0[:], op=ALU.mult),
                         [('qt', j), 't0'], [('qt', j)])
                    S.op('dve', lambda: nc.vector.tensor_tensor(out=KT[j][:], in0=T1[:], in1=T2[:], op=ALU.mult),
                         ['t1', 't2'], [('kt', j)])
                    S.op('pool', lambda: nc.gpsimd.tensor_tensor(
                        out=QH[j][:].rearrange("p (c t) -> p c t", t=32), in0=QT[j][:].rearrange("p (c t) -> p c t", t=32),
                        in1=EM[:].unsqueeze(2).to_broadcast([128, NCH, 32]), op=ALU.mult), [('qt', j), 'em'], [('qh', j)])
                    e3 = T0[:].rearrange("p (c t) -> p c t", t=32)
                    S.op('dve', lambda: nc.vector.tensor_tensor(
                        out=KH[:].rearrange("p (c t) -> p c t", t=32), in0=KT[j][:].rearrange("p (c t) -> p c t", t=32),
                        in1=e3[:, :, 31:32].to_broadcast([128, NCH, 32]), op=ALU.mult), [('kt', j), 't0'], ['kh'])
                    wv, wk = next(ws_h)
                    self.zgate_block(seg, h, wv, wk, HN, MIX, SZ)
                    for half in range(2):
                        b = self.bank()
                        pbf = self.bfbank(b)
                        for jj in range(4):
                            tb = half * 4 + jj
                            self.transpose(pbf[:, jj * 128:(jj + 1) * 128], [('pb', b)],
                                           KH[:, tb * 128:(tb + 1) * 128], self.identb[:], ['kh', 'identb'])
                        for ci in range(4):
                            S.op('act', lambda: nc.scalar.activation(
                                out=KM[j][:, half * 4:(half + 1) * 4, ci, :],
                                in_=pbf[:, 0:512].rearrange("p (a k) -> p a k", k=128), func=AF.Identity,
                                scale=self.CMASK[:, ci:ci + 1]), [('pb', b), 'CMASK'], [('km', j)])
                B_ATT = 0
                B_U = ((1, 2), (3, 4))
                B_IO = (5, 6)

                def emit_front(tb):
                    cs = slice(tb * 128, (tb + 1) * 128)
                    bint = B_IO[tb % 2]
                    for j in range(NH):
                        self.mm(self.PB[B_ATT][:, j * 128:(j + 1) * 128], [('pb', B_ATT)], [(KT[j][:, cs], QT[j][:, cs])],
                                [('kt', j), ('qt', j)])
                        S.op('dve', lambda: nc.vector.tensor_tensor(
                            out=AM[j][:], in0=self.PB[B_ATT][:, j * 128:(j + 1) * 128], in1=self.BMASK[:], op=ALU.mult),
                            [('pb', B_ATT), 'BMASK'], [('am', j)])
                        self.mm(self.PB[bint][:, j * 128:(j + 1) * 128], [('pb', bint)], [(ITOK[j][:, tb, :], AM[j][:])],
                                [('itok', j), ('am', j)])
                    for ci in range(4):
                        for j in range(NH):
                            k = ci * NH + j
                            bu = B_U[tb % 2][k // 4]
                            self.mm(self.PB[bu][:, (k % 4) * 128:(k % 4 + 1) * 128], [('pb', bu)],
                                    [(KM[j][:, tb, ci, :], ITOK[j][:, tb, :])], [('km', j), ('itok', j)])

                def emit_chain(tb):
                    for ci in range(4):
                        ch = tb * 4 + ci
                        for j in range(NH):
                            h = hg * NH + j
                            k = ci * NH + j
                            bu = B_U[tb % 2][k // 4]
                            ic, inx = ch % 3, (ch + 1) % 3
                            cur, nxt = SPP[j][ic], SPP[j][inx]
                            self.copy('act', SNAP[tb % 2][j][:, ci, :], cur[:], [('spp', j, ic)], [('snap', tb % 2, j)])
                            S.op('dve', lambda: nc.vector.scalar_tensor_tensor(
                                out=nxt[:], in0=cur[:], scalar=DL[j][:, ch:ch + 1],
                                in1=self.PB[bu][:, (k % 4) * 128:(k % 4 + 1) * 128], op0=ALU.mult, op1=ALU.add),
                                [('spp', j, ic), ('dl', j), ('pb', bu)], [('spp', j, inx)])

                def emit_back(tb):
                    cs = slice(tb * 128, (tb + 1) * 128)
                    bint = bin_ = B_IO[tb % 2]
                    for ci in range(4):
                        for j in range(NH):
                            c0 = tb * 128 + ci * 32
                            self.mm(self.PB[bin_][:, 256 + j * 128 + ci * 32:256 + j * 128 + ci * 32 + 32], [('pb', bin_)],
                                    [(SNAP[tb % 2][j][:, ci, :], QH[j][:, c0:c0 + 32])], [('snap', tb % 2, j), ('qh', j)])
                    for j in range(NH):
                        self.copy('act', OSB[j][:, cs], self.PB[bint][:, j * 128:(j + 1) * 128], [('pb', bint)], [('osb', j)])
                        S.op('dve', lambda: nc.vector.tensor_tensor(out=OSB[j][:, cs], in0=OSB[j][:, cs],
                                                                    in1=self.PB[bin_][:, 256 + j * 128:256 + (j + 1) * 128], op=ALU.add),
                             [('osb', j), ('pb', bin_)], [('osb', j)])

                for j in range(NH):
                    self.copy('dve', SPP[j][0][:], self.HGS[:, hg * NH + j, :], [('HGS', hg * NH + j)], [('spp', j, 0)])
                emit_front(0)
                for tb in range(NTB):
                    if tb >= 1:
                        emit_back(tb - 1)
                    if tb + 1 < NTB:
                        emit_front(tb + 1)
                    emit_chain(tb)
                emit_back(NTB - 1)
                fin = (NTB * 4) % 3
                for j in range(NH):
                    self.copy('dve', self.HGS[:, hg * NH + j, :], SPP[j][fin][:], [('spp', j, fin)], [('HGS', hg * NH + j)])
                for j in range(NH):
                    h = hg * NH + j
                    S.op('act', lambda: nc.scalar.activation(out=SQ[:], in_=OSB[j][:], func=AF.Square), [('osb', j)], ['hsq'])
                    for ti in range(2):
                        c0 = ti * 512
                        b = self.bank()
                        self.mm(self.PB[b][:, :], [('pb', b)], [(self.onesb[:], SQ[:, c0:c0 + 512])], ['hsq', 'onesb'])
                        S.op('act', lambda: nc.scalar.activation(out=RS[:, c0:c0 + 512], in_=self.PB[b][:, :], func=AF.Sqrt,
                                                                 bias=self.epsc[:, 0:1], scale=1.0 / 128.0),
                             [('pb', b), 'epsc'], ['hrs'])
                    S.op('dve', lambda: nc.vector.reciprocal(out=RS[:], in_=RS[:]), ['hrs'], ['hrs'])
                    S.op('dve', lambda: nc.vector.scalar_tensor_tensor(out=OSB[j][:], in0=OSB[j][:], scalar=self.HNW[:, 0:1],
                                                                       in1=RS[:], op0=ALU.mult, op1=ALU.mult),
                         [('osb', j), 'hrs', 'HNW'], [('osb', j)])
                    S.op('dve', lambda: nc.vector.tensor_tensor(out=MIX[:, 8 + h, 0:SEG], in0=OSB[j][:], in1=SG[j][:], op=ALU.mult),
                         [('osb', j), ('sg', j)], [('mix', 8 + h, 0), ('mix', 8 + h, 1)])
            if seg == NSEG - 1:
                for h in range(8):
                    S.dma('sp', self.p_hg[h], self.HGS[:, h, :], reads=[('HGS', h)])
            S.barrier()

    def hg_sample_proj(self, which, h, wv, wk, HN):
        pass


class ProgAB3(ProgAB2):
    def setup_ab_sample(self):
        nc = self.nc
        def scr(name, shape):
            return nc.dram_tensor(name, list(shape), F32, kind="Internal").ap()
        self.SC = {k: scr("sc_" + k, [NS, D]) for k in ('z', 'q', 'f', 'i', 'g', 'y', 'o')}
        self.SC['xbc'] = scr("sc_xbc", [NS, 1280])
        self.SC['xa'] = scr("sc_xa", [NS, D])
        self.SC['bc'] = scr("sc_bc", [NS, 256])
        self.SC['dt'] = scr("sc_dt", [NS, 16])
        self.st_sconv = self.inp("st_sconv", [NS, 3, 1280])
        self.st_ssm = self.inp("st_ssm", [NS * 16, 4096])
        self.st_hg = self.inp("st_hg", [NS * 8, 128 * 128])
        self.s_sconv = self.outp("s_sconv", [NS, 3, 1280])
        self.s_ssm = self.outp("s_ssm", [NS * 16, 4096])
        self.s_hg = self.outp("s_hg", [NS * 8, 128 * 128])
        self.ssm_conv_wb = self.inp("ssm_conv_wb", [5, 1280])
        self.v16bh = self.inp("v16bh", [128, 3])
        self.snw_row = self.inp("snw_row", [1, D])
        self.hnw_row = self.inp("hnw_row", [1, 128])
        self.hgrn_lb_bh = self.inp("hgrn_lb_bh", [2, 128, 128])
        self.stg = [self.sb(self.es, "stg%d" % i, [NS, 128]) for i in range(4)]
        self.stg_rr = 0

    def stage_tok(self, b, name, col0, ncols=128, func=None):
        nc, S = self.nc, self.S
        i = self.stg_rr
        self.stg_rr = (self.stg_rr + 1) % len(self.stg)
        t = self.stg[i]
        self.copy('dve', t[:, 0:ncols], self.PB[b][0:NS, 0:ncols], [('pb', b)], [('stg', i)])
        S.dma('sp', self.SC[name][:, col0:col0 + ncols], t[:, 0:ncols], reads=[('stg', i)], writes=[('sc', name)])

    def sample_proj(self, name, col0, wv, wk, HN, ncols=128):
        b = self.bank()
        self.mm(self.PB[b][0:NS, 0:ncols], [('pb', b)],
                [(HN[:, kc, SEG:SEG + NS], wv[:, kc, :]) for kc in range(8)], [wk] + tkeys('hn', range(8), 2))
        self.stage_tok(b, name, col0, ncols)

    def ssd_sample_proj(self, blk, wv, wk, HN):
        self.sample_proj('xbc', blk * 128, wv, wk, HN)

    def hg_sample_proj(self, which, h, wv, wk, HN):
        self.sample_proj(which, h * 128, wv, wk, HN)

    def zgate_block(self, seg, c, wv, wk, HN, MIX, SZ):
        nc, S = self.nc, self.S
        for ti in range(2):
            c0 = ti * 512
            b = self.bank()
            self.mm(self.PB[b][:, :], [('pb', b)], [(wv[:, kc, :], HN[:, kc, c0:c0 + 512]) for kc in range(8)],
                    [wk] + tkeys('hn', range(8), ti))
            sz = SZ[ti]
            S.op('act', lambda: nc.scalar.activation(out=sz[:], in_=self.PB[b][:, :], func=AF.Silu),
                 [('pb', b)], [('sz', ti)])
            S.op('dve', lambda: nc.vector.tensor_tensor(out=MIX[:, c, c0:c0 + 512], in0=MIX[:, c, c0:c0 + 512],
                                                        in1=sz[:], op=ALU.mult),
                 [('sz', ti), ('mix', c, ti)], [('mix', c, ti)])
        if seg == 0:
            self.sample_proj('z', c * 128, wv, wk, HN)

    def zgate_phase(self, seg, HN, MIX):
        nc, S = self.nc, self.S
        W = self.ab_in_w
        with ExitStack() as es:
            SQ = [self.sb(es, "zsq%d" % i, [128, 8, 512], BF16) for i in range(2)]
            RS = [self.sb(es, "zrs%d" % i, [128, 512]) for i in range(2)]
            for ti in range(2):
                c0 = ti * 512
                sq, rs = SQ[ti], RS[ti]
                S.op('act', lambda: nc.scalar.activation(out=sq[:], in_=MIX[:, 0:8, c0:c0 + 512], func=AF.Square),
                     tkeys('mix', range(8), ti), [('zsq', ti)])
                b = self.bank()
                self.mm(self.PB[b][:, :], [('pb', b)], [(self.onesb[:], sq[:, c, :]) for c in range(8)], [('zsq', ti), 'onesb'])
                S.op('act', lambda: nc.scalar.activation(out=rs[:], in_=self.PB[b][:, :], func=AF.Sqrt,
                                                         bias=self.epsc[:, 0:1], scale=1.0 / 1024.0), [('pb', b), 'epsc'], [('zrs', ti)])
                S.op('dve', lambda: nc.vector.reciprocal(out=rs[:], in_=rs[:]), [('zrs', ti)], [('zrs', ti)])
                for c in range(8):
                    S.op('dve', lambda: nc.vector.scalar_tensor_tensor(
                        out=MIX[:, c, c0:c0 + 512], in0=MIX[:, c, c0:c0 + 512], scalar=self.SNW[:, c:c + 1], in1=rs[:],
                        op0=ALU.mult, op1=ALU.mult), [('mix', c, ti), ('zrs', ti), 'NW'], [('mix', c, ti)])
            S.barrier()

    def ab_sample_phase(self, MIX):
        nc, S = self.nc, self.S
        SC = self.SC
        tt = nc.vector.tensor_tensor
        with ExitStack() as es:
            XB = self.sb(es, "s_xb", [NS, 1280])
            CS = self.sb(es, "s_cs", [NS, 3, 1280])
            CW = self.sb(es, "s_cw", [NS, 5, 1280])
            AC = self.sb(es, "s_ac", [NS, 1280])
            TM = self.sb(es, "s_tm", [NS, 1280])
            S.dma('sp', XB[:], SC['xbc'][:, :], reads=[('sc', 'xbc')], writes=['s_xb'])
            S.dma('sp', CS[:], self.st_sconv[:, :, :], writes=['s_cs'])
            for r in range(5):
                S.dma('sp', CW[:, r, :], self.ssm_conv_wb[r:r + 1, :].partition_broadcast(NS), writes=['s_cw'])
            S.dma('sp', self.s_sconv[:, 0:2, :], CS[:, 1:3, :], reads=['s_cs'])
            S.dma('sp', self.s_sconv[:, 2, :], XB[:], reads=['s_xb'])
            S.op('dve', lambda: tt(out=AC[:], in0=XB[:], in1=CW[:, 3, :], op=ALU.mult), ['s_xb', 's_cw'], ['s_ac'])
            S.op('dve', lambda: tt(out=AC[:], in0=AC[:], in1=CW[:, 4, :], op=ALU.add), ['s_ac', 's_cw'], ['s_ac'])
            for k in range(3):
                S.op('dve', lambda: tt(out=TM[:], in0=CS[:, k, :], in1=CW[:, k, :], op=ALU.mult), ['s_cs', 's_cw'], ['s_tm'])
                S.op('dve', lambda: tt(out=AC[:], in0=AC[:], in1=TM[:], op=ALU.add), ['s_ac', 's_tm'], ['s_ac'])
            S.op('act', lambda: nc.scalar.activation(out=AC[:], in_=AC[:], func=AF.Silu), ['s_ac'], ['s_ac'])
            S.dma('sp', SC['xa'][:, :], AC[:, 0:D], reads=['s_ac'], writes=[('sc', 'xa')])
            S.dma('sp', SC['bc'][:, :], AC[:, D:1280], reads=['s_ac'], writes=[('sc', 'bc')])
            S.barrier()
        with ExitStack() as es:
            BA = self.sb(es, "s_bufA", [128, 4096])
            BC = self.sb(es, "s_bufC", [128, 4096])
            OP = self.sb(es, "s_op", [128, 4096])
            STB = [BA, BC]
            Q = self.sb(es, "g_q", [128, 128])
            Fg = self.sb(es, "g_f", [128, 128])
            KK = self.sb(es, "g_kk", [128, 128])
            Iv = self.sb(es, "g_i", [128, 128])
            G = self.sb(es, "g_g", [128, 128])
            L0 = self.sb(es, "g_l0", [128, 128])
            L1 = self.sb(es, "g_l1", [128, 128])
            NWr = self.sb(es, "g_nw", [128, 128])
            O = self.sb(es, "g_o", [128, 128])
            O2 = self.sb(es, "g_o2", [128, 128])
            SS = self.sb(es, "g_ss", [128, 1])
            Xh = [self.sb(es, "s_xh%d" % i, [128, 64]) for i in range(2)]
            Bh = [self.sb(es, "s_bh%d" % i, [128, 64]) for i in range(2)]
            Ch = [self.sb(es, "s_ch%d" % i, [128, 64]) for i in range(2)]
            DTh = [self.sb(es, "s_dth%d" % i, [128, 1]) for i in range(2)]
            V3 = self.sb(es, "s_v3", [128, 3])
            DEC = self.sb(es, "s_dec", [128, 1])
            DTX = self.sb(es, "s_dtx", [128, 64])
            Yh = self.sb(es, "s_yh", [128, 64])
            xa_bh = SC['xa'].rearrange("b (h p) -> (b h) p", p=64)
            dt_bh = SC['dt'].rearrange("b (h o) -> (b h) o", o=1)
            y_bh = SC['y'].rearrange("b (h p) -> (b h) p", p=64)
            S.dma('sp', V3[:], self.v16bh[:, :], writes=['s_v3'])
            st_v = self.st_ssm.rearrange("(b g r) f -> g b r f", g=2, r=8)
            so_v = self.s_ssm.rearrange("(b g r) f -> g b r f", g=2, r=8)
            xa_v = SC['xa'].rearrange("b (g r p) -> g b r p", g=2, r=8)
            dt_v = SC['dt'].rearrange("b (g r o) -> g b r o", g=2, o=1)
            y_v = SC['y'].rearrange("b (g r p) -> g b r p", g=2, r=8)
            bc_v = SC['bc'].rearrange("b (w g n) -> w g b n", w=2, g=2)
            for half in range(2):
                bs = slice(half * 8, (half + 1) * 8)
                for g in range(2):
                    gp = slice(g * 64, (g + 1) * 64)
                    S.dma('sp', STB[half][gp, :], st_v[g, bs], writes=[('stb', half)])
            for half in range(2):
                bs = slice(half * 8, (half + 1) * 8)
                for g in range(2):
                    gp = slice(g * 64, (g + 1) * 64)
                    S.dma('sp', Xh[half][gp, :], xa_v[g, bs], reads=[('sc', 'xa')], writes=[('s_xh', half)])
                    S.dma('sp', DTh[half][gp, :], dt_v[g, bs], reads=[('sc', 'dt')], writes=[('s_dth', half)])
                    for w, (dst, kd) in enumerate(((Bh[half], 's_bh'), (Ch[half], 's_ch'))):
                        src = bc_v[w, g, bs].unsqueeze(1).to_broadcast([8, 8, 64])
                        S.dma('sp', dst[gp, :], src, reads=[('sc', 'bc')], writes=[(kd, half)])
            for t_, name in ((Q, 'q'), (Fg, 'f'), (Iv, 'i'), (G, 'g')):
                S.dma('sp', t_[:], SC[name].rearrange("b (h k) -> (b h) k", k=128), reads=[('sc', name)], writes=['g_' + name])
            for t_, r in ((L0, 0), (L1, 1)):
                S.dma('sp', t_[:], self.hgrn_lb_bh[r], writes=['g_l%d' % r])
            S.dma('sp', NWr[:], self.hnw_row[0:1, :].partition_broadcast(128), writes=['g_nw'])
            for half in range(2):
                rs = slice(half * 128, (half + 1) * 128)
                ST = STB[half]
                kst = ('stb', half)
                xh, bh, ch, dth = Xh[half], Bh[half], Ch[half], DTh[half]
                kx, kb, kc_, kd_ = ('s_xh', half), ('s_bh', half), ('s_ch', half), ('s_dth', half)
                S.op('dve', lambda: tt(out=dth[:], in0=dth[:], in1=V3[:, 0:1], op=ALU.add), [kd_, 's_v3'], [kd_])
                S.op('act', lambda: nc.scalar.activation(out=dth[:], in_=dth[:], func=AF.Exp), [kd_], [kd_])
                S.op('act', lambda: nc.scalar.activation(out=dth[:], in_=dth[:], func=AF.Ln, bias=self.ONEC[:, 0:1], scale=1.0),
                     [kd_, 'ONEC'], [kd_])
                S.op('act', lambda: nc.scalar.activation(out=DEC[:], in_=V3[:, 1:2], func=AF.Exp), ['s_v3'], ['s_dec'])
                S.op('dve', lambda: tt(out=DEC[:], in0=DEC[:], in1=dth[:], op=ALU.mult), ['s_dec', kd_], ['s_dec'])
                S.op('act', lambda: nc.scalar.activation(out=DEC[:], in_=DEC[:], func=AF.Exp, scale=-1.0), ['s_dec'], ['s_dec'])
                S.op('dve', lambda: nc.vector.tensor_scalar(out=DTX[:], in0=xh[:], scalar1=dth[:, 0:1], scalar2=None, op0=ALU.mult),
                     [kx, kd_], ['s_dtx'])
                o3 = OP[:].rearrange("q (p n) -> q p n", n=64)
                s3 = ST[:].rearrange("q (p n) -> q p n", n=64)
                S.op('dve', lambda: tt(out=o3, in0=DTX[:].unsqueeze(2).to_broadcast([128, 64, 64]),
                                       in1=bh[:].unsqueeze(1).to_broadcast([128, 64, 64]), op=ALU.mult),
                     ['s_dtx', kb], ['s_op'])
                S.op('dve', lambda: nc.vector.scalar_tensor_tensor(out=ST[:], in0=ST[:], scalar=DEC[:, 0:1], in1=OP[:],
                                                                   op0=ALU.mult, op1=ALU.add), [kst, 's_dec', 's_op'], [kst])
                for g in range(2):
                    S.dma('sp', so_v[g, half * 8:(half + 1) * 8], ST[g * 64:(g + 1) * 64, :], reads=[kst])
                S.op('dve', lambda: tt(out=o3, in0=s3, in1=ch[:].unsqueeze(1).to_broadcast([128, 64, 64]), op=ALU.mult),
                     [kst, kc_], ['s_op'])
                S.dma('sp', ST[:], self.st_hg[:, half * 4096:(half + 1) * 4096], writes=[kst])
                S.op('dve', lambda: nc.vector.tensor_reduce(out=Yh[:], in_=o3, axis=AX.X, op=ALU.add), ['s_op'], ['s_yh'])
                S.op('dve', lambda: nc.vector.scalar_tensor_tensor(out=Yh[:], in0=xh[:], scalar=V3[:, 2:3], in1=Yh[:],
                                                                   op0=ALU.mult, op1=ALU.add), [kx, 's_v3', 's_yh'], ['s_yh'])
                for g in range(2):
                    S.dma('sp', y_v[g, half * 8:(half + 1) * 8], Yh[g * 64:(g + 1) * 64, :], reads=['s_yh'], writes=[('sc', 'y')])
            S.op('dve', lambda: tt(out=L0[:], in0=L0[:], in1=L1[:], op=ALU.subtract), ['g_l0', 'g_l1'], ['g_l0'])
            S.op('act', lambda: nc.scalar.activation(out=L0[:], in_=L0[:], func=AF.Sigmoid), ['g_l0'], ['g_l0'])
            S.op('dve', lambda: nc.vector.tensor_scalar(out=L1[:], in0=L0[:], scalar1=-1.0, scalar2=1.0, op0=ALU.mult, op1=ALU.add),
                 ['g_l0', 'g_l1'], ['g_l1'])
            S.op('act', lambda: nc.scalar.activation(out=Q[:], in_=Q[:], func=AF.Silu), ['g_q'], ['g_q'])
            S.op('act', lambda: nc.scalar.activation(out=G[:], in_=G[:], func=AF.Silu), ['g_g'], ['g_g'])
            S.op('act', lambda: nc.scalar.activation(out=Fg[:], in_=Fg[:], func=AF.Sigmoid), ['g_f'], ['g_f'])
            S.op('dve', lambda: nc.vector.tensor_scalar(out=KK[:], in0=Fg[:], scalar1=-1.0, scalar2=1.0, op0=ALU.mult, op1=ALU.add),
                 ['g_f'], ['g_kk'])
            S.op('dve', lambda: tt(out=KK[:], in0=KK[:], in1=L1[:], op=ALU.mult), ['g_kk', 'g_l1'], ['g_kk'])
            S.op('dve', lambda: tt(out=Fg[:], in0=Fg[:], in1=L1[:], op=ALU.mult), ['g_f', 'g_l1'], ['g_f'])
            S.op('dve', lambda: tt(out=Fg[:], in0=Fg[:], in1=L0[:], op=ALU.add), ['g_f', 'g_l0'], ['g_f'])
            for part in range(4):
                ks = slice(part * 32, (part + 1) * 32)
                cols = slice(part * 4096, (part + 1) * 4096)
                SP = STB[part % 2]
                ksp = ('stb', part % 2)
                sp3 = SP[:].rearrange("q (k v) -> q k v", v=128)
                op3 = OP[:].rearrange("q (k v) -> q k v", v=128)
                S.op('dve', lambda: tt(out=sp3, in0=sp3, in1=Fg[:, ks].unsqueeze(2).to_broadcast([128, 32, 128]), op=ALU.mult),
                     [ksp, 'g_f'], [ksp])
                S.op('pool', lambda: nc.gpsimd.tensor_tensor(out=op3, in0=KK[:, ks].unsqueeze(2).to_broadcast([128, 32, 128]),
                                                             in1=Iv[:].unsqueeze(1).to_broadcast([128, 32, 128]), op=ALU.mult),
                     ['g_kk', 'g_i'], ['s_op'])
                S.op('dve', lambda: tt(out=SP[:], in0=SP[:], in1=OP[:], op=ALU.add), [ksp, 's_op'], [ksp])
                S.dma('sp', self.s_hg[:, cols], SP[:], reads=[ksp])
                S.op('dve', lambda: tt(out=op3, in0=sp3, in1=Q[:, ks].unsqueeze(2).to_broadcast([128, 32, 128]), op=ALU.mult),
                     [ksp, 'g_q'], ['s_op'])
                if part + 2 < 4:
                    S.dma('sp', SP[:], self.st_hg[:, (part + 2) * 4096:(part + 3) * 4096], writes=[ksp])
                dst = O if part == 0 else O2
                S.op('dve', lambda: nc.vector.tensor_reduce(out=dst[:], in_=OP[:].rearrange("q (k v) -> q v k", v=128),
                                                            axis=AX.X, op=ALU.add), ['s_op'], ['g_o' if part == 0 else 'g_o2'])
                if part:
                    S.op('dve', lambda: tt(out=O[:], in0=O[:], in1=O2[:], op=ALU.add), ['g_o', 'g_o2'], ['g_o'])
            S.op('act', lambda: nc.scalar.activation(out=O2[:], in_=O[:], func=AF.Square, accum_out=SS[:, 0:1]), ['g_o'], ['g_o2', 'g_ss'])
            S.op('act', lambda: nc.scalar.activation(out=SS[:], in_=SS[:], func=AF.Sqrt, bias=self.epsc[:, 0:1], scale=1.0 / 128.0),
                 ['g_ss', 'epsc'], ['g_ss'])
            S.op('dve', lambda: nc.vector.reciprocal(out=SS[:], in_=SS[:]), ['g_ss'], ['g_ss'])
            S.op('dve', lambda: nc.vector.scalar_tensor_tensor(out=O[:], in0=O[:], scalar=SS[:, 0:1], in1=NWr[:],
                                                               op0=ALU.mult, op1=ALU.mult), ['g_o', 'g_ss', 'g_nw'], ['g_o'])
            S.op('dve', lambda: tt(out=O[:], in0=O[:], in1=G[:], op=ALU.mult), ['g_o', 'g_g'], ['g_o'])
            S.dma('sp', SC['o'].rearrange("b (h k) -> (b h) k", k=128), O[:], reads=['g_o'], writes=[('sc', 'o')])
            S.barrier()
        with ExitStack() as es:
            MS = self.sb(es, "m_ms", [NS, 2 * D])
            Z = self.sb(es, "m_z", [NS, D])
            NWt = self.sb(es, "m_nw", [NS, D])
            SQt = self.sb(es, "m_sq", [NS, D])
            SS = self.sb(es, "m_ss", [NS, 1])
            S.dma('sp', MS[:, 0:D], SC['y'][:, :], reads=[('sc', 'y')], writes=['m_ms'])
            S.dma('sp', MS[:, D:2 * D], SC['o'][:, :], reads=[('sc', 'o')], writes=['m_ms'])
            S.dma('sp', Z[:], SC['z'][:, :], reads=[('sc', 'z')], writes=['m_z'])
            S.dma('sp', NWt[:], self.snw_row[0:1, :].partition_broadcast(NS), writes=['m_nw'])
            S.op('act', lambda: nc.scalar.activation(out=Z[:], in_=Z[:], func=AF.Silu), ['m_z'], ['m_z'])
            S.op('dve', lambda: tt(out=MS[:, 0:D], in0=MS[:, 0:D], in1=Z[:], op=ALU.mult), ['m_ms', 'm_z'], ['m_ms'])
            S.op('act', lambda: nc.scalar.activation(out=SQt[:], in_=MS[:, 0:D], func=AF.Square, accum_out=SS[:, 0:1]),
                 ['m_ms'], ['m_sq', 'm_ss'])
            S.op('act', lambda: nc.scalar.activation(out=SS[:], in_=SS[:], func=AF.Sqrt, bias=self.epsc[0:NS, 0:1], scale=1.0 / D),
                 ['m_ss', 'epsc'], ['m_ss'])
            S.op('dve', lambda: nc.vector.reciprocal(out=SS[:], in_=SS[:]), ['m_ss'], ['m_ss'])
            S.op('dve', lambda: nc.vector.scalar_tensor_tensor(out=MS[:, 0:D], in0=MS[:, 0:D], scalar=SS[:, 0:1], in1=NWt[:],
                                                               op0=ALU.mult, op1=ALU.mult), ['m_ms', 'm_ss', 'm_nw'], ['m_ms'])
            for half in range(2):
                b = self.bank()
                for j in range(8):
                    c = half * 8 + j
                    self.transpose(self.PB[b][:, j * NS:(j + 1) * NS], [('pb', b)], MS[:, c * 128:(c + 1) * 128],
                                   self.ident[0:NS, 0:NS], ['m_ms', 'ident'])
                self.copy('dve', MIX[:, half * 8:(half + 1) * 8, SEG:SEG + NS],
                          self.PB[b][:, 0:8 * NS].rearrange("p (c t) -> p c t", t=NS), [('pb', b)],
                          tkeys('mix', range(half * 8, half * 8 + 8), 2))
            S.barrier()

    def mixer_ab(self, seg):
        nc, S = self.nc, self.S
        ncol = seg_cols(seg)
        with ExitStack() as es0:
            MIX = self.sb(es0, "mix", [128, 16, ncol], BF16)
            HN = self.sb(es0, "hn", [128, 8, ncol], BF16)
            with ExitStack() as es2:
                self.rmsnorm_fm(es2, seg, self.XR, 'xr', self.NW[:, 0, 0, :], HN, 'hn')
                S.barrier()
            self.ssd_phase(seg, HN, MIX)
            self.hgrn_phase(seg, HN, MIX)
            self.zgate_phase(seg, HN, MIX)
            if seg == 0:
                self.ab_sample_phase(MIX)
            with ExitStack() as es:
                Mo = self.sb(es, "mo", [128, 8, ncol])
                ws_o = wstream(self, [(self.ab_out_w[:, ob * 128:(ob + 1) * 128], 16) for ob in range(8)])
                for ob in range(8):
                    wv, wk = next(ws_o)
                    for ti, (c0, n) in enumerate(ttiles(seg)):
                        b = self.bank()
                        self.mm(self.PB[b][:, 0:n], [('pb', b)], [(wv[:, kc, :], MIX[:, kc, c0:c0 + n]) for kc in range(16)],
                                [wk] + tkeys('mix', range(16), ti))
                        self.copy(self.evac_eng(), Mo[:, ob, c0:c0 + n], self.PB[b][:, 0:n], [('pb', b)], [('mo', ob, ti)])
                with ExitStack() as es2:
                    self.rmsnorm_fm(es2, seg, Mo, 'mo', self.NW[:, 1, 0, :], self.XR, 'xr', residual=True)
                    S.barrier()
                S.barrier()
            S.barrier()


L8 = 8
NJ = SEG // L8
TWO_PI = 2.0 * np.pi


class ProgC(ProgAB3):
    def s5_trig(self, arg, arg_keys, sn, cs, np_, U, KI):
        nc, S = self.nc, self.S
        for t, key, shift in ((sn, 'trg_s', 0.0), (cs, 'trg_c', 0.5 * np.pi)):
            S.op('dve', lambda: nc.vector.tensor_scalar(out=U, in0=arg, scalar1=float(shift), scalar2=float(1.0 / TWO_PI),
                                                        op0=ALU.add, op1=ALU.mult), arg_keys, ['trg_u'])
            S.op('dve', lambda: nc.vector.tensor_copy(out=KI, in_=U), ['trg_u'], ['trg_k'])
            S.op('dve', lambda: nc.vector.tensor_copy(out=U, in_=KI), ['trg_k'], ['trg_u'])
            S.op('dve', lambda: nc.vector.scalar_tensor_tensor(out=t, in0=U, scalar=float(-TWO_PI), in1=arg,
                                                               op0=ALU.mult, op1=ALU.add), ['trg_u'] + arg_keys, [key])
            S.op('dve', lambda: nc.vector.tensor_scalar(out=t, in0=t, scalar1=float(shift), scalar2=float(np.pi),
                                                        op0=ALU.add, op1=ALU.min), [key], [key])
            S.op('dve', lambda: nc.vector.tensor_scalar(out=t, in0=t, scalar1=float(-np.pi), scalar2=None, op0=ALU.max), [key], [key])
            S.op('act', lambda: nc.scalar.activation(out=t, in_=t, func=AF.Sin), [key], [key])
        return 'trg_s', 'trg_c'

    def setup_s5(self):
        nc, S, es = self.nc, self.S, self.es
        tt = nc.vector.tensor_tensor
        def scr(name, shape, dt):
            return nc.dram_tensor(name, list(shape), dt, kind="Internal").ap()
        self.SC_WA = scr("sc_wa", [128, 8 * L8 * 2 * 128], BF16)
        self.SC_WV = scr("sc_wv", [128, 8 * L8 * 2 * 128], BF16)
        self.SC_CW = scr("sc_cw", [128, 32 * L8 * 2 * 32], BF16)
        self.SC_CT = scr("sc_ct", [128, 32 * NJ], F32)
        self.SC_ST = scr("sc_st", [128, 32 * NJ], F32)
        self.SC_RH = scr("sc_rh", [128, 32 * NJ], F32)
        self.SC['bur'] = scr("sc_bur", [NS, 4096], F32)
        self.SC['bui'] = scr("sc_bui", [NS, 4096], F32)
        lamA = self.inp("s5_lamA", [2, 128, 512])
        lsA = self.inp("s5_lsA", [128, 8])
        bA = self.inp("s5_bA", [2, 128, 512])
        lamC = self.inp("s5_lamC", [2, 128, 32])
        lsC = self.inp("s5_lsC", [128, 32])
        cC = self.inp("s5_cC", [2, 128, 512])
        self.s5_lam_row = self.inp("s5_lam_row", [2, 4096])
        self.s5_ls_row = self.inp("s5_ls_row", [1, 4096])
        self.glu_w = self.inp("s5_glu_w", [D, 2 * D])
        dsk = self.inp("s5_d_fm", [128, 8])
        cm = self.inp("c_s5masks", [128, 8])
        iota = self.inp("c_iota", [128, NJ])
        rst8 = self.inp("c_rst8", [128, 512])
        rstj = self.inp("c_rstj", [128, NJ])
        self.st_s5 = [self.inp("st_s5_re", [NS, 4096]), self.inp("st_s5_im", [NS, 4096])]
        self.s_s5 = [self.outp("s_s5_re", [NS, 4096]), self.outp("s_s5_im", [NS, 4096])]
        self.p_s5 = [self.outp("p_s5_re", [64, 64]), self.outp("p_s5_im", [64, 64])]
        self.DSK = self.sb(es, "DSK", [128, 8])
        self.S5M = self.sb(es, "S5M", [128, 8])
        self.RST8 = self.sb(es, "RST8", [128, 512])
        self.NPI = self.sb(es, "NPI", [128, 1])
        self.XPV = [self.sb(es, "XPV%d" % i, [128, 32]) for i in range(2)]
        self.AINV = [self.sb(es, "AINV%d" % i, [128, 32]) for i in range(2)]
        self.RHO8 = self.sb(es, "RHO8", [128, 32])
        S.dma('sp', self.DSK[:], dsk[:, :], writes=['DSK'])
        S.dma('sp', self.S5M[:], cm[:, :], writes=['S5M'])
        S.dma('sp', self.RST8[:], rst8[:, :], writes=['RST8'])
        S.op('dve', lambda: nc.vector.memset(self.NPI[:], -float(np.pi)), [], ['NPI'])
        for i in range(2):
            S.op('dve', lambda: nc.vector.memset(self.XPV[i][:], 0.0), [], [('XPV', i)])
        with ExitStack() as es1, nc.allow_non_contiguous_dma(reason="one-time S5 parameter re-layout"):
            A3 = [128, 8, 64]
            LRa = self.sb(es1, "a_lr", A3)
            LIa = self.sb(es1, "a_li", A3)
            LSa = self.sb(es1, "a_ls", [128, 8])
            BTr = self.sb(es1, "a_btr", A3)
            BTi = self.sb(es1, "a_bti", A3)
            for t_, r in ((LRa, 0), (LIa, 1)):
                S.dma('sp', t_[:].rearrange("p f n -> p (f n)"), lamA[r], writes=['a_l'])
            S.dma('sp', LSa[:], lsA[:, :], writes=['a_l'])
            for t_, r in ((BTr, 0), (BTi, 1)):
                S.dma('sp', t_[:].rearrange("p f n -> p (f n)"), bA[r], writes=['a_bt'])
            S.op('act', lambda: nc.scalar.activation(out=LSa[:], in_=LSa[:], func=AF.Exp), ['a_l'], ['a_l'])
            stp = LSa[:].unsqueeze(2).to_broadcast(A3)
            LAMr = self.sb(es1, "a_lamr", A3)
            LAMi = self.sb(es1, "a_lami", A3)
            self.copy('dve', LAMr[:], LRa[:], ['a_l'], ['a_lam'])
            self.copy('dve', LAMi[:], LIa[:], ['a_l'], ['a_lam'])
            S.op('dve', lambda: tt(out=LRa[:], in0=LRa[:], in1=stp, op=ALU.mult), ['a_l', 'a_lam'], ['a_l'])
            S.op('dve', lambda: tt(out=LIa[:], in0=LIa[:], in1=stp, op=ALU.mult), ['a_l'], [('arg', 'a1'), 'a_l'])
            SN = self.sb(es1, "a_sn", A3)
            CSn = self.sb(es1, "a_cs", A3)
            TU = self.sb(es1, "a_tu", A3)
            TK = self.sb(es1, "a_tk", A3, mybir.dt.int32)
            ks1, kc1 = self.s5_trig(LIa[:], ['a_l'], SN[:], CSn[:], 128, TU[:], TK[:])
            s1, c1 = SN, CSn
            MAG = self.sb(es1, "a_mag", A3)
            S.op('act', lambda: nc.scalar.activation(out=MAG[:], in_=LRa[:], func=AF.Exp), ['a_l'], ['a_mag'])
            ABr = self.sb(es1, "a_abr", A3)
            ABi = self.sb(es1, "a_abi", A3)
            S.op('dve', lambda: tt(out=ABr[:], in0=MAG[:], in1=c1[:], op=ALU.mult), ['a_mag', kc1], ['a_ab'])
            S.op('dve', lambda: tt(out=ABi[:], in0=MAG[:], in1=s1[:], op=ALU.mult), ['a_mag', ks1], ['a_ab'])
            DEN = self.sb(es1, "a_den", A3)
            T1 = self.sb(es1, "a_t1", A3)
            T2 = self.sb(es1, "a_t2", A3)
            CFr = self.sb(es1, "a_cfr", A3)
            CFi = self.sb(es1, "a_cfi", A3)
            S.op('dve', lambda: tt(out=DEN[:], in0=LAMr[:], in1=LAMr[:], op=ALU.mult), ['a_lam'], ['a_den'])
            S.op('dve', lambda: tt(out=T1[:], in0=LAMi[:], in1=LAMi[:], op=ALU.mult), ['a_lam'], ['a_t1'])
            S.op('dve', lambda: tt(out=DEN[:], in0=DEN[:], in1=T1[:], op=ALU.add), ['a_den', 'a_t1'], ['a_den'])
            S.op('dve', lambda: nc.vector.reciprocal(out=DEN[:], in_=DEN[:]), ['a_den'], ['a_den'])
            S.op('dve', lambda: nc.vector.tensor_scalar(out=T2[:], in0=ABr[:], scalar1=-1.0, scalar2=None, op0=ALU.add), ['a_ab'], ['a_t2'])
            S.op('dve', lambda: tt(out=CFr[:], in0=T2[:], in1=LAMr[:], op=ALU.mult), ['a_t2', 'a_lam'], ['a_cfr'])
            S.op('dve', lambda: tt(out=T1[:], in0=ABi[:], in1=LAMi[:], op=ALU.mult), ['a_ab', 'a_lam', 'a_den'], ['a_t1'])
            S.op('dve', lambda: tt(out=CFr[:], in0=CFr[:], in1=T1[:], op=ALU.add), ['a_cfr', 'a_t1'], ['a_cfr'])
            S.op('dve', lambda: tt(out=CFr[:], in0=CFr[:], in1=DEN[:], op=ALU.mult), ['a_cfr', 'a_den'], ['a_cfr'])
            S.op('dve', lambda: tt(out=CFi[:], in0=ABi[:], in1=LAMr[:], op=ALU.mult), ['a_ab', 'a_lam'], ['a_cfi'])
            S.op('dve', lambda: tt(out=T1[:], in0=T2[:], in1=LAMi[:], op=ALU.mult), ['a_t2', 'a_lam', 'a_cfr'], ['a_t1'])
            S.op('dve', lambda: tt(out=CFi[:], in0=CFi[:], in1=T1[:], op=ALU.subtract), ['a_cfi', 'a_t1'], ['a_cfi'])
            S.op('dve', lambda: tt(out=CFi[:], in0=CFi[:], in1=DEN[:], op=ALU.mult), ['a_cfi', 'a_den'], ['a_cfi'])
            BBr = self.sb(es1, "a_bbr", A3)
            BBi = self.sb(es1, "a_bbi", A3)
            def cmul(outr, outi, ar, ai, br, bi, keys_in, key_out, neg_ai=False):
                S.op('dve', lambda: tt(out=outr, in0=ar, in1=br, op=ALU.mult), keys_in + ['a_t1', 'a_t2'], [key_out + 'r'])
                S.op('dve', lambda: tt(out=T1[:], in0=ai, in1=bi, op=ALU.mult), keys_in, ['a_t1'])
                S.op('dve', lambda: tt(out=outr, in0=outr, in1=T1[:], op=ALU.subtract), [key_out + 'r', 'a_t1'], [key_out + 'r'])
                S.op('dve', lambda: tt(out=outi, in0=ar, in1=bi, op=ALU.mult), keys_in, [key_out + 'i'])
                S.op('dve', lambda: tt(out=T2[:], in0=ai, in1=br, op=ALU.mult), keys_in, ['a_t2'])
                S.op('dve', lambda: tt(out=outi, in0=outi, in1=T2[:], op=ALU.add), [key_out + 'i', 'a_t2'], [key_out + 'i'])
            cmul(BBr[:], BBi[:], CFr[:], CFi[:], BTr[:], BTi[:], ['a_cfr', 'a_cfi', 'a_bt'], 'a_bb')
            WT = self.sb(es1, "a_wt", [128, 8, L8, 2, 128], BF16)
            ARG = self.sb(es1, "a_arg", A3)
            PM = self.sb(es1, "a_pm", A3)
            Pr = self.sb(es1, "a_pr", A3)
            Pi = self.sb(es1, "a_pi", A3)
            Wr = self.sb(es1, "a_wr", A3)
            Wi = self.sb(es1, "a_wi", A3)
            Qr = self.sb(es1, "a_qr", A3)
            Qi = self.sb(es1, "a_qi", A3)
            S.op('act', lambda: nc.scalar.activation(out=PM[:], in_=LRa[:], func=AF.Exp, scale=-1.0), ['a_l'], ['a_pm'])
            S.op('dve', lambda: tt(out=Qr[:], in0=PM[:], in1=c1[:], op=ALU.mult), ['a_pm', kc1, 'a_ab'], ['a_q'])
            S.op('dve', lambda: tt(out=Qi[:], in0=PM[:], in1=s1[:], op=ALU.mult), ['a_pm', ks1, 'a_ab'], ['a_q'])
            S.op('dve', lambda: nc.vector.tensor_scalar(out=Qi[:], in0=Qi[:], scalar1=-1.0, scalar2=None, op0=ALU.mult), ['a_q'], ['a_q'])
            S.op('dve', lambda: nc.vector.tensor_scalar(out=ARG[:], in0=LIa[:], scalar1=float(L8), scalar2=None, op0=ALU.mult),
                 ['a_l'], ['a_arg'])
            ksn, kcs = self.s5_trig(ARG[:], ['a_arg', 'a_q'], SN[:], CSn[:], 128, TU[:], TK[:])
            S.op('act', lambda: nc.scalar.activation(out=PM[:], in_=LRa[:], func=AF.Exp, scale=float(L8)), ['a_l', 'a_q'], ['a_pm'])
            S.op('dve', lambda: tt(out=Pr[:], in0=PM[:], in1=CSn[:], op=ALU.mult), ['a_pm', kcs], ['a_p'])
            S.op('dve', lambda: tt(out=Pi[:], in0=PM[:], in1=SN[:], op=ALU.mult), ['a_pm', ksn], ['a_p'])
            Wr2 = self.sb(es1, "a_wr2", A3)
            Wi2 = self.sb(es1, "a_wi2", A3)
            WW = [(Wr, Wi, 'a_w'), (Wr2, Wi2, 'a_x')]
            for wset, dst in ((0, self.SC_WA), (1, self.SC_WV)):
                for l in range(L8):
                    wr_, wi_, kk = WW[l % 2]
                    if l == 0 and wset == 0:
                        self.copy('dve', wr_[:], BBr[:], ['a_bbr'], [kk + 'r'])
                        self.copy('dve', wi_[:], BBi[:], ['a_bbi'], [kk + 'i'])
                    elif l == 0:
                        cmul(wr_[:], wi_[:], Pr[:], Pi[:], BBr[:], BBi[:], ['a_p', 'a_bbr', 'a_bbi'], kk)
                    else:
                        pr_, pi_, pk = WW[(l - 1) % 2]
                        cmul(wr_[:], wi_[:], pr_[:], pi_[:], Qr[:], Qi[:], [pk + 'r', pk + 'i', 'a_q'], kk)
                    for ri, wsrc, kw in ((0, wr_, kk + 'r'), (1, wi_, kk + 'i')):
                        for g2 in range(2):
                            S.op('act', lambda: nc.scalar.activation(out=WT[:, :, l, ri, g2 * 64:(g2 + 1) * 64], in_=wsrc[:],
                                                                     func=AF.Identity, scale=self.S5M[:, g2:g2 + 1]),
                                 [kw, 'S5M'], ['a_wt'])
                S.dma('sp', dst[:, :], WT[:].rearrange("p f l r m -> p (f l r m)"), reads=['a_wt'], writes=[('scw', wset)])
            S.barrier()
        with ExitStack() as es1, nc.allow_non_contiguous_dma(reason="one-time S5 parameter re-layout"):
            C2 = [128, 32]
            LRc = self.sb(es1, "c_lr", C2)
            LIc = self.sb(es1, "c_li", C2)
            LSc = self.sb(es1, "c_ls", C2)
            CTr = self.sb(es1, "c_ctr", [128, 32, 16])
            CTi = self.sb(es1, "c_cti", [128, 32, 16])
            for t_, r in ((LRc, 0), (LIc, 1)):
                S.dma('sp', t_[:], lamC[r], writes=['c_l'])
            S.dma('sp', LSc[:], lsC[:, :], writes=['c_l'])
            for t_, r in ((CTr, 0), (CTi, 1)):
                S.dma('sp', t_[:].rearrange("p a c -> p (a c)"), cC[r], writes=['c_ct'])
            S.op('act', lambda: nc.scalar.activation(out=LSc[:], in_=LSc[:], func=AF.Exp), ['c_l'], ['c_l'])
            S.op('dve', lambda: tt(out=LRc[:], in0=LRc[:], in1=LSc[:], op=ALU.mult), ['c_l'], ['c_l'])
            S.op('dve', lambda: tt(out=LIc[:], in0=LIc[:], in1=LSc[:], op=ALU.mult), ['c_l'], ['c_l'])
            ARG = self.sb(es1, "c_arg", C2)
            PM = self.sb(es1, "c_pm", C2)
            Ar = self.sb(es1, "c_ar", C2)
            Ai = self.sb(es1, "c_ai", C2)
            SNc = self.sb(es1, "c_sn", C2)
            CSc = self.sb(es1, "c_cs", C2)
            TUc = self.sb(es1, "c_tu", C2)
            TKc = self.sb(es1, "c_tk", C2, mybir.dt.int32)
            def apow(m, tag):
                S.op('dve', lambda: nc.vector.tensor_scalar(out=ARG[:], in0=LIc[:], scalar1=float(m), scalar2=None, op0=ALU.mult),
                     ['c_l', 'c_a'], ['c_arg'])
                ksn, kcs = self.s5_trig(ARG[:], ['c_arg'], SNc[:], CSc[:], 128, TUc[:], TKc[:])
                sn, cs = SNc, CSc
                S.op('act', lambda: nc.scalar.activation(out=PM[:], in_=LRc[:], func=AF.Exp, scale=float(m)), ['c_l', 'c_a'], ['c_pm'])
                S.op('dve', lambda: tt(out=Ar[:], in0=PM[:], in1=cs[:], op=ALU.mult), ['c_pm', kcs], ['c_a'])
                S.op('dve', lambda: tt(out=Ai[:], in0=PM[:], in1=sn[:], op=ALU.mult), ['c_pm', ksn], ['c_a'])
            CW = self.sb(es1, "c_cw", [128, 32, L8, 2, 32], BF16)
            R1 = self.sb(es1, "c_r1", [128, 32, 16])
            R2 = self.sb(es1, "c_r2", [128, 32, 16])
            for l in range(L8):
                apow(l, 'c%d' % l)
                arb = Ar[:].unsqueeze(2).to_broadcast([128, 32, 16])
                aib = Ai[:].unsqueeze(2).to_broadcast([128, 32, 16])
                S.op('dve', lambda: tt(out=R1[:], in0=CTr[:], in1=arb, op=ALU.mult), ['c_ct', 'c_a'], ['c_r1'])
                S.op('dve', lambda: tt(out=R2[:], in0=CTi[:], in1=aib, op=ALU.mult), ['c_ct', 'c_a'], ['c_r2'])
                S.op('dve', lambda: tt(out=R1[:], in0=R1[:], in1=R2[:], op=ALU.subtract), ['c_r1', 'c_r2'], ['c_r1'])
                for g2 in range(2):
                    S.op('act', lambda: nc.scalar.activation(out=CW[:, :, l, 0, g2 * 16:(g2 + 1) * 16], in_=R1[:], func=AF.Identity,
                                                             scale=self.S5M[:, 6 + g2:7 + g2]), ['c_r1', 'S5M'], ['c_cw'])
                S.op('dve', lambda: tt(out=R1[:], in0=CTr[:], in1=aib, op=ALU.mult), ['c_ct', 'c_a', 'c_r1'], ['c_r1'])
                S.op('dve', lambda: tt(out=R2[:], in0=CTi[:], in1=arb, op=ALU.mult), ['c_ct', 'c_a', 'c_r2'], ['c_r2'])
                S.op('dve', lambda: tt(out=R1[:], in0=R1[:], in1=R2[:], op=ALU.add), ['c_r1', 'c_r2'], ['c_r1'])
                S.op('dve', lambda: nc.vector.tensor_scalar(out=R1[:], in0=R1[:], scalar1=-1.0, scalar2=None, op0=ALU.mult), ['c_r1'], ['c_r1'])
                for g2 in range(2):
                    S.op('act', lambda: nc.scalar.activation(out=CW[:, :, l, 1, g2 * 16:(g2 + 1) * 16], in_=R1[:], func=AF.Identity,
                                                             scale=self.S5M[:, 6 + g2:7 + g2]), ['c_r1', 'S5M'], ['c_cw'])
            S.dma('sp', self.SC_CW[:, :], CW[:].rearrange("p a l r c -> p (a l r c)"), reads=['c_cw'], writes=['sccw'])
            apow(-1, 'cm1')
            self.copy('dve', self.AINV[0][:], Ar[:], ['c_a'], ['AINV'])
            self.copy('dve', self.AINV[1][:], Ai[:], ['c_a'], ['AINV'])
            S.op('act', lambda: nc.scalar.activation(out=self.RHO8[:], in_=LRc[:], func=AF.Exp, scale=float(L8)), ['c_l'], ['RHO8'])
            IO = self.sb(es1, "c_io", [128, NJ])
            RJ = self.sb(es1, "c_rj", [128, NJ])
            S.dma('sp', IO[:], iota[:, :], writes=['c_io'])
            S.dma('sp', RJ[:], rstj[:, :], writes=['c_rj'])
            ANG = self.sb(es1, "c_ang", [128, 32, NJ])
            PH = self.sb(es1, "c_ph", C2)
            S.op('dve', lambda: nc.vector.tensor_scalar(out=TUc[:], in0=LIc[:], scalar1=float(L8 / TWO_PI), scalar2=None, op0=ALU.mult),
                 ['c_l', 'trg_u'], ['trg_u'])
            S.op('dve', lambda: nc.vector.tensor_copy(out=TKc[:], in_=TUc[:]), ['trg_u', 'trg_k'], ['trg_k'])
            S.op('dve', lambda: nc.vector.tensor_copy(out=TUc[:], in_=TKc[:]), ['trg_k'], ['trg_u'])
            S.op('dve', lambda: nc.vector.tensor_scalar(out=PH[:], in0=LIc[:], scalar1=float(L8), scalar2=None, op0=ALU.mult), ['c_l'], ['c_ph'])
            S.op('dve', lambda: nc.vector.scalar_tensor_tensor(out=PH[:], in0=TUc[:], scalar=float(-TWO_PI), in1=PH[:],
                                                               op0=ALU.mult, op1=ALU.add), ['trg_u', 'c_ph'], ['c_ph'])
            S.op('dve', lambda: tt(out=ANG[:], in0=IO[:].unsqueeze(1).to_broadcast([128, 32, NJ]),
                                   in1=PH[:].unsqueeze(2).to_broadcast([128, 32, NJ]), op=ALU.mult), ['c_io', 'c_ph'], ['c_ang'])
            a2 = ANG[:].rearrange("p a j -> p (a j)")
            TUj = self.sb(es1, "c_tuj", [128, 32 * NJ])
            TKj = self.sb(es1, "c_tkj", [128, 32 * NJ], mybir.dt.int32)
            SNj = self.sb(es1, "c_snj", [128, 32 * NJ])
            CSj = self.sb(es1, "c_csj", [128, 32 * NJ])
            ksn, kcs = self.s5_trig(a2, ['c_ang'], SNj[:], CSj[:], 128, TUj[:], TKj[:])
            S.dma('sp', self.SC_CT[:, :], CSj[:], reads=[kcs], writes=['scct'])
            S.dma('sp', self.SC_ST[:, :], SNj[:], reads=[ksn], writes=['scct'])
            S.op('dve', lambda: tt(out=ANG[:], in0=RJ[:].unsqueeze(1).to_broadcast([128, 32, NJ]),
                                   in1=self.RHO8[:].unsqueeze(2).to_broadcast([128, 32, NJ]), op=ALU.mult),
                 ['c_rj', 'RHO8', kcs, ksn], ['c_ang'])
            S.dma('sp', self.SC_RH[:, :], a2, reads=['c_ang'], writes=['scct'])
            S.barrier()


class ProgC2(ProgC):
    def mixer_c(self, seg):
        nc, S = self.nc, self.S
        tt = nc.vector.tensor_tensor
        ncol = seg_cols(seg)
        NG = 8
        with ExitStack() as es0:
            HN = self.sb(es0, "c_hn", [128, 8, ncol], BF16)
            GY = self.sb(es0, "c_gy", [128, 8, ncol], BF16)
            with ExitStack() as es2:
                self.rmsnorm_fm(es2, seg, self.XR, 'xr', self.NW[:, 0, 1, :], HN, 'hn')
                S.barrier()
            with ExitStack() as esT:
                TT = [self.sb(esT, "c_tt%d" % i, [128, 32, NJ]) for i in range(2)]
                self.s5_scan_phases(seg, HN, GY, TT)
                S.barrier()
            self.nrot = 7
            if seg == 0:
                self.s5_sample_phase(HN, GY)
            if seg == NSEG - 1:
                self.s5_final_state()
            self.s5_glu(seg, GY)
            S.barrier()

    def s5_scan_phases(self, seg, HN, GY, TT):
        nc, S = self.nc, self.S
        tt = nc.vector.tensor_tensor
        NG = 8
        if True:
            if self.dbg.get('s5_stop') == 1:
                S.barrier()
                return
            with ExitStack() as es:
                WPQ = [self.sb(es, "c_wpq%d" % i, [128, L8, 2, 128], BF16) for i in range(4)]
                for i in range(4):
                    S.op('pool', lambda: nc.gpsimd.memset(WPQ[i][:].rearrange("p l r m -> p (l r m)"), 0.0), [], [('wp', i)])
                G3 = [128, NG, NJ]
                CT = self.sb(es, "c_ct", G3)
                ST = self.sb(es, "c_st", G3)
                RH = self.sb(es, "c_rh", G3)
                U = [self.sb(es, "c_u%d" % i, G3) for i in range(2)]
                Z = [self.sb(es, "c_z%d" % i, G3) for i in range(2)]
                T1 = self.sb(es, "c_t1", G3)
                T2 = self.sb(es, "c_t2", G3)
                SD = self.sb(es, "c_sd", [128, NG])
                def pass1(fc):
                        for p4 in range(4):
                            p = fc * 4 + p4
                            wp = WPQ[p4]
                            rows = slice(32 * p4, 32 * p4 + 32)
                            S.dma('sp', wp[rows, :, :, :].rearrange("p l r m -> p (l r m)"), self.SC_WV[rows, fc * 2048:(fc + 1) * 2048],
                                  reads=[('scw', 1)], writes=[('wp', p4)])
                            if self.dbg.get('s5_stop') == 21:
                                continue
                            b = self.bank()
                            for ri in range(2):
                                self.mm(self.PB[b][:, ri * NJ:(ri + 1) * NJ], [('pb', b)],
                                        [(wp[:, l, ri, :] if self.dbg.get('s5_stop') != 23 else self.identb[:], HN[:, fc, l:SEG:L8] if self.dbg.get('s5_stop') not in (22, 23) else HN[:, fc, l * 128:(l + 1) * 128]) for l in range(L8)],
                                        [('wp', p4)] + tkeys('hn', [fc], 0) + tkeys('hn', [fc], 1))
                            for ri in range(2):
                                if self.dbg.get('s5_stop') == 24:
                                    continue
                                if self.dbg.get('s5_stop') == 25:
                                    self.copy('dve', WF[0][:, 0, 0, :], self.PB[b][:, ri * NJ:(ri + 1) * NJ], [('pb', b)], [('wf', 0)])
                                    continue
                                self.copy('dve', TT[ri][:, p, :], self.PB[b][:, ri * NJ:(ri + 1) * NJ], [('pb', b)], [('tt', ri, p // NG)])
                def coarse(pg):
                        pp = slice(pg * NG, (pg + 1) * NG)
                        cols = slice(pg * NG * NJ, (pg + 1) * NG * NJ)
                        for t_, d_, k_ in ((CT, self.SC_CT, 'cct'), (ST, self.SC_ST, 'cst'), (RH, self.SC_RH, 'crh')):
                            S.dma('sp', t_[:].rearrange("p a j -> p (a j)"), d_[:, cols], reads=['scct'], writes=[k_])
                        Vr, Vi = TT[0][:, pp, :], TT[1][:, pp, :]
                        kv = [('tt', 0, pg), ('tt', 1, pg)]
                        S.op('dve', lambda: tt(out=U[0][:], in0=CT[:], in1=Vr, op=ALU.mult), ['cct'] + kv, ['cu0'])
                        S.op('pool', lambda: nc.gpsimd.tensor_tensor(out=T1[:], in0=ST[:], in1=Vi, op=ALU.mult), ['cst'] + kv, ['ct1'])
                        S.op('dve', lambda: tt(out=U[0][:], in0=U[0][:], in1=T1[:], op=ALU.add), ['cu0', 'ct1'], ['cu0'])
                        S.op('dve', lambda: tt(out=U[1][:], in0=CT[:], in1=Vi, op=ALU.mult), ['cct'] + kv, ['cu1'])
                        S.op('pool', lambda: nc.gpsimd.tensor_tensor(out=T2[:], in0=ST[:], in1=Vr, op=ALU.mult), ['cst'] + kv, ['ct2'])
                        S.op('dve', lambda: tt(out=U[1][:], in0=U[1][:], in1=T2[:], op=ALU.subtract), ['cu1', 'ct2'], ['cu1'])
                        for ri in range(2):
                            S.op('dve', lambda: tt(out=SD[:], in0=self.RHO8[:, pp], in1=self.XPV[ri][:, pp], op=ALU.mult),
                                 ['RHO8', ('XPV', ri)], ['csd'])
                            S.op('dve', lambda: tt(out=U[ri][:, :, 0], in0=U[ri][:, :, 0], in1=SD[:], op=ALU.add), ['cu%d' % ri, 'csd'], ['cu%d' % ri])
                        for ri in range(2):
                            S.op('dve', lambda: nc.vector.tensor_tensor_scan(
                                out=Z[ri][:].rearrange("p a j -> p (a j)"), data0=RH[:].rearrange("p a j -> p (a j)"),
                                data1=U[ri][:].rearrange("p a j -> p (a j)"), initial=0.0, op0=ALU.mult, op1=ALU.add),
                                ['crh', 'cu%d' % ri], ['cz%d' % ri])
                        S.op('dve', lambda: tt(out=U[0][:], in0=CT[:], in1=Z[0][:], op=ALU.mult), ['cct', 'cz0'], ['cu0'])
                        S.op('pool', lambda: nc.gpsimd.tensor_tensor(out=T1[:], in0=ST[:], in1=Z[1][:], op=ALU.mult), ['cst', 'cz1'], ['ct1'])
                        S.op('dve', lambda: tt(out=U[0][:], in0=U[0][:], in1=T1[:], op=ALU.subtract), ['cu0', 'ct1'], ['cu0'])
                        S.op('dve', lambda: tt(out=U[1][:], in0=CT[:], in1=Z[1][:], op=ALU.mult), ['cct', 'cz1'], ['cu1'])
                        S.op('pool', lambda: nc.gpsimd.tensor_tensor(out=T2[:], in0=ST[:], in1=Z[0][:], op=ALU.mult), ['cst', 'cz0'], ['ct2'])
                        S.op('dve', lambda: tt(out=U[1][:], in0=U[1][:], in1=T2[:], op=ALU.add), ['cu1', 'ct2'], ['cu1'])
                        for ri in range(2):
                            self.copy('pool', TT[ri][:, pp, 1:NJ], U[ri][:, :, 0:NJ - 1], ['cu%d' % ri], [('tt', ri, pg)])
                            self.copy('dve', TT[ri][:, pp, 0], self.XPV[ri][:, pp], [('XPV', ri)], [('tt', ri, pg)])
                            self.copy('dve', self.XPV[ri][:, pp], U[ri][:, :, NJ - 1], ['cu%d' % ri], [('XPV', ri)])
                npg = 32 // NG
                fpg = 8 // npg
                for fc in range(fpg):
                    pass1(fc)
                for pg in range(npg):
                    if pg + 1 < npg:
                        for fc in range((pg + 1) * fpg, (pg + 2) * fpg):
                            pass1(fc)
                    coarse(pg)
                S.barrier()
            with ExitStack() as es:
                WPQ = [self.sb(es, "c_wpq%d" % i, [128, L8, 2, 128], BF16) for i in range(4)]
                for i in range(4):
                    S.op('pool', lambda: nc.gpsimd.memset(WPQ[i][:].rearrange("p l r m -> p (l r m)"), 0.0), [], [('wp', i)])
                CWF = [self.sb(es, "c_cwf%d" % i, [128, 4, L8, 2, 32], BF16) for i in range(2)]
                ZB = [[self.sb(es, "c_zb%d_%d" % (i, ri), [128, SEG], BF16) for ri in range(2)] for i in range(2)]
                YTS = self.sb(es, "c_yts", [128, L8, 128])
                TY = self.sb(es, "c_ty", [128, SEG])
                for fc in range(8):
                    cwf = CWF[fc % 2]
                    S.dma('sp', cwf[:].rearrange("p a l r c -> p (a l r c)"), self.SC_CW[:, fc * 2048:(fc + 1) * 2048],
                          reads=['sccw'], writes=[('cwf', fc % 2)])
                    self.nrot = 6
                    by = [6, 7]
                    for p4 in range(4):
                        p = fc * 4 + p4
                        wp = WPQ[p4]
                        zb = ZB[p % 2]
                        rows = slice(32 * p4, 32 * p4 + 32)
                        S.dma('sp', wp[rows, :, :, :].rearrange("p l r m -> p (l r m)"), self.SC_WA[rows, fc * 2048:(fc + 1) * 2048],
                              reads=[('scw', 0)], writes=[('wp', p4)])
                        for ri in range(2):
                            for half in range(2):
                                b = self.bank()
                                S.deps('pe', [('wp', p4)] + tkeys('hn', [fc], half), [('pb', b)])
                                ins = None
                                for l in range(L8):
                                    ins = nc.tensor.matmul(self.PB[b][:, l:512:L8], lhsT=wp[:, l, ri, :],
                                                           rhs=HN[:, fc, half * 512 + l:(half + 1) * 512:L8], start=True, stop=True)
                                    S.n_inst += 1
                                S.group_end('pe', ins, [('wp', p4)] + tkeys('hn', [fc], half), [('pb', b)])
                                S.op('dve', lambda: tt(out=self.PB[b][:, 0:512:L8], in0=self.PB[b][:, 0:512:L8],
                                                       in1=TT[ri][:, p, half * 64:(half + 1) * 64], op=ALU.add),
                                     [('pb', b), ('tt', ri, p // NG)], [('pb', b)])
                                S.op('dve', lambda: nc.vector.tensor_tensor_scan(
                                    out=zb[ri][:, half * 512:(half + 1) * 512], data0=self.RST8[:], data1=self.PB[b][:, :],
                                    initial=0.0, op0=ALU.mult, op1=ALU.add), [('pb', b), 'RST8'], [('zb', p % 2, ri)])
                        if seg == 0:
                            b = self.bank()
                            for ri in range(2):
                                self.mm(self.PB[b][0:NS, ri * 128:(ri + 1) * 128], [('pb', b)],
                                        [(HN[:, fc, SEG:SEG + NS], wp[:, 0, ri, :])], [('wp', p4)] + tkeys('hn', [fc], 2))
                            for ri, nm in ((0, 'bur'), (1, 'bui')):
                                i = self.stg_rr
                                self.stg_rr = (self.stg_rr + 1) % len(self.stg)
                                self.copy('dve', self.stg[i][:, :], self.PB[b][0:NS, ri * 128:(ri + 1) * 128], [('pb', b)], [('stg', i)])
                                S.dma('sp', self.SC[nm][:, p * 128:(p + 1) * 128], self.stg[i][:, :], reads=[('stg', i)], writes=[('sc', nm)])
                        if self.dbg.get('s5_stop') == 4:
                            continue
                        for l in range(L8):
                            bb = by[l // 4]
                            c0 = (l % 4) * 128 + p4 * 32
                            self.mm(self.PB[bb][:, c0:c0 + 32], [('pb', bb)],
                                    [(zb[0][:, l:SEG:L8], cwf[:, p4, l, 0, :]), (zb[1][:, l:SEG:L8], cwf[:, p4, l, 1, :])],
                                    [('zb', p % 2, 0), ('zb', p % 2, 1), ('cwf', fc % 2)])
                    if self.dbg.get('s5_stop') == 4:
                        continue
                    for hh in range(2):
                        self.copy(self.evac_eng(), YTS[:, hh * 4:(hh + 1) * 4, :], self.PB[by[hh]][:, :].rearrange("p (l f) -> p l f", f=128),
                                  [('pb', by[hh])], ['yts'])
                    for hh in range(2):
                        b = self.bank()
                        for li in range(4):
                            self.transpose(self.PB[b][:, li * 128:(li + 1) * 128], [('pb', b)], YTS[:, hh * 4 + li, :], self.ident[:],
                                           ['yts', 'ident'])
                        S.op('dve', lambda: nc.vector.scalar_tensor_tensor(
                            out=TY[:].rearrange("p (j l) -> p j l", l=L8)[:, :, hh * 4:(hh + 1) * 4],
                            in0=HN[:, fc, 0:SEG].rearrange("p (j l) -> p j l", l=L8)[:, :, hh * 4:(hh + 1) * 4],
                            scalar=self.DSK[:, fc:fc + 1],
                            in1=self.PB[b][:, :].rearrange("p (l j) -> p j l", l=4), op0=ALU.mult, op1=ALU.add),
                            [('pb', b), 'DSK'] + tkeys('hn', [fc], 0) + tkeys('hn', [fc], 1), ['ty'])
                    if seg == 0 and fc == 0:
                        self.dump("ty", TY[:], ['ty'])
                    S.op('act', lambda: nc.scalar.activation(out=GY[:, fc, 0:SEG], in_=TY[:], func=AF.Gelu_apprx_tanh), ['ty'],
                         [('gy', fc, 0), ('gy', fc, 1)])
                S.barrier()

    def s5_glu(self, seg, GY):
        nc, S = self.nc, self.S
        tt = nc.vector.tensor_tensor
        ncol = seg_cols(seg)
        if True:
            with ExitStack() as es:
                Mo = self.sb(es, "c_mo", [128, 8, ncol])
                SGt = [self.sb(es, "c_sg%d" % i, [128, 512]) for i in range(2)]
                ws_g = wstream(self, [(self.glu_w[:, vg * D + ob * 128:vg * D + (ob + 1) * 128], 8)
                                      for ob in range(8) for vg in range(2)], depth=2)
                for ob in range(8):
                    wv, wkv = next(ws_g)
                    wg, wkg = next(ws_g)
                    for ti, (c0, n) in enumerate(ttiles(seg)):
                        bv, bg = self.bank(), self.bank()
                        self.mm(self.PB[bv][:, 0:n], [('pb', bv)], [(wv[:, kc, :], GY[:, kc, c0:c0 + n]) for kc in range(8)],
                                [wkv] + tkeys('gy', range(8), ti))
                        self.mm(self.PB[bg][:, 0:n], [('pb', bg)], [(wg[:, kc, :], GY[:, kc, c0:c0 + n]) for kc in range(8)],
                                [wkg] + tkeys('gy', range(8), ti))
                        sg = SGt[ti % 2]
                        S.op('act', lambda: nc.scalar.activation(out=sg[:, 0:n], in_=self.PB[bg][:, 0:n], func=AF.Sigmoid),
                             [('pb', bg)], [('csg', ti % 2)])
                        S.op('dve', lambda: tt(out=Mo[:, ob, c0:c0 + n], in0=self.PB[bv][:, 0:n], in1=sg[:, 0:n], op=ALU.mult),
                             [('pb', bv), ('csg', ti % 2)], [('mo', ob, ti)])
                with ExitStack() as es2:
                    self.rmsnorm_fm(es2, seg, Mo, 'mo', self.NW[:, 1, 1, :], self.XR, 'xr', residual=True)
                    S.barrier()
                S.barrier()
            S.barrier()

    def s5_final_state(self):
        nc, S = self.nc, self.S
        tt = nc.vector.tensor_tensor
        with ExitStack() as es, nc.allow_non_contiguous_dma(reason="tiny state scatter"):
            R = [self.sb(es, "fs_r%d" % i, [128, 32]) for i in range(2)]
            T = self.sb(es, "fs_t", [128, 32])
            xr, xi = self.XPV
            ar, ai = self.AINV
            S.op('dve', lambda: tt(out=R[0][:], in0=xr[:], in1=ar[:], op=ALU.mult), [('XPV', 0), 'AINV'], ['fs0'])
            S.op('dve', lambda: tt(out=T[:], in0=xi[:], in1=ai[:], op=ALU.mult), [('XPV', 1), 'AINV'], ['fst'])
            S.op('dve', lambda: tt(out=R[0][:], in0=R[0][:], in1=T[:], op=ALU.subtract), ['fs0', 'fst'], ['fs0'])
            S.op('dve', lambda: tt(out=R[1][:], in0=xr[:], in1=ai[:], op=ALU.mult), [('XPV', 0), 'AINV'], ['fs1'])
            S.op('dve', lambda: tt(out=T[:], in0=xi[:], in1=ar[:], op=ALU.mult), [('XPV', 1), 'AINV', 'fs0'], ['fst'])
            S.op('dve', lambda: tt(out=R[1][:], in0=R[1][:], in1=T[:], op=ALU.add), ['fs1', 'fst'], ['fs1'])
            for ri in range(2):
                for g2 in range(2):
                    ps = slice(g2 * 64, (g2 + 1) * 64)
                    S.dma('sp', self.p_s5[ri].rearrange("(p g) n -> g n p", g=2)[g2], R[ri][ps, :], reads=['fs%d' % ri])
            S.barrier()

    def s5_sample_phase(self, HN, GY):
        nc, S = self.nc, self.S
        tt = nc.vector.tensor_tensor
        with ExitStack() as es:
            Q = [NS, 1024]
            LR = self.sb(es, "q_lr", Q)
            LI = self.sb(es, "q_li", Q)
            LS = self.sb(es, "q_ls", Q)
            SN = self.sb(es, "q_sn", Q)
            CS = self.sb(es, "q_cs", Q)
            TU = self.sb(es, "q_tu", Q)
            TK = self.sb(es, "q_tk", Q, mybir.dt.int32)
            MG = self.sb(es, "q_mg", Q)
            S0 = [self.sb(es, "q_s%d" % i, Q) for i in range(2)]
            BU = [self.sb(es, "q_bu%d" % i, Q) for i in range(2)]
            X = [self.sb(es, "q_x%d" % i, Q) for i in range(2)]
            T1 = self.sb(es, "q_t1", Q)
            XS = [self.sb(es, "q_xs%d" % i, [128, 32, NS], BF16) for i in range(2)]
            CW0 = self.sb(es, "q_cw0", [128, 32, 64], BF16)
            YT = self.sb(es, "q_yt", [NS, D])
            TYs = self.sb(es, "q_ty", [128, 8, NS])
            S.dma('sp', CW0[:], self.SC_CW.rearrange("p (a l x) -> p a l x", l=L8, x=64)[:, :, 0, :], reads=['sccw'], writes=['q_cw0'])
            for qd in range(4):
                cols = slice(qd * 1024, (qd + 1) * 1024)
                S.dma('sp', LR[:], self.s5_lam_row[0:1, cols].partition_broadcast(NS), writes=['q_lr'])
                S.dma('sp', LI[:], self.s5_lam_row[1:2, cols].partition_broadcast(NS), writes=['q_li'])
                S.dma('sp', LS[:], self.s5_ls_row[0:1, cols].partition_broadcast(NS), writes=['q_ls'])
                for ri in range(2):
                    S.dma('sp', S0[ri][:], self.st_s5[ri][:, cols], writes=[('q_s', ri)])
                    S.dma('sp', BU[ri][:], self.SC['bur' if ri == 0 else 'bui'][:, cols], reads=[('sc', 'bur' if ri == 0 else 'bui')],
                          writes=[('q_bu', ri)])
                S.op('act', lambda: nc.scalar.activation(out=LS[:], in_=LS[:], func=AF.Exp), ['q_ls'], ['q_ls'])
                S.op('dve', lambda: tt(out=LR[:], in0=LR[:], in1=LS[:], op=ALU.mult), ['q_lr', 'q_ls'], ['q_lr'])
                S.op('dve', lambda: tt(out=LI[:], in0=LI[:], in1=LS[:], op=ALU.mult), ['q_li', 'q_ls'], ['q_li'])
                ksn, kcs = self.s5_trig(LI[:], ['q_li'], SN[:], CS[:], NS, TU[:], TK[:])
                S.op('act', lambda: nc.scalar.activation(out=MG[:], in_=LR[:], func=AF.Exp), ['q_lr'], ['q_mg'])
                S.op('dve', lambda: tt(out=CS[:], in0=CS[:], in1=MG[:], op=ALU.mult), [kcs, 'q_mg'], [kcs])
                S.op('dve', lambda: tt(out=SN[:], in0=SN[:], in1=MG[:], op=ALU.mult), [ksn, 'q_mg'], [ksn])
                S.op('dve', lambda: tt(out=X[0][:], in0=CS[:], in1=S0[0][:], op=ALU.mult), [kcs, ('q_s', 0)], [('q_x', 0)])
                S.op('dve', lambda: tt(out=T1[:], in0=SN[:], in1=S0[1][:], op=ALU.mult), [ksn, ('q_s', 1)], ['q_t1'])
                S.op('dve', lambda: tt(out=X[0][:], in0=X[0][:], in1=T1[:], op=ALU.subtract), [('q_x', 0), 'q_t1'], [('q_x', 0)])
                S.op('dve', lambda: tt(out=X[0][:], in0=X[0][:], in1=BU[0][:], op=ALU.add), [('q_x', 0), ('q_bu', 0)], [('q_x', 0)])
                S.op('dve', lambda: tt(out=X[1][:], in0=CS[:], in1=S0[1][:], op=ALU.mult), [kcs, ('q_s', 1)], [('q_x', 1)])
                S.op('dve', lambda: tt(out=T1[:], in0=SN[:], in1=S0[0][:], op=ALU.mult), [ksn, ('q_s', 0), ('q_x', 0)], ['q_t1'])
                S.op('dve', lambda: tt(out=X[1][:], in0=X[1][:], in1=T1[:], op=ALU.add), [('q_x', 1), 'q_t1'], [('q_x', 1)])
                S.op('dve', lambda: tt(out=X[1][:], in0=X[1][:], in1=BU[1][:], op=ALU.add), [('q_x', 1), ('q_bu', 1)], [('q_x', 1)])
                for ri in range(2):
                    S.dma('sp', self.s_s5[ri][:, cols], X[ri][:], reads=[('q_x', ri)])
                    b = self.bank()
                    for j in range(8):
                        self.transpose(self.PB[b][:, j * NS:(j + 1) * NS], [('pb', b)], X[ri][:, j * 128:(j + 1) * 128],
                                       self.ident[0:NS, 0:NS], [('q_x', ri), 'ident'])
                    self.copy('dve', XS[ri][:, qd * 8:(qd + 1) * 8, :], self.PB[b][:, 0:8 * NS].rearrange("p (a t) -> p a t", t=NS),
                              [('pb', b)], [('q_xs', ri)])
            for hh in range(2):
                b = self.bank()
                for pj in range(16):
                    p = hh * 16 + pj
                    self.mm(self.PB[b][0:NS, pj * 32:(pj + 1) * 32], [('pb', b)],
                            [(XS[0][:, p, :], CW0[:, p, 0:32]), (XS[1][:, p, :], CW0[:, p, 32:64])],
                            [('q_xs', 0), ('q_xs', 1), 'q_cw0'])
                self.copy('dve', YT[:, hh * 512:(hh + 1) * 512], self.PB[b][0:NS, :], [('pb', b)], ['q_yt'])
            b = self.bank()
            for fc in range(8):
                self.transpose(self.PB[b][:, fc * NS:(fc + 1) * NS], [('pb', b)], YT[:, fc * 128:(fc + 1) * 128],
                               self.ident[0:NS, 0:NS], ['q_yt', 'ident'])
            for fc in range(8):
                S.op('dve', lambda: nc.vector.scalar_tensor_tensor(
                    out=TYs[:, fc, :], in0=HN[:, fc, SEG:SEG + NS], scalar=self.DSK[:, fc:fc + 1],
                    in1=self.PB[b][:, fc * NS:(fc + 1) * NS], op0=ALU.mult, op1=ALU.add),
                    [('pb', b), 'DSK', ('hn', fc, 2)], ['q_ty'])
            S.op('act', lambda: nc.scalar.activation(out=GY[:, :, SEG:SEG + NS], in_=TYs[:], func=AF.Gelu_apprx_tanh), ['q_ty'],
                 tkeys('gy', range(8), 2))
            S.barrier()


def kernel(**inputs):
    inp = {k: np.asarray(v) for k, v in inputs.items()}
    res = run_stage(inp, "full", cores=8)
    B = 8
    def cat(name, shape=None):
        a = [np.asarray(r[name], np.float32) for r in res]
        return a
    y_prompt = np.stack(cat("y_p"), 0)
    y_sample = np.concatenate(cat("y_s"), 0).reshape(B * NS, 1, D)
    p_ssm = np.stack(cat("p_ssm"), 0)[None]
    p_sconv = np.stack(cat("p_sconv"), 0)[None]
    p_hg = np.stack(cat("p_hg"), 0)[None]
    p_s5r = np.stack(cat("p_s5_re"), 0)[None]
    p_s5i = np.stack(cat("p_s5_im"), 0)[None]
    p_fconv = np.stack(cat("p_fconv"), 1)
    s_ssm = np.concatenate(cat("s_ssm"), 0).reshape(1, B * NS, 16, 64, 64)
    s_sconv = np.concatenate(cat("s_sconv"), 0)[None]
    s_hg = np.concatenate(cat("s_hg"), 0).reshape(1, B * NS, 8, 128, 128)
    s_s5r = np.concatenate(cat("s_s5_re"), 0).reshape(1, B * NS, 64, 64)
    s_s5i = np.concatenate(cat("s_s5_im"), 0).reshape(1, B * NS, 64, 64)
    s_fconv = np.concatenate(cat("s_fconv"), 1)
    return (y_prompt, y_sample, p_ssm, p_sconv, p_hg, p_s5r, p_s5i, p_fconv,
            s_ssm, s_sconv, s_hg, s_s5r, s_s5i, s_fconv)
```

```python
import numpy as np
from contextlib import ExitStack
import concourse.bass as bass
import concourse.mybir as mybir
from concourse.bass_utils import run_bass_kernel_spmd

F32 = mybir.dt.float32
BF16 = mybir.dt.bfloat16
AF = mybir.ActivationFunctionType
ALU = mybir.AluOpType
AX = mybir.AxisListType

D = 1024
SEQ = 2048
SEG = 1024
NSEG = 2
NS = 16
FFN = 2816
NFB = 22
AB_IN = 6416
EPS = 1e-6


class Sched:
    NDMA = 24

    def __init__(self, nc, es, same_engine_sync=True):
        self.nc = nc
        self.E = {'pe': nc.tensor, 'act': nc.scalar, 'dve': nc.vector, 'pool': nc.gpsimd, 'sp': nc.sync}
        self.sem = {e: es.enter_context(nc.semaphore("s_" + e)) for e in ('pe', 'act', 'dve', 'pool')}
        self.cnt = {e: 0 for e in self.sem}
        self.dsem = [es.enter_context(nc.semaphore("d%d" % i)) for i in range(self.NDMA)]
        self.dcnt = [0] * self.NDMA
        self.dnext = 0
        self.seen = {e: {} for e in self.E}
        self.lw = {}
        self.rd = {}
        self.semobj = {}
        self.same = same_engine_sync
        self.n_wait = 0
        self.n_inst = 0
        for e, s in self.sem.items():
            self.semobj[e] = s
        for i, s in enumerate(self.dsem):
            self.semobj[('d', i)] = s

    def _wait(self, eng, tokens):
        need = {}
        for t in tokens:
            if t is None:
                continue
            sid, val = t
            if sid == eng and (not self.same or eng == 'pe'):
                continue
            if val > need.get(sid, 0):
                need[sid] = val
        for sid, val in need.items():
            if self.seen[eng].get(sid, 0) >= val:
                continue
            self.E[eng].wait_ge(self.semobj[sid], val)
            self.seen[eng][sid] = val
            self.n_wait += 1

    def deps(self, eng, reads, writes):
        toks = []
        for k in reads:
            toks.append(self.lw.get(k))
        for k in writes:
            toks.append(self.lw.get(k))
            toks.extend(self.rd.get(k, ()))
        self._wait(eng, toks)

    def commit(self, tok, reads, writes):
        for k in reads:
            self.rd.setdefault(k, []).append(tok)
        for k in writes:
            self.lw[k] = tok
            self.rd[k] = []

    def op(self, eng, fn, reads=(), writes=()):
        self.deps(eng, reads, writes)
        ins = fn()
        self.n_inst += 1
        self.cnt[eng] += 1
        ins.then_inc(self.sem[eng], 1)
        self.commit((eng, self.cnt[eng]), reads, writes)
        return ins

    def group_end(self, eng, ins, reads, writes):
        self.cnt[eng] += 1
        ins.then_inc(self.sem[eng], 1)
        self.commit((eng, self.cnt[eng]), reads, writes)

    def dma(self, q, out, in_, reads=(), writes=(), **kw):
        i = self.dnext
        self.dnext = (self.dnext + 1) % self.NDMA
        sid = ('d', i)
        if self.dcnt[i]:
            self._wait(q, [(sid, self.dcnt[i])])
        self.deps(q, reads, writes)
        self.dcnt[i] += 16
        ins = self.E[q].dma_start(out=out, in_=in_, **kw)
        ins.then_inc(self.dsem[i], 16)
        self.n_inst += 1
        self.commit((sid, self.dcnt[i]), reads, writes)
        return ins

    def all_tokens(self):
        toks = [(('d', i), self.dcnt[i]) for i in range(self.NDMA) if self.dcnt[i]]
        toks += [(e, c) for e, c in self.cnt.items() if c]
        return toks

    def barrier(self):
        toks = self.all_tokens()
        for e in ('pe', 'act', 'dve', 'pool', 'sp'):
            self._wait(e, toks)
        self.lw = {}
        self.rd = {}

    def finish(self):
        self._wait('sp', self.all_tokens())


class KB:
    def __init__(self, nc, es, dbg=None):
        self.nc = nc
        self.es = es
        import os
        self.S = Sched(nc, es, same_engine_sync=os.environ.get('K_SAME', '1') == '1')
        self.dbg = dbg or {}
        self.din = {}
        self.dout = {}
        self.uid = 0
        self.rr = 0
        self.evac_rr = 0

    def inp(self, name, shape, dt=F32):
        self.din[name] = self.nc.dram_tensor(name, list(shape), dt, kind="ExternalInput").ap()
        return self.din[name]

    def outp(self, name, shape, dt=F32):
        self.dout[name] = self.nc.dram_tensor(name, list(shape), dt, kind="ExternalOutput").ap()
        return self.dout[name]

    def sb(self, es, name, shape, dt=F32):
        self.nalloc = getattr(self, 'nalloc', 0) + 1
        return es.enter_context(self.nc.sbuf_tensor("%s_%d" % (name, self.nalloc), list(shape), dt))

    def dump(self, name, ap, reads, dt=F32):
        if not self.dbg.get('dump'):
            return
        o = self.outp("dbg_" + name, list(ap.shape), dt)
        self.S.dma('sp', o, ap, reads=reads)

    def init_psum(self):
        self.PB = [self.es.enter_context(self.nc.psum_tensor("pb%d" % i, [128, 512], F32)) for i in range(8)]

    def bank(self):
        n = getattr(self, 'nrot', 7)
        i = self.rr % n
        self.rr = (i + 1) % n
        return i

    def mm(self, out_ap, out_keys, pairs, read_keys, **kw):
        S = self.S
        S.deps('pe', read_keys, out_keys)
        n = len(pairs)
        ins = None
        for i, (l, r) in enumerate(pairs):
            ins = self.nc.tensor.matmul(out_ap, lhsT=l, rhs=r, start=(i == 0), stop=(i == n - 1), **kw)
            S.n_inst += 1
        S.group_end('pe', ins, read_keys, out_keys)

    def transpose(self, out_ap, out_keys, in_ap, ident_ap, read_keys):
        S = self.S
        S.deps('pe', read_keys, out_keys)
        ins = self.nc.tensor.transpose(out_ap, in_ap, ident_ap)
        S.n_inst += 1
        S.group_end('pe', ins, read_keys, out_keys)

    def evac_eng(self):
        self.evac_rr ^= 1
        return 'act' if self.evac_rr else 'dve'

    def copy(self, eng, out, in_, reads, writes):
        nc = self.nc
        if eng == 'act':
            return self.S.op('act', lambda: nc.scalar.copy(out=out, in_=in_), reads, writes)
        if eng == 'pool':
            return self.S.op('pool', lambda: nc.gpsimd.tensor_copy(out=out, in_=in_), reads, writes)
        return self.S.op('dve', lambda: nc.vector.tensor_copy(out=out, in_=in_), reads, writes)

    def init_wbufs(self, es, n=4, width=NFB * 128):
        self.WB = [self.sb(es, "wb%d" % i, [128, width], BF16) for i in range(n)]
        self.wrr = 0

    def load_w(self, w2d, kc, ncol=128):
        i = self.wrr
        self.wrr = (self.wrr + 1) % len(self.WB)
        key = ('wb', i)
        view = self.WB[i][:, 0:kc * ncol].rearrange("p (k n) -> p k n", n=ncol)
        self.S.dma('pool', view, w2d.rearrange("(k p) n -> p k n", p=128), writes=[key])
        return view, key


def wstream(kb, specs, depth=3):
    specs = list(specs)
    q = []
    nxt = 0
    while nxt < len(specs) and len(q) < depth:
        q.append(kb.load_w(*specs[nxt]))
        nxt += 1
    for _ in range(len(specs)):
        cur = q.pop(0)
        if nxt < len(specs):
            q.append(kb.load_w(*specs[nxt]))
            nxt += 1
        yield cur


def seg_cols(seg):
    return SEG + (NS if seg == 0 else 0)


def ttiles(seg):
    t = [(0, 512), (512, 512)]
    if seg == 0:
        t.append((SEG, NS))
    return t


def tkeys(name, cs, tt):
    return [(name, c, tt) for c in cs]


class Prog(KB):
    def setup(self):
        nc, S, es = self.nc, self.S, self.es
        self.init_psum()
        c_ident = self.inp("c_ident", [128, 128])
        self.ident = self.sb(es, "ident", [128, 128])
        self.identb = self.sb(es, "identb", [128, 128], BF16)
        self.onesb = self.sb(es, "onesb", [128, 128], BF16)
        self.epsc = self.sb(es, "epsc", [128, 1])
        S.dma('sp', self.ident[:], c_ident[:, :], writes=['ident'])
        S.op('dve', lambda: nc.vector.tensor_copy(out=self.identb[:], in_=self.ident[:]), ['ident'], ['identb'])
        S.op('dve', lambda: nc.vector.memset(self.onesb[:], 1.0), [], ['onesb'])
        S.op('dve', lambda: nc.vector.memset(self.epsc[:], EPS), [], ['epsc'])
        nw = self.inp("nw", [128, 4 * 2 * 8])
        self.NW = self.sb(es, "NW", [128, 4, 2, 8])
        S.dma('sp', self.NW[:].rearrange("p a l c -> p (a l c)"), nw[:, :], writes=['NW'])
        fcw = self.inp("fcw", [128, 2 * 44 * 3])
        fcb = self.inp("fcb", [128, 2 * 44])
        self.FCW = self.sb(es, "FCW", [128, 2, 44, 3])
        self.FCB = self.sb(es, "FCB", [128, 2, 44])
        S.dma('sp', self.FCW[:].rearrange("p l b k -> p (l b k)"), fcw[:, :], writes=['FCW'])
        S.dma('sp', self.FCB[:].rearrange("p l b -> p (l b)"), fcb[:, :], writes=['FCB'])
        self.XR = self.sb(es, "XR", [128, 8, SEG + NS])
        self.FH = self.sb(es, "FH", [128, 2, 44, 2])
        S.op('dve', lambda: nc.vector.memset(self.FH[:].rearrange("p l b k -> p (l b k)"), 0.0), [], ['FH'])
        self.init_wbufs(es)
        self.x_p = self.inp("x_p", [SEQ, D])
        self.x_s = self.inp("x_s", [NS, D])
        self.y_p = self.outp("y_p", [SEQ, D])
        self.y_s = self.outp("y_s", [NS, D])
        self.ffn_up_w = self.inp("ffn_up_w", [2, D, 2 * FFN])
        self.ffn_down_w = self.inp("ffn_down_w", [2, FFN, D])
        self.st_fconv = self.inp("st_fconv", [2, NS * 2, 2 * FFN])
        self.p_fconv = self.outp("p_fconv", [2, 2, 2 * FFN])
        self.s_fconv = self.outp("s_fconv", [2, NS, 2, 2 * FFN])

    def load_x(self, seg):
        nc, S = self.nc, self.S
        with ExitStack() as es:
            XT = [self.sb(es, "xt%d" % i, [128, D]) for i in range(2)]
            for tb in range(8):
                xt = XT[tb % 2]
                k = ('xt', tb % 2)
                r0 = seg * SEG + tb * 128
                S.dma('sp', xt[:], self.x_p[r0:r0 + 128, :], writes=[k])
                for half in range(2):
                    b = self.bank()
                    for j in range(4):
                        c = half * 4 + j
                        self.transpose(self.PB[b][:, j * 128:(j + 1) * 128], [('pb', b)],
                                       xt[:, c * 128:(c + 1) * 128], self.ident[:], [k, 'ident'])
                    self.copy(self.evac_eng(), self.XR[:, half * 4:(half + 1) * 4, tb * 128:(tb + 1) * 128],
                              self.PB[b][:, :].rearrange("p (c t) -> p c t", t=128),
                              [('pb', b)], tkeys('xr', range(half * 4, half * 4 + 4), tb // 4))
            if seg == 0:
                xs = self.sb(es, "xs_in", [NS, D])
                S.dma('sp', xs[:], self.x_s[:, :], writes=['xs_in'])
                b = self.bank()
                for c in range(8):
                    self.transpose(self.PB[b][:, c * NS:(c + 1) * NS], [('pb', b)],
                                   xs[:, c * 128:(c + 1) * 128], self.ident[0:NS, 0:NS], ['xs_in', 'ident'])
                self.copy('dve', self.XR[:, :, SEG:SEG + NS],
                          self.PB[b][:, 0:8 * NS].rearrange("p (c t) -> p c t", t=NS),
                          [('pb', b)], tkeys('xr', range(8), 2))
            S.barrier()

    def store_x(self, seg):
        nc, S = self.nc, self.S
        with ExitStack() as es:
            YT = [self.sb(es, "yt%d" % i, [128, D]) for i in range(2)]
            for tb in range(8):
                yt = YT[tb % 2]
                k = ('yt', tb % 2)
                for half in range(2):
                    b = self.bank()
                    for j in range(4):
                        c = half * 4 + j
                        self.transpose(self.PB[b][:, j * 128:(j + 1) * 128], [('pb', b)],
                                       self.XR[:, c, tb * 128:(tb + 1) * 128], self.ident[:],
                                       [('xr', c, tb // 4), 'ident'])
                    self.copy(self.evac_eng(), yt[:, half * 512:(half + 1) * 512], self.PB[b][:, :],
                              [('pb', b)], [k])
                r0 = seg * SEG + tb * 128
                S.dma('sp', self.y_p[r0:r0 + 128, :], yt[:], reads=[k])
            if seg == 0:
                ys = self.sb(es, "ys_out", [NS, D])
                for half in range(2):
                    b = self.bank()
                    for j in range(4):
                        c = half * 4 + j
                        self.transpose(self.PB[b][0:NS, j * 128:(j + 1) * 128], [('pb', b)],
                                       self.XR[:, c, SEG:SEG + NS], self.ident[:], [('xr', c, 2), 'ident'])
                    self.copy('dve', ys[:, half * 512:(half + 1) * 512], self.PB[b][0:NS, :], [('pb', b)], ['ys_out'])
                S.dma('sp', self.y_s[:, :], ys[:], reads=['ys_out'])
            S.barrier()

    def rmsnorm_fm(self, es, seg, src, sname, wcol, dst, dname, nch=8, residual=False, ncols_div=None):
        nc, S = self.nc, self.S
        nfeat = float(nch * 128) if ncols_div is None else float(ncols_div)
        SQ = [self.sb(es, "sq%d_%d" % (self.uid, i), [128, nch, 512], BF16) for i in range(2)]
        RS = [self.sb(es, "rs%d_%d" % (self.uid, i), [128, 512]) for i in range(2)]
        TMP = [self.sb(es, "nt%d_%d" % (self.uid, i), [128, 512]) for i in range(2)] if residual else None
        self.uid += 1
        for ti, (c0, n) in enumerate(ttiles(seg)):
            sq, rs = SQ[ti % 2], RS[ti % 2]
            ksq, krs = ('sq', self.uid, ti % 2), ('rs', self.uid, ti % 2)
            S.op('act', lambda: nc.scalar.activation(out=sq[:, :, 0:n], in_=src[:, 0:nch, c0:c0 + n], func=AF.Square),
                 tkeys(sname, range(nch), ti), [ksq])
            b = self.bank()
            self.mm(self.PB[b][:, 0:n], [('pb', b)],
                    [(self.onesb[:], sq[:, c, 0:n]) for c in range(nch)], [ksq, 'onesb'])
            S.op('act', lambda: nc.scalar.activation(out=rs[:, 0:n], in_=self.PB[b][:, 0:n], func=AF.Sqrt,
                                                     bias=self.epsc[:, 0:1], scale=1.0 / nfeat),
                 [('pb', b), 'epsc'], [krs])
            if self.dbg.get('dump') and seg == 1 and ti == 0 and not residual and not self.dbg.get('d1'):
                self.dbg['d1'] = 1
                self.dump("rs_sqrt", rs[:, 0:16], [krs])
                self.dump("sq", sq[:, :, 0:16], [ksq], BF16)
                self.dump("xr", src[:, :, 0:16], tkeys(sname, range(nch), ti))
            S.op('dve', lambda: nc.vector.reciprocal(out=rs[:, 0:n], in_=rs[:, 0:n]), [krs], [krs])
            for c in range(nch):
                if not residual:
                    S.op('dve', lambda: nc.vector.scalar_tensor_tensor(
                        out=dst[:, c, c0:c0 + n], in0=src[:, c, c0:c0 + n], scalar=wcol[:, c:c + 1],
                        in1=rs[:, 0:n], op0=ALU.mult, op1=ALU.mult),
                        [(sname, c, ti), krs, 'NW'], [(dname, c, ti)])
                else:
                    tmp = TMP[c % 2]
                    kt = ('ntmp', self.uid, c % 2)
                    S.op('dve', lambda: nc.vector.scalar_tensor_tensor(
                        out=tmp[:, 0:n], in0=src[:, c, c0:c0 + n], scalar=wcol[:, c:c + 1],
                        in1=rs[:, 0:n], op0=ALU.mult, op1=ALU.mult),
                        [(sname, c, ti), krs, 'NW'], [kt])
                    S.op('pool', lambda: nc.gpsimd.tensor_tensor(
                        out=dst[:, c, c0:c0 + n], in0=dst[:, c, c0:c0 + n], in1=tmp[:, 0:n], op=ALU.add),
                        [kt, (dname, c, ti)], [(dname, c, ti)])

    def ffn(self, l, seg):
        nc, S = self.nc, self.S
        ncol = seg_cols(seg)
        tts = ttiles(seg)
        with ExitStack() as es0:
          H = self.sb(es0, "f_h", [128, NFB, ncol], BF16)
          NDW = 2 if seg > 0 else 0
          DW = [self.sb(es0, "f_dw%d" % i, [128, NFB * 128], BF16) for i in range(NDW)]
          with ExitStack() as es:
            HN = self.sb(es, "f_hn", [128, 8, ncol], BF16)
            with ExitStack() as es2:
                self.rmsnorm_fm(es2, seg, self.XR, 'xr', self.NW[:, 2, l, :], HN, 'hn')
                S.barrier()
            PRE = [[self.sb(es, "pre%d_%d" % (i, vg), [128, 2 + ncol]) for vg in range(2)] for i in range(2)]
            ACC = [[self.sb(es, "acc%d_%d" % (i, vg), [128, ncol]) for vg in range(2)] for i in range(2)]
            if seg == 0:
                CS = self.sb(es, "f_cs", [128, 44, 2 * NS])
                SNEW = self.sb(es, "f_snew", [128, 44, NS])
                with ExitStack() as es2:
                    cst = self.sb(es2, "f_cst", [2 * NS, 2 * FFN])
                    S.dma('sp', cst[:], self.st_fconv[l, :, :], writes=['cst'])
                    for g in range(11):
                        b = self.bank()
                        for j in range(4):
                            blk = g * 4 + j
                            self.transpose(self.PB[b][:, j * 32:(j + 1) * 32], [('pb', b)],
                                           cst[:, blk * 128:(blk + 1) * 128], self.ident[0:32, 0:32],
                                           ['cst', 'ident'])
                        self.copy(self.evac_eng(), CS[:, g * 4:(g + 1) * 4, :],
                                  self.PB[b][:, 0:128].rearrange("p (j r) -> p j r", r=32),
                                  [('pb', b)], ['f_cs'])
                    S.barrier()
            ws_up = wstream(self, [(self.ffn_up_w[l, :, (vg * NFB + j) * 128:(vg * NFB + j + 1) * 128], 8)
                                   for j in range(NFB) for vg in range(2)])
            for j in range(NFB):
                pre, acc = PRE[j % 2], ACC[j % 2]
                if NDW and j in (4, 12):
                    i = 0 if j == 4 else 1
                    S.dma('pool', DW[i][:].rearrange("p (k n) -> p k n", n=128),
                          self.ffn_down_w[l, :, i * 128:(i + 1) * 128].rearrange("(k p) n -> p k n", p=128), writes=[('dw', i)])
                for vg in range(2):
                    blk = vg * NFB + j
                    kpre, kacc = ('pre', j % 2, vg), ('acc', j % 2, vg)
                    wv, wk = next(ws_up)
                    S.op('pool', lambda: nc.gpsimd.tensor_copy(out=pre[vg][:, 0:2], in_=self.FH[:, l, blk, :]),
                         ['FH'], [kpre])
                    for ti, (c0, n) in enumerate(tts):
                        b = self.bank()
                        self.mm(self.PB[b][:, 0:n], [('pb', b)],
                                [(wv[:, kc, :], HN[:, kc, c0:c0 + n]) for kc in range(8)],
                                [wk] + tkeys('hn', range(8), ti))
                        if ti == 2:
                            self.copy('act', SNEW[:, blk, :], self.PB[b][:, 0:n], [('pb', b)], [('snew', blk)])
                        else:
                            self.copy('act', pre[vg][:, 2 + c0:2 + c0 + n], self.PB[b][:, 0:n],
                                      [('pb', b)], [kpre])
                    S.op('pool', lambda: nc.gpsimd.tensor_copy(out=self.FH[:, l, blk, :], in_=pre[vg][:, SEG:SEG + 2]),
                         [kpre], ['FH'])
                    w = self.FCW[:, l, blk, :]
                    S.op('act', lambda: nc.scalar.activation(out=acc[vg][:, 0:SEG], in_=pre[vg][:, 2:2 + SEG],
                                                             func=AF.Identity, bias=self.FCB[:, l, blk:blk + 1],
                                                             scale=w[:, 2:3]),
                         [kpre, 'FCW', 'FCB'], [kacc])
                    S.op('dve', lambda: nc.vector.scalar_tensor_tensor(
                        out=acc[vg][:, 0:SEG], in0=pre[vg][:, 1:1 + SEG], scalar=w[:, 1:2], in1=acc[vg][:, 0:SEG],
                        op0=ALU.mult, op1=ALU.add), [kpre, kacc, 'FCW'], [kacc])
                    S.op('dve', lambda: nc.vector.scalar_tensor_tensor(
                        out=acc[vg][:, 0:SEG], in0=pre[vg][:, 0:SEG], scalar=w[:, 0:1], in1=acc[vg][:, 0:SEG],
                        op0=ALU.mult, op1=ALU.add), [kpre, kacc, 'FCW'], [kacc])
                if seg == 1 and j == 0 and l == 0:
                    self.dump("pre_v", pre[0][:, 0:16], [('pre', 0, 0)])
                    self.dump("acc_v", acc[0][:, 0:16], [('acc', 0, 0)])
                    self.dump("acc_g", acc[1][:, 0:16], [('acc', 0, 1)])
                    self.dump("hn", HN[:, 0, 0:16], tkeys('hn', [0], 0), BF16)
                kv, kg = ('acc', j % 2, 0), ('acc', j % 2, 1)
                S.op('act', lambda: nc.scalar.activation(out=acc[1][:, 0:SEG], in_=acc[1][:, 0:SEG], func=AF.Silu),
                     [kg], [kg])
                S.op('dve', lambda: nc.vector.tensor_tensor(out=H[:, j, 0:SEG], in0=acc[0][:, 0:SEG], in1=acc[1][:, 0:SEG],
                                                            op=ALU.mult), [kv, kg], tkeys('h', [j], 0) + tkeys('h', [j], 1))
            if seg == 0:
                es3 = ExitStack()
                ACS = self.sb(es3, "f_acs", [128, 44, NS])
                TMS = self.sb(es3, "f_tms", [128, 44, NS])
                ksn = [('snew', blk) for blk in range(44)]
                cs4 = CS[:].rearrange("p a (b k) -> p a b k", k=2)
                wb = lambda k: self.FCW[:, l, :, k:k + 1].to_broadcast([128, 44, NS])
                S.op('dve', lambda: nc.vector.tensor_tensor(out=ACS[:], in0=SNEW[:], in1=wb(2), op=ALU.mult), ksn + ['FCW'], ['f_acs'])
                for k in (1, 0):
                    S.op('dve', lambda: nc.vector.tensor_tensor(out=TMS[:], in0=cs4[:, :, :, k], in1=wb(k), op=ALU.mult),
                         ['f_cs', 'FCW'], ['f_tms'])
                    S.op('dve', lambda: nc.vector.tensor_tensor(out=ACS[:], in0=ACS[:], in1=TMS[:], op=ALU.add), ['f_acs', 'f_tms'], ['f_acs'])
                S.op('dve', lambda: nc.vector.tensor_tensor(out=ACS[:], in0=ACS[:], in1=self.FCB[:, l, :].unsqueeze(2).to_broadcast([128, 44, NS]),
                                                            op=ALU.add), ['f_acs', 'FCB'], ['f_acs'])
                S.op('act', lambda: nc.scalar.activation(out=ACS[:, NFB:2 * NFB, :], in_=ACS[:, NFB:2 * NFB, :], func=AF.Silu), ['f_acs'], ['f_acs'])
                S.op('dve', lambda: nc.vector.tensor_tensor(out=H[:, :, SEG:SEG + NS], in0=ACS[:, 0:NFB, :], in1=ACS[:, NFB:2 * NFB, :],
                                                            op=ALU.mult), ['f_acs'], tkeys('h', range(NFB), 2))
                S.barrier()
                es3.close()
            if seg == NSEG - 1:
                with nc.allow_non_contiguous_dma(reason="tiny conv-state scatter"):
                    for k in range(2):
                        S.dma('sp', self.p_fconv[l, k].rearrange("(b p) -> p b", p=128), self.FH[:, l, :, k], reads=['FH'])
            if seg == 0:
                with ExitStack() as es2:
                    so = self.sb(es2, "f_so", [NS, 2 * FFN])
                    for g in range(11):
                        b = self.bank()
                        for jj in range(4):
                            blk = g * 4 + jj
                            self.transpose(self.PB[b][0:NS, jj * 128:(jj + 1) * 128], [('pb', b)],
                                           SNEW[:, blk, :], self.ident[:], [('snew', blk), 'ident'])
                        self.copy(self.evac_eng(), so[:, g * 512:(g + 1) * 512], self.PB[b][0:NS, :], [('pb', b)], ['f_so'])
                    S.dma('sp', self.s_fconv[l, :, 1, :], so[:], reads=['f_so'])
                    S.dma('sp', self.s_fconv[l, :, 0, :],
                          self.st_fconv[l].rearrange("(b k) c -> b k c", k=2)[:, 1, :])
                    S.barrier()
            S.barrier()
          with ExitStack() as es:
            Fo = self.sb(es, "f_out", [128, 8, ncol])
            ws_dn = wstream(self, [(self.ffn_down_w[l, :, ob * 128:(ob + 1) * 128], NFB) for ob in range(NDW, 8)])
            for ob in range(8):
                if ob < NDW:
                    wv, wk = DW[ob][:].rearrange("p (k n) -> p k n", n=128), ('dw', ob)
                else:
                    wv, wk = next(ws_dn)
                for ti, (c0, n) in enumerate(tts):
                    b = self.bank()
                    self.mm(self.PB[b][:, 0:n], [('pb', b)],
                            [(wv[:, kc, :], H[:, kc, c0:c0 + n]) for kc in range(NFB)],
                            [wk] + tkeys('h', range(NFB), ti))
                    self.copy(self.evac_eng(), Fo[:, ob, c0:c0 + n], self.PB[b][:, 0:n], [('pb', b)], [('fo', ob, ti)])
            with ExitStack() as es2:
                self.rmsnorm_fm(es2, seg, Fo, 'fo', self.NW[:, 3, l, :], self.XR, 'xr', residual=True)
                S.barrier()
            S.barrier()


def build_program(stage="full", dbg=None):
    nc = bass.Bass("TRN2", target_bir_lowering=False)
    es = ExitStack()
    with es:
        P = ProgC2(nc, es, dbg)
        P.setup()
        if stage in ('ssd', 'hg', 'ab', 'full'):
            P.setup_ab()
            P.setup_hg()
            P.setup_ab_sample()
        if stage in ('s5', 'full'):
            if stage == 's5':
                P.stg = [P.sb(es, 'stg%d' % i, [NS, 128]) for i in range(4)]
                P.stg_rr = 0
                P.SC = {}
            P.setup_s5()
        if stage == "copy":
            for seg in range(NSEG):
                P.load_x(seg)
                P.store_x(seg)
        if stage == "ssd":
            ydbg = P.outp("dbg_y", [128, 8, SEQ], BF16)
            for seg in range(NSEG):
                P.load_x(seg)
                with ExitStack() as es0:
                    MIX = P.sb(es0, "mix", [128, 16, seg_cols(seg)], BF16)
                    HN = P.sb(es0, "hn", [128, 8, seg_cols(seg)], BF16)
                    with ExitStack() as es2:
                        P.rmsnorm_fm(es2, seg, P.XR, 'xr', P.NW[:, 0, 0, :], HN, 'hn')
                        P.S.barrier()
                    P.ssd_phase(seg, HN, MIX)
                    P.S.dma('sp', ydbg[:, :, seg * SEG:(seg + 1) * SEG], MIX[:, 0:8, 0:SEG],
                            reads=[('mix', c, t) for c in range(8) for t in range(2)])
                    P.S.barrier()
        if stage == "hg":
            ydbg = P.outp("dbg_o", [128, 8, SEQ], BF16)
            for seg in range(NSEG):
                P.load_x(seg)
                with ExitStack() as es0:
                    MIX = P.sb(es0, "mix", [128, 16, seg_cols(seg)], BF16)
                    HN = P.sb(es0, "hn", [128, 8, seg_cols(seg)], BF16)
                    with ExitStack() as es2:
                        P.rmsnorm_fm(es2, seg, P.XR, 'xr', P.NW[:, 0, 0, :], HN, 'hn')
                        P.S.barrier()
                    P.hgrn_phase(seg, HN, MIX)
                    P.S.dma('sp', ydbg[:, :, seg * SEG:(seg + 1) * SEG], MIX[:, 8:16, 0:SEG],
                            reads=[('mix', c, t) for c in range(8, 16) for t in range(2)])
                    P.S.barrier()
        if stage == "ab":
            for seg in range(NSEG):
                P.load_x(seg)
                P.mixer_ab(seg)
                P.store_x(seg)
        if stage == "s5":
            for seg in range(NSEG):
                P.load_x(seg)
                P.mixer_c(seg)
                P.store_x(seg)
        if stage == "full":
            for seg in range(NSEG):
                P.load_x(seg)
                P.mixer_ab(seg)
                P.ffn(0, seg)
                P.mixer_c(seg)
                P.ffn(1, seg)
                P.store_x(seg)
        if stage == "ffn":
            for seg in range(NSEG):
                P.load_x(seg)
                P.ffn(0, seg)
                P.store_x(seg)
        P.S.finish()
        print("program: inst=%d waits=%d" % (P.S.n_inst, P.S.n_wait), flush=True)
    return nc, P


def _fm(v, nch):
    return np.ascontiguousarray(np.asarray(v, np.float32).reshape(nch, 128).T)


def host_inputs(inp, core):
    m = {}
    m["c_ident"] = np.eye(128, dtype=np.float32)
    m["x_p"] = np.ascontiguousarray(inp["x_prompt"][core])
    m["x_s"] = np.ascontiguousarray(inp["x_sample"][core * NS:(core + 1) * NS, 0, :])
    nw = np.zeros((128, 4, 2, 8), np.float32)
    for a, k in enumerate(["norm_mix_pre", "norm_mix_post", "norm_ffn_pre", "norm_ffn_post"]):
        for l in range(2):
            nw[:, a, l, :] = _fm(inp[k][l], 8)
    m["nw"] = nw.reshape(128, -1)
    fcw = np.zeros((128, 2, 44, 3), np.float32)
    fcb = np.zeros((128, 2, 44), np.float32)
    for l in range(2):
        for k in range(3):
            fcw[:, l, :, k] = _fm(inp["ffn_conv_w"][l, k], 44)
        fcb[:, l, :] = _fm(inp["ffn_conv_b"][l], 44)
    m["fcw"] = fcw.reshape(128, -1)
    m["fcb"] = fcb.reshape(128, -1)
    m["ffn_up_w"] = np.asarray(inp["ffn_up_w"], np.float32)
    m["ffn_down_w"] = np.asarray(inp["ffn_down_w"], np.float32)
    m["ab_in_w"] = np.asarray(inp["ab_in_w"][0], np.float32)
    m["ab_out_w"] = np.asarray(inp["ab_out_w"][0], np.float32)
    t = np.arange(128)
    m["c_tri"] = (t[:, None] <= t[None, :]).astype(np.float32)
    m["c_neg"] = np.where(t[:, None] <= t[None, :], 0.0, -30000.0).astype(np.float32)
    m["c_bmask"] = ((t[:, None] <= t[None, :]) & (t[:, None] // 32 == t[None, :] // 32)).astype(np.float32)
    rst = np.ones((128, SEG), np.float32)
    rst[:, 0::32] = 0.0
    m["c_rst"] = rst
    scw = np.zeros((128, 10, 4), np.float32)
    for k in range(4):
        scw[:, :, k] = _fm(inp["ssm_conv_w"][0, k], 10)
    m["scw"] = scw.reshape(128, 40)
    m["scb"] = _fm(inp["ssm_conv_b"][0], 10)
    m["ssm_v16"] = np.stack([inp["ssm_dt_bias"][0], inp["ssm_a_log"][0], inp["ssm_d"][0]]).astype(np.float32)
    m["snw"] = _fm(inp["ssm_norm_w"][0], 8)
    lbf = np.zeros((128, 2, 8), np.float32)
    for r in range(2):
        lbf[:, r, :] = _fm(inp["hgrn_lb"][r], 8)
    m["hgrn_lb_fm"] = lbf.reshape(128, 16)
    m["hgrn_nw"] = np.asarray(inp["hgrn_norm_w"][0], np.float32).reshape(128, 1)
    m["c_cmask"] = (t[:, None] // 32 == np.arange(4)[None, :]).astype(np.float32)
    sl = slice(core * NS, (core + 1) * NS)
    m["st_sconv"] = np.ascontiguousarray(inp["state_ssm_conv"][0, sl])
    m["st_ssm"] = np.ascontiguousarray(inp["state_ssm"][0, sl].reshape(NS * 16, 4096))
    m["st_hg"] = np.ascontiguousarray(inp["state_hgrn"][0, sl].reshape(NS * 8, 128 * 128))
    m["ssm_conv_wb"] = np.concatenate([inp["ssm_conv_w"][0], inp["ssm_conv_b"][0][None]], 0).astype(np.float32)
    v16 = m["ssm_v16"].T.reshape(2, 1, 8, 3)
    m["v16bh"] = np.ascontiguousarray(np.broadcast_to(v16, (2, 8, 8, 3))).reshape(128, 3)
    m["snw_row"] = np.asarray(inp["ssm_norm_w"][0], np.float32).reshape(1, D)
    m["hnw_row"] = np.asarray(inp["hgrn_norm_w"][0], np.float32).reshape(1, 128)
    m["hgrn_lb_bh"] = np.ascontiguousarray(np.broadcast_to(
        np.asarray(inp["hgrn_lb"], np.float32).reshape(2, 1, 8, 128), (2, NS, 8, 128))).reshape(2, 128, 128)
    lam = np.stack([inp["s5_lam_re"][0], inp["s5_lam_im"][0]]).astype(np.float32)
    ls = np.asarray(inp["s5_log_step"][0], np.float32)
    bb = np.stack([inp["s5_b_re"][0], inp["s5_b_im"][0]]).astype(np.float32)
    cc = np.stack([inp["s5_c_re"][0], inp["s5_c_im"][0]]).astype(np.float32)
    lamA = np.broadcast_to(lam.reshape(2, 8, 8, 1, 64).transpose(0, 2, 3, 1, 4), (2, 8, 16, 8, 64))
    m["s5_lamA"] = np.ascontiguousarray(lamA).reshape(2, 128, 512)
    m["s5_lsA"] = np.ascontiguousarray(np.broadcast_to(ls.reshape(8, 8).T[:, None, :], (8, 16, 8))).reshape(128, 8)
    m["s5_bA"] = np.ascontiguousarray(bb.reshape(2, 8, 8, 64, 16).transpose(0, 2, 4, 1, 3)).reshape(2, 128, 512)
    m["s5_lamC"] = np.ascontiguousarray(lam.reshape(2, 32, 2, 64).transpose(0, 2, 3, 1)).reshape(2, 128, 32)
    m["s5_lsC"] = np.ascontiguousarray(np.broadcast_to(ls.reshape(32, 2).T[:, None, :], (2, 64, 32))).reshape(128, 32)
    m["s5_cC"] = np.ascontiguousarray(cc.reshape(2, 32, 2, 16, 64).transpose(0, 2, 4, 1, 3)).reshape(2, 128, 512)
    m["s5_lam_row"] = lam.reshape(2, 4096)
    m["s5_ls_row"] = np.ascontiguousarray(np.broadcast_to(ls[:, None], (64, 64))).reshape(1, 4096)
    m["s5_glu_w"] = np.asarray(inp["s5_glu_w"][0], np.float32)
    m["s5_d_fm"] = _fm(inp["s5_d"][0], 8)
    q = np.arange(128)
    s5m = np.zeros((128, 8), np.float32)
    s5m[:, 0] = ((q // 16) % 2 == 0); s5m[:, 1] = ((q // 16) % 2 == 1)
    for p4 in range(4):
        s5m[:, 2 + p4] = (q // 32 == p4)
    s5m[:, 6] = (q // 64 == 0); s5m[:, 7] = (q // 64 == 1)
    m["c_s5masks"] = s5m
    m["c_iota"] = np.tile(np.arange(1, 129, dtype=np.float32)[None, :], (128, 1))
    r8 = np.ones((128, 512), np.float32); r8[:, 0::8] = 0.0
    m["c_rst8"] = r8
    rj = np.ones((128, 128), np.float32); rj[:, 0] = 0.0
    m["c_rstj"] = rj
    m["st_s5_re"] = np.ascontiguousarray(inp["state_s5_re"][0, sl].reshape(NS, 4096))
    m["st_s5_im"] = np.ascontiguousarray(inp["state_s5_im"][0, sl].reshape(NS, 4096))
    m["st_fconv"] = np.ascontiguousarray(
        inp["state_ffn_conv"][:, core * NS:(core + 1) * NS].reshape(2, NS * 2, 2 * FFN))
    return m


_CACHE = {}


def run_stage(inp, stage, cores=8, dbg=None):
    if stage not in _CACHE:
        _CACHE[stage] = build_program(stage, dbg)
    nc, P = _CACHE[stage]
    in_maps = []
    for c in range(cores):
        hm = host_inputs(inp, c)
        in_maps.append({k: hm[k] for k in P.din})
    res = run_bass_kernel_spmd(nc, in_maps, core_ids=list(range(cores)))
    return res.results


OFF_Z, OFF_X, OFF_B, OFF_C, OFF_DT, OFF_Q, OFF_F, OFF_I, OFF_G = 0, 1024, 2048, 2176, 2304, 2320, 3344, 4368, 5392
NTB = SEG // 128


class ProgAB(Prog):
    def setup_ab(self):
        nc, S, es = self.nc, self.S, self.es
        self.ab_in_w = self.inp("ab_in_w", [D, AB_IN])
        self.ab_out_w = self.inp("ab_out_w", [2 * D, D])
        c_tri = self.inp("c_tri", [128, 128])
        c_neg = self.inp("c_neg", [128, 128])
        c_bmask = self.inp("c_bmask", [128, 128])
        c_rst = self.inp("c_rst", [128, SEG])
        self.TRI = self.sb(es, "TRI", [128, 128])
        self.NEG = self.sb(es, "NEG", [128, 128])
        self.BMASK = self.sb(es, "BMASK", [128, 128])
        self.RST = self.sb(es, "RST", [128, SEG])
        self.ONESF = self.sb(es, "ONESF", [128, 128])
        self.ONEC = self.sb(es, "ONEC", [128, 1])
        S.dma('sp', self.TRI[:], c_tri[:, :], writes=['TRI'])
        S.dma('sp', self.NEG[:], c_neg[:, :], writes=['NEG'])
        S.dma('sp', self.BMASK[:], c_bmask[:, :], writes=['BMASK'])
        S.dma('sp', self.RST[:], c_rst[:, :], writes=['RST'])
        S.op('dve', lambda: nc.vector.memset(self.ONESF[:], 1.0), [], ['ONESF'])
        S.op('dve', lambda: nc.vector.memset(self.ONEC[:], 1.0), [], ['ONEC'])
        scw = self.inp("scw", [128, 40])
        scb = self.inp("scb", [128, 10])
        self.SCW = self.sb(es, "SCW", [128, 10, 4])
        self.SCB = self.sb(es, "SCB", [128, 10])
        S.dma('sp', self.SCW[:].rearrange("p b k -> p (b k)"), scw[:, :], writes=['SCW'])
        S.dma('sp', self.SCB[:], scb[:, :], writes=['SCB'])
        v16 = self.inp("ssm_v16", [3, 16])
        self.V16 = self.sb(es, "V16", [128, 3, 16])
        for r in range(3):
            S.dma('sp', self.V16[:, r, :], v16[r:r + 1, :].partition_broadcast(128), writes=['V16'])
        self.ABC = self.sb(es, "ABC", [128, 16])
        S.op('act', lambda: nc.scalar.activation(out=self.ABC[:], in_=self.V16[:, 1, :], func=AF.Exp), ['V16'], ['ABC'])
        S.op('dve', lambda: nc.vector.tensor_scalar(out=self.ABC[:], in0=self.ABC[:], scalar1=-1.0, scalar2=None,
                                                    op0=ALU.mult), ['ABC'], ['ABC'])
        snw = self.inp("snw", [128, 8])
        self.SNW = self.sb(es, "SNW", [128, 8])
        S.dma('sp', self.SNW[:], snw[:, :], writes=['NW'])
        self.XH = self.sb(es, "XH", [128, 10, 3])
        self.HS = self.sb(es, "HS", [128, 512])
        self.HSB = self.sb(es, "HSB", [128, 512], BF16)
        S.op('dve', lambda: nc.vector.memset(self.XH[:].rearrange("p b k -> p (b k)"), 0.0), [], ['XH'])
        S.op('dve', lambda: nc.vector.memset(self.HS[:], 0.0), [], ['HS'])
        S.op('dve', lambda: nc.vector.memset(self.HSB[:], 0.0), [], ['HSB'])
        self.DIH = self.sb(es, "DIH", [128, 16, 128], BF16)
        for h in range(16):
            S.op('dve', lambda: nc.vector.tensor_scalar(out=self.DIH[:, h, :], in0=self.ident[:], scalar1=self.V16[:, 2, h:h + 1],
                                                        scalar2=None, op0=ALU.mult), ['ident', 'V16'], ['DIH'])
        self.p_sconv = self.outp("p_sconv", [3, 1280])
        self.p_ssm = self.outp("p_ssm", [16, 64, 64])

    def bfbank(self, b):
        return self.PB[b][:, :].bitcast(BF16)

    def ssd_phase(self, seg, HN, MIX):
        nc, S = self.nc, self.S
        W = self.ab_in_w
        ncol = seg_cols(seg)
        with ExitStack() as es:
            XTOK = self.sb(es, "xtok", [128, NTB, D], BF16)
            BFM = self.sb(es, "bfm", [128, SEG], BF16)
            BFMG = [self.sb(es, "bfmg%d" % g, [128, SEG], BF16) for g in range(2)]
            for g in range(2):
                S.op('pool', lambda: nc.gpsimd.memset(BFMG[g][:], 0.0), [], [('bfmg', g)])
            CFM = self.sb(es, "cfm", [128, SEG], BF16)
            BTOK = self.sb(es, "btok", [128, NTB, 128], BF16)
            DT = self.sb(es, "dt", [128, NTB, 16])
            DTA = self.sb(es, "dta", [128, NTB, 16])
            CUMT = self.sb(es, "cumt", [128, NTB, 16])
            wdt, wk = self.load_w(W[:, OFF_DT:OFF_DT + 16], 8, ncol=16)
            for tb in range(NTB):
                b = self.bank()
                self.mm(self.PB[b][:, 0:16], [('pb', b)],
                        [(HN[:, kc, tb * 128:(tb + 1) * 128], wdt[:, kc, :]) for kc in range(8)],
                        [wk] + tkeys('hn', range(8), tb // 4))
                S.op('dve', lambda: nc.vector.tensor_tensor(out=DT[:, tb, :], in0=self.PB[b][:, 0:16],
                                                            in1=self.V16[:, 0, :], op=ALU.add),
                     [('pb', b), 'V16'], ['dt'])
            if seg == 0:
                self.sample_proj('dt', 0, wdt, wk, HN, ncols=16)
            dt2 = DT[:].rearrange("p a h -> p (a h)")
            S.op('act', lambda: nc.scalar.activation(out=dt2, in_=dt2, func=AF.Exp), ['dt'], ['dt'])
            S.op('act', lambda: nc.scalar.activation(out=dt2, in_=dt2, func=AF.Ln, bias=self.ONEC[:, 0:1], scale=1.0),
                 ['dt', 'ONEC'], ['dt'])
            S.op('dve', lambda: nc.vector.tensor_tensor(out=DTA[:], in0=DT[:],
                                                        in1=self.ABC[:].unsqueeze(1).to_broadcast([128, NTB, 16]),
                                                        op=ALU.mult), ['dt', 'ABC'], ['dta'])
            if self.dbg.get('ssd_stop') == 1:
                S.barrier()
                return
            with ExitStack() as es1:
                PRE = [self.sb(es1, "spre%d" % i, [128, 3 + SEG]) for i in range(2)]
                ACC = [self.sb(es1, "sacc%d" % i, [128, SEG]) for i in range(2)]
                XS = [self.sb(es1, "sxs%d" % i, [128, SEG], BF16) for i in range(2)]
                blks = [8, 9, 0, 1, 2, 3, 4, 5, 6, 7]
                ws_x = wstream(self, [(W[:, OFF_X + blk * 128:OFF_X + (blk + 1) * 128], 8) for blk in blks])
                for it, blk in enumerate(blks):
                    pre, acc, xs = PRE[it % 2], ACC[it % 2], XS[it % 2]
                    kpre, kacc, kxs = ('spre', it % 2), ('sacc', it % 2), ('sxs', it % 2)
                    wv, wk = next(ws_x)
                    S.op('pool', lambda: nc.gpsimd.tensor_copy(out=pre[:, 0:3], in_=self.XH[:, blk, :]), ['XH'], [kpre])
                    for ti in range(2):
                        c0 = ti * 512
                        b = self.bank()
                        self.mm(self.PB[b][:, :], [('pb', b)],
                                [(wv[:, kc, :], HN[:, kc, c0:c0 + 512]) for kc in range(8)],
                                [wk] + tkeys('hn', range(8), ti))
                        self.copy(self.evac_eng(), pre[:, 3 + c0:3 + c0 + 512], self.PB[b][:, :], [('pb', b)], [kpre])
                    if seg == 0:
                        self.ssd_sample_proj(blk, wv, wk, HN)
                    S.op('pool', lambda: nc.gpsimd.tensor_copy(out=self.XH[:, blk, :], in_=pre[:, SEG:SEG + 3]),
                         [kpre], ['XH'])
                    w = self.SCW[:, blk, :]
                    S.op('act', lambda: nc.scalar.activation(out=acc[:], in_=pre[:, 3:3 + SEG], func=AF.Identity,
                                                             bias=self.SCB[:, blk:blk + 1], scale=w[:, 3:4]),
                         [kpre, 'SCW', 'SCB'], [kacc])
                    for k in (2, 1, 0):
                        S.op('dve', lambda: nc.vector.scalar_tensor_tensor(
                            out=acc[:], in0=pre[:, k:k + SEG], scalar=w[:, k:k + 1], in1=acc[:],
                            op0=ALU.mult, op1=ALU.add), [kpre, kacc, 'SCW'], [kacc])
                    dst, kd = (BFM, 'bfm') if blk == 8 else (CFM, 'cfm') if blk == 9 else (xs, kxs)
                    S.op('act', lambda: nc.scalar.activation(out=dst[:], in_=acc[:], func=AF.Silu), [kacc], [kd])
                    if blk == 9:
                        continue
                    if blk == 8:
                        for g in range(2):
                            ps = slice(g * 64, (g + 1) * 64)
                            self.copy('pool', BFMG[g][ps, :], BFM[ps, :], ['bfm'], [('bfmg', g)])
                    for half in range(2):
                        b = self.bank()
                        pbf = self.bfbank(b)
                        for j in range(4):
                            tb = half * 4 + j
                            self.transpose(pbf[:, j * 128:(j + 1) * 128], [('pb', b)],
                                           dst[:, tb * 128:(tb + 1) * 128], self.identb[:], [kd, 'identb'])
                        src = pbf[:, 0:512].rearrange("p (j f) -> p j f", f=128)
                        if blk == 8:
                            self.copy(self.evac_eng(), BTOK[:, half * 4:(half + 1) * 4, :], src, [('pb', b)], ['btok'])
                        else:
                            self.copy(self.evac_eng(), XTOK[:, half * 4:(half + 1) * 4, blk * 128:(blk + 1) * 128], src,
                                      [('pb', b)], ['xtok'])
                S.barrier()
            if seg == NSEG - 1:
                with nc.allow_non_contiguous_dma(reason="tiny conv-state scatter"):
                    for k in range(3):
                        S.dma('sp', self.p_sconv[k].rearrange("(b p) -> p b", p=128), self.XH[:, :, k], reads=['XH'])
            if self.dbg.get('ssd_stop') == 2:
                S.barrier()
                return
            with ExitStack() as es1:
                R = self.sb(es1, "ssd_r", [128, 16, 128])
                CB = self.sb(es1, "ssd_cb", [128, 16, 128])
                ECB = self.sb(es1, "ssd_ecb", [128, 16, 128])
                CPG = [self.sb(es1, "ssd_cp%d" % g, [128, 8, 128], BF16) for g in range(2)]
                for g in range(2):
                    S.op('pool', lambda: nc.gpsimd.memset(CPG[g][:].rearrange("p h t -> p (h t)"), 0.0), [], [('ssd_cp', g)])
                XW = self.sb(es1, "ssd_xw", [128, D], BF16)
                XDT = self.sb(es1, "ssd_xdt", [128, D], BF16)
                NCUM = self.sb(es1, "ssd_ncum", [128, 16])
                WE = self.sb(es1, "ssd_we", [128, 16])
                LL = [self.sb(es1, "ssd_l%d" % i, [128, 128]) for i in range(3)]
                WT = [self.sb(es1, "ssd_wt%d" % i, [128, 128], BF16) for i in range(3)]
                for tb in range(NTB):
                    cs = slice(tb * 128, (tb + 1) * 128)
                    b = self.bank()
                    self.mm(self.PB[b][:, 0:16], [('pb', b)], [(self.TRI[:], DTA[:, tb, :])], ['TRI', 'dta'])
                    self.copy('act', CUMT[:, tb, :], self.PB[b][:, 0:16], [('pb', b)], [('cumt', tb)])
                    if self.dbg.get('ssd_stop') == 3:
                        S.barrier()
                        return
                    S.op('dve', lambda: nc.vector.tensor_tensor(
                        out=R[:], in0=self.TRI[:].unsqueeze(1).to_broadcast([128, 16, 128]),
                        in1=DTA[:, tb, :].unsqueeze(2).to_broadcast([128, 16, 128]), op=ALU.mult),
                        ['TRI', 'dta'], ['ssd_r'])
                    for hg in range(4):
                        b = self.bank()
                        self.mm(self.PB[b][:, :], [('pb', b)],
                                [(self.ONESF[:], R[:, hg * 4:(hg + 1) * 4, :].rearrange("p h t -> p (h t)"))],
                                ['ONESF', 'ssd_r'])
                        self.copy('act', CB[:, hg * 4:(hg + 1) * 4, :].rearrange("p h t -> p (h t)"), self.PB[b][:, :],
                                  [('pb', b)], ['ssd_cb'])
                    S.op('act', lambda: nc.scalar.activation(out=ECB[:].rearrange("p h t -> p (h t)"),
                                                             in_=CB[:].rearrange("p h t -> p (h t)"), func=AF.Exp),
                         ['ssd_cb'], ['ssd_ecb'])
                    if self.dbg.get('ssd_stop') == 4:
                        S.barrier()
                        return
                    for g in range(2):
                        ps = slice(g * 64, (g + 1) * 64)
                        S.op('dve', lambda: nc.vector.tensor_tensor(
                            out=CPG[g][ps, :, :], in0=CFM[ps, cs].unsqueeze(1).to_broadcast([64, 8, 128]),
                            in1=ECB[ps, g * 8:(g + 1) * 8, :], op=ALU.mult), ['cfm', 'ssd_ecb'], [('ssd_cp', g)])
                    S.op('dve', lambda: nc.vector.tensor_tensor(out=WE[:], in0=CB[:, :, 127], in1=CUMT[:, tb, :],
                                                                op=ALU.subtract), ['ssd_cb', ('cumt', tb)], ['ssd_we'])
                    S.op('act', lambda: nc.scalar.activation(out=WE[:], in_=WE[:], func=AF.Exp), ['ssd_we'], ['ssd_we'])
                    S.op('dve', lambda: nc.vector.tensor_tensor(
                        out=XDT[:].rearrange("p (h q) -> p h q", q=64),
                        in0=XTOK[:, tb, :].rearrange("p (h q) -> p h q", q=64),
                        in1=DT[:, tb, :].unsqueeze(2).to_broadcast([128, 16, 64]), op=ALU.mult),
                        ['xtok', 'dt'], ['ssd_xdt'])
                    S.op('dve', lambda: nc.vector.tensor_tensor(
                        out=XW[:].rearrange("p (h q) -> p h q", q=64),
                        in0=XDT[:].rearrange("p (h q) -> p h q", q=64),
                        in1=WE[:].unsqueeze(2).to_broadcast([128, 16, 64]), op=ALU.mult),
                        ['ssd_xdt', 'ssd_we'], ['ssd_xw'])
                    S.op('dve', lambda: nc.vector.tensor_tensor(out=CB[:], in0=CB[:], in1=self.NEG[:].unsqueeze(1).to_broadcast([128, 16, 128]),
                                                                op=ALU.add), ['ssd_cb', 'NEG', 'ssd_ecb', 'ssd_we'], ['ssd_cb'])
                    S.op('dve', lambda: nc.vector.tensor_scalar(out=NCUM[:], in0=CUMT[:, tb, :], scalar1=-1.0, scalar2=None, op0=ALU.mult),
                         [('cumt', tb)], ['ssd_ncum'])
                    if self.dbg.get('ssd_stop') == 5:
                        S.barrier()
                        return
                    bcb = self.bank()
                    for g in range(2):
                        ps = slice(g * 64, (g + 1) * 64)
                        self.mm(self.PB[bcb][:, g * 128:(g + 1) * 128], [('pb', bcb)],
                                [(BFMG[g][:, cs], CFM[:, cs])], [('bfmg', g), 'cfm'])
                    if self.dbg.get('ssd_stop') == 51:
                        S.barrier()
                        return
                    for q in range(4):
                        by = self.bank()
                        for hh in range(4):
                            h = q * 4 + hh
                            g = h // 8
                            i2 = h % 3
                            ll, wt = LL[i2], WT[i2]
                            S.op('act', lambda: nc.scalar.activation(out=ll[:], in_=CB[:, h, :], func=AF.Exp,
                                                                     bias=NCUM[:, h:h + 1], scale=1.0),
                                 ['ssd_cb', 'ssd_ncum'], [('ll', i2)])
                            S.op('dve', lambda: nc.vector.tensor_tensor(
                                out=wt[:], in0=ll[:], in1=self.PB[bcb][:, g * 128:(g + 1) * 128], op=ALU.mult),
                                [('ll', i2), ('pb', bcb)], [('wt', i2)])
                            pr = h // 2
                            gs = slice(g * 64, (g + 1) * 64)
                            hp = (pr % 4) * 128
                            if self.dbg.get('ssd_stop') == 52:
                                continue
                            prs = [(XDT[:, pr * 128:(pr + 1) * 128], wt[:]),
                                   (XTOK[:, tb, pr * 128:(pr + 1) * 128], self.DIH[:, h, :]),
                                   (self.HSB[:, hp:hp + 128], CPG[g][:, h % 8, :])]
                            if self.dbg.get('ssd_stop') == 53:
                                prs = prs[:1]
                            self.mm(self.PB[by][:, hh * 128:(hh + 1) * 128], [('pb', by)], prs,
                                    ['xtok', 'ssd_xdt', 'DIH', ('wt', i2), 'HSB', ('ssd_cp', g)])
                        if self.dbg.get('ssd_stop') in (52, 54):
                            continue
                        v = self.PB[by][:, :].rearrange("p (h t) -> p h t", t=128)
                        for par in range(2):
                            ps = slice(par * 64, (par + 1) * 64)
                            self.copy(self.evac_eng(), MIX[ps, 2 * q:2 * q + 2, cs], v[ps, par::2, :],
                                      [('pb', by)], [('mix', 2 * q, tb // 4), ('mix', 2 * q + 1, tb // 4)])
                    if self.dbg.get('ssd_stop') == 6:
                        S.barrier()
                        return
                    if self.dbg.get('ssd_stop') in (52, 53, 54):
                        continue
                    for g in range(2):
                        ps = slice(g * 64, (g + 1) * 64)
                        b = self.bank()
                        self.mm(self.PB[b][:, :], [('pb', b)], [(BTOK[:, tb, :], XW[:, g * 512:(g + 1) * 512])],
                                ['btok', 'ssd_xw'])
                        S.op('dve', lambda: nc.vector.tensor_tensor(
                            out=self.HS[ps, :].rearrange("p (h q) -> p h q", q=64),
                            in0=self.HS[ps, :].rearrange("p (h q) -> p h q", q=64),
                            in1=ECB[ps, g * 8:(g + 1) * 8, 127:128].to_broadcast([64, 8, 64]), op=ALU.mult),
                            ['HS', 'ssd_ecb', 'HSB'], ['HS'])
                        S.op('dve', lambda: nc.vector.tensor_tensor(out=self.HS[ps, :], in0=self.HS[ps, :],
                                                                    in1=self.PB[b][ps, :], op=ALU.add),
                             ['HS', ('pb', b)], ['HS'])
                        self.copy('act', self.HSB[ps, :], self.HS[ps, :], ['HS'], ['HSB'])
                S.barrier()
            if seg == NSEG - 1:
                self.ssd_final_state()
            S.barrier()

    def ssd_final_state(self):
        nc, S = self.nc, self.S
        with ExitStack() as es:
            ot = self.sb(es, "ssd_fs", [128, 8, 64])
            b = self.bank()
            for j in range(4):
                self.transpose(self.PB[b][:, j * 128:(j + 1) * 128], [('pb', b)],
                               self.HS[:, j * 128:(j + 1) * 128], self.ident[:], ['HS', 'ident'])
            self.copy('dve', ot[:].rearrange("p (g j) n -> p j g n", g=2),
                      self.PB[b][:, :].rearrange("p (j g n) -> p j g n", g=2, n=64), [('pb', b)], ['ssd_fs'])
            S.dma('sp', self.p_ssm.rearrange("(j a) p n -> (a p) j n", a=2), ot[:], reads=['ssd_fs'])
            S.barrier()

    def ssd_sample_proj(self, blk, wv, wk, HN):
        pass


class ProgAB2(ProgAB):
    NH = 2

    def setup_hg(self):
        nc, S, es = self.nc, self.S, self.es
        lbin = self.inp("hgrn_lb_fm", [128, 16])
        hnw = self.inp("hgrn_nw", [128, 1])
        c_cmask = self.inp("c_cmask", [128, 4])
        t = self.sb(es, "lbtmp", [128, 2, 8])
        self.LB = self.sb(es, "LB", [128, 8])
        self.OML = self.sb(es, "OML", [128, 8])
        self.NOML = self.sb(es, "NOML", [128, 8])
        self.HNW = self.sb(es, "HNW", [128, 1])
        self.CMASK = self.sb(es, "CMASK", [128, 4])
        S.dma('sp', t[:].rearrange("p r h -> p (r h)"), lbin[:, :], writes=['lbtmp'])
        S.dma('sp', self.HNW[:], hnw[:, :], writes=['HNW'])
        S.dma('sp', self.CMASK[:], c_cmask[:, :], writes=['CMASK'])
        S.op('dve', lambda: nc.vector.tensor_tensor(out=self.LB[:], in0=t[:, 0, :], in1=t[:, 1, :], op=ALU.subtract),
             ['lbtmp'], ['LB'])
        S.op('act', lambda: nc.scalar.activation(out=self.LB[:], in_=self.LB[:], func=AF.Sigmoid), ['LB'], ['LB'])
        S.op('dve', lambda: nc.vector.tensor_scalar(out=self.OML[:], in0=self.LB[:], scalar1=-1.0, scalar2=1.0,
                                                    op0=ALU.mult, op1=ALU.add), ['LB'], ['OML'])
        S.op('dve', lambda: nc.vector.tensor_scalar(out=self.NOML[:], in0=self.OML[:], scalar1=-1.0, scalar2=None,
                                                    op0=ALU.mult), ['OML'], ['NOML'])
        self.HGS = self.sb(es, "HGS", [128, 8, 128])
        self.HGSB = self.sb(es, "HGSB", [128, 8, 128], BF16)
        S.op('dve', lambda: nc.vector.memset(self.HGS[:].rearrange("p h v -> p (h v)"), 0.0), [], ['HGS'])
        S.op('dve', lambda: nc.vector.memset(self.HGSB[:].rearrange("p h v -> p (h v)"), 0.0), [], ['HGSB'])
        self.p_hg = self.outp("p_hg", [8, 128, 128])

    def hgrn_phase(self, seg, HN, MIX):
        nc, S = self.nc, self.S
        W = self.ab_in_w
        NH = self.NH
        NCH = SEG // 32
        with ExitStack() as es:
            T0 = self.sb(es, "hg_t0", [128, SEG])
            T1 = self.sb(es, "hg_t1", [128, SEG])
            T2 = self.sb(es, "hg_t2", [128, SEG])
            KH = self.sb(es, "hg_kh", [128, SEG], BF16)
            EM = self.sb(es, "hg_em", [128, NCH])
            RS = self.sb(es, "hg_rs", [128, SEG])
            SQ = self.sb(es, "hg_sq", [128, SEG], BF16)
            QT = [self.sb(es, "hg_qt%d" % j, [128, SEG], BF16) for j in range(NH)]
            KT = [self.sb(es, "hg_kt%d" % j, [128, SEG], BF16) for j in range(NH)]
            QH = [self.sb(es, "hg_qh%d" % j, [128, SEG], BF16) for j in range(NH)]
            KM = [self.sb(es, "hg_km%d" % j, [128, NTB, 4, 128], BF16) for j in range(NH)]
            ITOK = [self.sb(es, "hg_it%d" % j, [128, NTB, 128], BF16) for j in range(NH)]
            SG = [self.sb(es, "hg_sg%d" % j, [128, SEG], BF16) for j in range(NH)]
            OSB = [self.sb(es, "hg_o%d" % j, [128, SEG]) for j in range(NH)]
            DL = [self.sb(es, "hg_dl%d" % j, [128, NCH]) for j in range(NH)]
            AM = [self.sb(es, "hg_am%d" % j, [128, 128], BF16) for j in range(NH)]
            SPP = [[self.sb(es, "hg_spp%d_%d" % (j, i), [128, 128]) for i in range(3)] for j in range(NH)]
            SNAP = [[self.sb(es, "hg_snap%d_%d" % (i, j), [128, 4, 128], BF16) for j in range(NH)] for i in range(2)]
            ws_h = wstream(self, [(W[:, off + h * 128:off + (h + 1) * 128], 8)
                                  for h in range(8) for off in (OFF_Q, OFF_F, OFF_G, OFF_I, OFF_Z)])
            SZ = [self.sb(es, "sz%d" % i, [128, 512]) for i in range(2)]
            for hg in range(8 // NH):
                for j in range(NH):
                    h = hg * NH + j
                    for which, off in (('q', OFF_Q), ('f', OFF_F), ('g', OFF_G)):
                        wv, wk = next(ws_h)
                        for ti in range(2):
                            c0 = ti * 512
                            b = self.bank()
                            self.mm(self.PB[b][:, :], [('pb', b)],
                                    [(wv[:, kc, :], HN[:, kc, c0:c0 + 512]) for kc in range(8)],
                                    [wk] + tkeys('hn', range(8), ti))
                            if which == 'q':
                                S.op('act', lambda: nc.scalar.activation(out=QT[j][:, c0:c0 + 512], in_=self.PB[b][:, :],
                                                                         func=AF.Silu), [('pb', b)], [('qt', j)])
                            elif which == 'f':
                                S.op('act', lambda: nc.scalar.activation(out=T0[:, c0:c0 + 512], in_=self.PB[b][:, :],
                                                                         func=AF.Sigmoid), [('pb', b)], ['t0'])
                            else:
                                S.op('act', lambda: nc.scalar.activation(out=SG[j][:, c0:c0 + 512], in_=self.PB[b][:, :],
                                                                         func=AF.Silu), [('pb', b)], [('sg', j)])
                        if seg == 0:
                            self.hg_sample_proj(which, h, wv, wk, HN)
                    wv, wk = next(ws_h)
                    for half in range(2):
                        b = self.bank()
                        for jj in range(4):
                            tb = half * 4 + jj
                            self.mm(self.PB[b][:, jj * 128:(jj + 1) * 128], [('pb', b)],
                                    [(HN[:, kc, tb * 128:(tb + 1) * 128], wv[:, kc, :]) for kc in range(8)],
                                    [wk] + tkeys('hn', range(8), tb // 4))
                        self.copy(self.evac_eng(), ITOK[j][:, half * 4:(half + 1) * 4, :],
                                  self.PB[b][:, :].rearrange("p (a v) -> p a v", v=128), [('pb', b)], [('itok', j)])
                    if seg == 0:
                        self.hg_sample_proj('i', h, wv, wk, HN)
                    S.op('dve', lambda: nc.vector.tensor_scalar(out=T1[:], in0=T0[:], scalar1=self.NOML[:, h:h + 1],
                                                                scalar2=self.OML[:, h:h + 1], op0=ALU.mult, op1=ALU.add),
                         ['t0', 'NOML', 'OML'], ['t1'])
                    S.op('dve', lambda: nc.vector.tensor_scalar(out=T0[:], in0=T0[:], scalar1=self.OML[:, h:h + 1],
                                                                scalar2=self.LB[:, h:h + 1], op0=ALU.mult, op1=ALU.add),
                         ['t0', 'LB', 'OML'], ['t0'])
                    S.op('act', lambda: nc.scalar.activation(out=T0[:], in_=T0[:], func=AF.Ln), ['t0'], ['t0'])
                    S.op('dve', lambda: nc.vector.tensor_tensor_scan(out=T2[:], data0=self.RST[:], data1=T0[:], initial=0.0,
                                                                     op0=ALU.mult, op1=ALU.add), ['t0', 'RST'], ['t2'])
                    c3 = T2[:].rearrange("p (c t) -> p c t", t=32)
                    S.op('act', lambda: nc.scalar.activation(out=DL[j][:], in_=c3[:, :, 31], func=AF.Exp), ['t2'], [('dl', j)])
                    S.op('act', lambda: nc.scalar.activation(out=EM[:], in_=c3[:, :, 15], func=AF.Exp), ['t2'], ['em'])
                    S.op('dve', lambda: nc.vector.tensor_tensor(out=c3, in0=c3, in1=c3[:, :, 15:16].to_broadcast([128, NCH, 32]),
                                                                op=ALU.subtract), ['t2'], ['t2'])
                    S.op('act', lambda: nc.scalar.activation(out=T0[:], in_=T2[:], func=AF.Exp), ['t2', 't0'], ['t0'])
                    S.op('act', lambda: nc.scalar.activation(out=T2[:], in_=T2[:], func=AF.Exp, scale=-1.0), ['t2'], ['t2'])
                    S.op('dve', lambda: nc.vector.tensor_tensor(out=QT[j][:], in0=QT[j][:], in1=T0[:], op=ALU.mult),
                         [('qt', j), 't0'], [('qt', j)])
                    S.op('dve', lambda: nc.vector.tensor_tensor(out=KT[j][:], in0=T1[:], in1=T2[:], op=ALU.mult),
                         ['t1', 't2'], [('kt', j)])
                    S.op('pool', lambda: nc.gpsimd.tensor_tensor(
                        out=QH[j][:].rearrange("p (c t) -> p c t", t=32), in0=QT[j][:].rearrange("p (c t) -> p c t", t=32),
                        in1=EM[:].unsqueeze(2).to_broadcast([128, NCH, 32]), op=ALU.mult), [('qt', j), 'em'], [('qh', j)])
                    e3 = T0[:].rearrange("p (c t) -> p c t", t=32)
                    S.op('dve', lambda: nc.vector.tensor_tensor(
                        out=KH[:].rearrange("p (c t) -> p c t", t=32), in0=KT[j][:].rearrange("p (c t) -> p c t", t=32),
                        in1=e3[:, :, 31:32].to_broadcast([128, NCH, 32]), op=ALU.mult), [('kt', j), 't0'], ['kh'])
                    wv, wk = next(ws_h)
                    self.zgate_block(seg, h, wv, wk, HN, MIX, SZ)
                    for half in range(2):
                        b = self.bank()
                        pbf = self.bfbank(b)
                        for jj in range(4):
                            tb = half * 4 + jj
                            self.transpose(pbf[:, jj * 128:(jj + 1) * 128], [('pb', b)],
                                           KH[:, tb * 128:(tb + 1) * 128], self.identb[:], ['kh', 'identb'])
                        for ci in range(4):
                            S.op('act', lambda: nc.scalar.activation(
                                out=KM[j][:, half * 4:(half + 1) * 4, ci, :],
                                in_=pbf[:, 0:512].rearrange("p (a k) -> p a k", k=128), func=AF.Identity,
                                scale=self.CMASK[:, ci:ci + 1]), [('pb', b), 'CMASK'], [('km', j)])
                B_ATT = 0
                B_U = ((1, 2), (3, 4))
                B_IO = (5, 6)

                def emit_front(tb):
                    cs = slice(tb * 128, (tb + 1) * 128)
                    bint = B_IO[tb % 2]
                    for j in range(NH):
                        self.mm(self.PB[B_ATT][:, j * 128:(j + 1) * 128], [('pb', B_ATT)], [(KT[j][:, cs], QT[j][:, cs])],
                                [('kt', j), ('qt', j)])
                        S.op('dve', lambda: nc.vector.tensor_tensor(
                            out=AM[j][:], in0=self.PB[B_ATT][:, j * 128:(j + 1) * 128], in1=self.BMASK[:], op=ALU.mult),
                            [('pb', B_ATT), 'BMASK'], [('am', j)])
                        self.mm(self.PB[bint][:, j * 128:(j + 1) * 128], [('pb', bint)], [(ITOK[j][:, tb, :], AM[j][:])],
                                [('itok', j), ('am', j)])
                    for ci in range(4):
                        for j in range(NH):
                            k = ci * NH + j
                            bu = B_U[tb % 2][k // 4]
                            self.mm(self.PB[bu][:, (k % 4) * 128:(k % 4 + 1) * 128], [('pb', bu)],
                                    [(KM[j][:, tb, ci, :], ITOK[j][:, tb, :])], [('km', j), ('itok', j)])

                def emit_chain(tb):
                    for ci in range(4):
                        ch = tb * 4 + ci
                        for j in range(NH):
                            h = hg * NH + j
                            k = ci * NH + j
                            bu = B_U[tb % 2][k // 4]
                            ic, inx = ch % 3, (ch + 1) % 3
                            cur, nxt = SPP[j][ic], SPP[j][inx]
                            self.copy('act', SNAP[tb % 2][j][:, ci, :], cur[:], [('spp', j, ic)], [('snap', tb % 2, j)])
                            S.op('dve', lambda: nc.vector.scalar_tensor_tensor(
                                out=nxt[:], in0=cur[:], scalar=DL[j][:, ch:ch + 1],
                                in1=self.PB[bu][:, (k % 4) * 128:(k % 4 + 1) * 128], op0=ALU.mult, op1=ALU.add),
                                [('spp', j, ic), ('dl', j), ('pb', bu)], [('spp', j, inx)])

                def emit_back(tb):
                    cs = slice(tb * 128, (tb + 1) * 128)
                    bint = bin_ = B_IO[tb % 2]
                    for ci in range(4):
                        for j in range(NH):
                            c0 = tb * 128 + ci * 32
                            self.mm(self.PB[bin_][:, 256 + j * 128 + ci * 32:256 + j * 128 + ci * 32 + 32], [('pb', bin_)],
                                    [(SNAP[tb % 2][j][:, ci, :], QH[j][:, c0:c0 + 32])], [('snap', tb % 2, j), ('qh', j)])
                    for j in range(NH):
                        self.copy('act', OSB[j][:, cs], self.PB[bint][:, j * 128:(j + 1) * 128], [('pb', bint)], [('osb', j)])
                        S.op('dve', lambda: nc.vector.tensor_tensor(out=OSB[j][:, cs], in0=OSB[j][:, cs],
                                                                    in1=self.PB[bin_][:, 256 + j * 128:256 + (j + 1) * 128], op=ALU.add),
                             [('osb', j), ('pb', bin_)], [('osb', j)])

                for j in range(NH):
                    self.copy('dve', SPP[j][0][:], self.HGS[:, hg * NH + j, :], [('HGS', hg * NH + j)], [('spp', j, 0)])
                emit_front(0)
                for tb in range(NTB):
                    if tb >= 1:
                        emit_back(tb - 1)
                    if tb + 1 < NTB:
                        emit_front(tb + 1)
                    emit_chain(tb)
                emit_back(NTB - 1)
                fin = (NTB * 4) % 3
                for j in range(NH):
                    self.copy('dve', self.HGS[:, hg * NH + j, :], SPP[j][fin][:], [('spp', j, fin)], [('HGS', hg * NH + j)])
                for j in range(NH):
                    h = hg * NH + j
                    S.op('act', lambda: nc.scalar.activation(out=SQ[:], in_=OSB[j][:], func=AF.Square), [('osb', j)], ['hsq'])
                    for ti in range(2):
                        c0 = ti * 512
                        b = self.bank()
                        self.mm(self.PB[b][:, :], [('pb', b)], [(self.onesb[:], SQ[:, c0:c0 + 512])], ['hsq', 'onesb'])
                        S.op('act', lambda: nc.scalar.activation(out=RS[:, c0:c0 + 512], in_=self.PB[b][:, :], func=AF.Sqrt,
                                                                 bias=self.epsc[:, 0:1], scale=1.0 / 128.0),
                             [('pb', b), 'epsc'], ['hrs'])
                    S.op('dve', lambda: nc.vector.reciprocal(out=RS[:], in_=RS[:]), ['hrs'], ['hrs'])
                    S.op('dve', lambda: nc.vector.scalar_tensor_tensor(out=OSB[j][:], in0=OSB[j][:], scalar=self.HNW[:, 0:1],
                                                                       in1=RS[:], op0=ALU.mult, op1=ALU.mult),
                         [('osb', j), 'hrs', 'HNW'], [('osb', j)])
                    S.op('dve', lambda: nc.vector.tensor_tensor(out=MIX[:, 8 + h, 0:SEG], in0=OSB[j][:], in1=SG[j][:], op=ALU.mult),
                         [('osb', j), ('sg', j)], [('mix', 8 + h, 0), ('mix', 8 + h, 1)])
            if seg == NSEG - 1:
                for h in range(8):
                    S.dma('sp', self.p_hg[h], self.HGS[:, h, :], reads=[('HGS', h)])
            S.barrier()

    def hg_sample_proj(self, which, h, wv, wk, HN):
        pass


class ProgAB3(ProgAB2):
    def setup_ab_sample(self):
        nc = self.nc
        def scr(name, shape):
            return nc.dram_tensor(name, list(shape), F32, kind="Internal").ap()
        self.SC = {k: scr("sc_" + k, [NS, D]) for k in ('z', 'q', 'f', 'i', 'g', 'y', 'o')}
        self.SC['xbc'] = scr("sc_xbc", [NS, 1280])
        self.SC['xa'] = scr("sc_xa", [NS, D])
        self.SC['bc'] = scr("sc_bc", [NS, 256])
        self.SC['dt'] = scr("sc_dt", [NS, 16])
        self.st_sconv = self.inp("st_sconv", [NS, 3, 1280])
        self.st_ssm = self.inp("st_ssm", [NS * 16, 4096])
        self.st_hg = self.inp("st_hg", [NS * 8, 128 * 128])
        self.s_sconv = self.outp("s_sconv", [NS, 3, 1280])
        self.s_ssm = self.outp("s_ssm", [NS * 16, 4096])
        self.s_hg = self.outp("s_hg", [NS * 8, 128 * 128])
        self.ssm_conv_wb = self.inp("ssm_conv_wb", [5, 1280])
        self.v16bh = self.inp("v16bh", [128, 3])
        self.snw_row = self.inp("snw_row", [1, D])
        self.hnw_row = self.inp("hnw_row", [1, 128])
        self.hgrn_lb_bh = self.inp("hgrn_lb_bh", [2, 128, 128])
        self.stg = [self.sb(self.es, "stg%d" % i, [NS, 128]) for i in range(4)]
        self.stg_rr = 0

    def stage_tok(self, b, name, col0, ncols=128, func=None):
        nc, S = self.nc, self.S
        i = self.stg_rr
        self.stg_rr = (self.stg_rr + 1) % len(self.stg)
        t = self.stg[i]
        self.copy('dve', t[:, 0:ncols], self.PB[b][0:NS, 0:ncols], [('pb', b)], [('stg', i)])
        S.dma('sp', self.SC[name][:, col0:col0 + ncols], t[:, 0:ncols], reads=[('stg', i)], writes=[('sc', name)])

    def sample_proj(self, name, col0, wv, wk, HN, ncols=128):
        b = self.bank()
        self.mm(self.PB[b][0:NS, 0:ncols], [('pb', b)],
                [(HN[:, kc, SEG:SEG + NS], wv[:, kc, :]) for kc in range(8)], [wk] + tkeys('hn', range(8), 2))
        self.stage_tok(b, name, col0, ncols)

    def ssd_sample_proj(self, blk, wv, wk, HN):
        self.sample_proj('xbc', blk * 128, wv, wk, HN)

    def hg_sample_proj(self, which, h, wv, wk, HN):
        self.sample_proj(which, h * 128, wv, wk, HN)

    def zgate_block(self, seg, c, wv, wk, HN, MIX, SZ):
        nc, S = self.nc, self.S
        for ti in range(2):
            c0 = ti * 512
            b = self.bank()
            self.mm(self.PB[b][:, :], [('pb', b)], [(wv[:, kc, :], HN[:, kc, c0:c0 + 512]) for kc in range(8)],
                    [wk] + tkeys('hn', range(8), ti))
            sz = SZ[ti]
            S.op('act', lambda: nc.scalar.activation(out=sz[:], in_=self.PB[b][:, :], func=AF.Silu),
                 [('pb', b)], [('sz', ti)])
            S.op('dve', lambda: nc.vector.tensor_tensor(out=MIX[:, c, c0:c0 + 512], in0=MIX[:, c, c0:c0 + 512],
                                                        in1=sz[:], op=ALU.mult),
                 [('sz', ti), ('mix', c, ti)], [('mix', c, ti)])
        if seg == 0:
            self.sample_proj('z', c * 128, wv, wk, HN)

    def zgate_phase(self, seg, HN, MIX):
        nc, S = self.nc, self.S
        W = self.ab_in_w
        with ExitStack() as es:
            SQ = [self.sb(es, "zsq%d" % i, [128, 8, 512], BF16) for i in range(2)]
            RS = [self.sb(es, "zrs%d" % i, [128, 512]) for i in range(2)]
            for ti in range(2):
                c0 = ti * 512
                sq, rs = SQ[ti], RS[ti]
                S.op('act', lambda: nc.scalar.activation(out=sq[:], in_=MIX[:, 0:8, c0:c0 + 512], func=AF.Square),
                     tkeys('mix', range(8), ti), [('zsq', ti)])
                b = self.bank()
                self.mm(self.PB[b][:, :], [('pb', b)], [(self.onesb[:], sq[:, c, :]) for c in range(8)], [('zsq', ti), 'onesb'])
                S.op('act', lambda: nc.scalar.activation(out=rs[:], in_=self.PB[b][:, :], func=AF.Sqrt,
                                                         bias=self.epsc[:, 0:1], scale=1.0 / 1024.0), [('pb', b), 'epsc'], [('zrs', ti)])
                S.op('dve', lambda: nc.vector.reciprocal(out=rs[:], in_=rs[:]), [('zrs', ti)], [('zrs', ti)])
                for c in range(8):
                    S.op('dve', lambda: nc.vector.scalar_tensor_tensor(
                        out=MIX[:, c, c0:c0 + 512], in0=MIX[:, c, c0:c0 + 512], scalar=self.SNW[:, c:c + 1], in1=rs[:],
                        op0=ALU.mult, op1=ALU.mult), [('mix', c, ti), ('zrs', ti), 'NW'], [('mix', c, ti)])
            S.barrier()

    def ab_sample_phase(self, MIX):
        nc, S = self.nc, self.S
        SC = self.SC
        tt = nc.vector.tensor_tensor
        with ExitStack() as es:
            XB = self.sb(es, "s_xb", [NS, 1280])
            CS = self.sb(es, "s_cs", [NS, 3, 1280])
            CW = self.sb(es, "s_cw", [NS, 5, 1280])
            AC = self.sb(es, "s_ac", [NS, 1280])
            TM = self.sb(es, "s_tm", [NS, 1280])
            S.dma('sp', XB[:], SC['xbc'][:, :], reads=[('sc', 'xbc')], writes=['s_xb'])
            S.dma('sp', CS[:], self.st_sconv[:, :, :], writes=['s_cs'])
            for r in range(5):
                S.dma('sp', CW[:, r, :], self.ssm_conv_wb[r:r + 1, :].partition_broadcast(NS), writes=['s_cw'])
            S.dma('sp', self.s_sconv[:, 0:2, :], CS[:, 1:3, :], reads=['s_cs'])
            S.dma('sp', self.s_sconv[:, 2, :], XB[:], reads=['s_xb'])
            S.op('dve', lambda: tt(out=AC[:], in0=XB[:], in1=CW[:, 3, :], op=ALU.mult), ['s_xb', 's_cw'], ['s_ac'])
            S.op('dve', lambda: tt(out=AC[:], in0=AC[:], in1=CW[:, 4, :], op=ALU.add), ['s_ac', 's_cw'], ['s_ac'])
            for k in range(3):
                S.op('dve', lambda: tt(out=TM[:], in0=CS[:, k, :], in1=CW[:, k, :], op=ALU.mult), ['s_cs', 's_cw'], ['s_tm'])
                S.op('dve', lambda: tt(out=AC[:], in0=AC[:], in1=TM[:], op=ALU.add), ['s_ac', 's_tm'], ['s_ac'])
            S.op('act', lambda: nc.scalar.activation(out=AC[:], in_=AC[:], func=AF.Silu), ['s_ac'], ['s_ac'])
            S.dma('sp', SC['xa'][:, :], AC[:, 0:D], reads=['s_ac'], writes=[('sc', 'xa')])
            S.dma('sp', SC['bc'][:, :], AC[:, D:1280], reads=['s_ac'], writes=[('sc', 'bc')])
            S.barrier()
        with ExitStack() as es:
            BA = self.sb(es, "s_bufA", [128, 4096])
            BC = self.sb(es, "s_bufC", [128, 4096])
            OP = self.sb(es, "s_op", [128, 4096])
            STB = [BA, BC]
            Q = self.sb(es, "g_q", [128, 128])
            Fg = self.sb(es, "g_f", [128, 128])
            KK = self.sb(es, "g_kk", [128, 128])
            Iv = self.sb(es, "g_i", [128, 128])
            G = self.sb(es, "g_g", [128, 128])
            L0 = self.sb(es, "g_l0", [128, 128])
            L1 = self.sb(es, "g_l1", [128, 128])
            NWr = self.sb(es, "g_nw", [128, 128])
            O = self.sb(es, "g_o", [128, 128])
            O2 = self.sb(es, "g_o2", [128, 128])
            SS = self.sb(es, "g_ss", [128, 1])
            Xh = [self.sb(es, "s_xh%d" % i, [128, 64]) for i in range(2)]
            Bh = [self.sb(es, "s_bh%d" % i, [128, 64]) for i in range(2)]
            Ch = [self.sb(es, "s_ch%d" % i, [128, 64]) for i in range(2)]
            DTh = [self.sb(es, "s_dth%d" % i, [128, 1]) for i in range(2)]
            V3 = self.sb(es, "s_v3", [128, 3])
            DEC = self.sb(es, "s_dec", [128, 1])
            DTX = self.sb(es, "s_dtx", [128, 64])
            Yh = self.sb(es, "s_yh", [128, 64])
            xa_bh = SC['xa'].rearrange("b (h p) -> (b h) p", p=64)
            dt_bh = SC['dt'].rearrange("b (h o) -> (b h) o", o=1)
            y_bh = SC['y'].rearrange("b (h p) -> (b h) p", p=64)
            S.dma('sp', V3[:], self.v16bh[:, :], writes=['s_v3'])
            st_v = self.st_ssm.rearrange("(b g r) f -> g b r f", g=2, r=8)
            so_v = self.s_ssm.rearrange("(b g r) f -> g b r f", g=2, r=8)
            xa_v = SC['xa'].rearrange("b (g r p) -> g b r p", g=2, r=8)
            dt_v = SC['dt'].rearrange("b (g r o) -> g b r o", g=2, o=1)
            y_v = SC['y'].rearrange("b (g r p) -> g b r p", g=2, r=8)
            bc_v = SC['bc'].rearrange("b (w g n) -> w g b n", w=2, g=2)
            for half in range(2):
                bs = slice(half * 8, (half + 1) * 8)
                for g in range(2):
                    gp = slice(g * 64, (g + 1) * 64)
                    S.dma('sp', STB[half][gp, :], st_v[g, bs], writes=[('stb', half)])
            for half in range(2):
                bs = slice(half * 8, (half + 1) * 8)
                for g in range(2):
                    gp = slice(g * 64, (g + 1) * 64)
                    S.dma('sp', Xh[half][gp, :], xa_v[g, bs], reads=[('sc', 'xa')], writes=[('s_xh', half)])
                    S.dma('sp', DTh[half][gp, :], dt_v[g, bs], reads=[('sc', 'dt')], writes=[('s_dth', half)])
                    for w, (dst, kd) in enumerate(((Bh[half], 's_bh'), (Ch[half], 's_ch'))):
                        src = bc_v[w, g, bs].unsqueeze(1).to_broadcast([8, 8, 64])
                        S.dma('sp', dst[gp, :], src, reads=[('sc', 'bc')], writes=[(kd, half)])
            for t_, name in ((Q, 'q'), (Fg, 'f'), (Iv, 'i'), (G, 'g')):
                S.dma('sp', t_[:], SC[name].rearrange("b (h k) -> (b h) k", k=128), reads=[('sc', name)], writes=['g_' + name])
            for t_, r in ((L0, 0), (L1, 1)):
                S.dma('sp', t_[:], self.hgrn_lb_bh[r], writes=['g_l%d' % r])
            S.dma('sp', NWr[:], self.hnw_row[0:1, :].partition_broadcast(128), writes=['g_nw'])
            for half in range(2):
                rs = slice(half * 128, (half + 1) * 128)
                ST = STB[half]
                kst = ('stb', half)
                xh, bh, ch, dth = Xh[half], Bh[half], Ch[half], DTh[half]
                kx, kb, kc_, kd_ = ('s_xh', half), ('s_bh', half), ('s_ch', half), ('s_dth', half)
                S.op('dve', lambda: tt(out=dth[:], in0=dth[:], in1=V3[:, 0:1], op=ALU.add), [kd_, 's_v3'], [kd_])
                S.op('act', lambda: nc.scalar.activation(out=dth[:], in_=dth[:], func=AF.Exp), [kd_], [kd_])
                S.op('act', lambda: nc.scalar.activation(out=dth[:], in_=dth[:], func=AF.Ln, bias=self.ONEC[:, 0:1], scale=1.0),
                     [kd_, 'ONEC'], [kd_])
                S.op('act', lambda: nc.scalar.activation(out=DEC[:], in_=V3[:, 1:2], func=AF.Exp), ['s_v3'], ['s_dec'])
                S.op('dve', lambda: tt(out=DEC[:], in0=DEC[:], in1=dth[:], op=ALU.mult), ['s_dec', kd_], ['s_dec'])
                S.op('act', lambda: nc.scalar.activation(out=DEC[:], in_=DEC[:], func=AF.Exp, scale=-1.0), ['s_dec'], ['s_dec'])
                S.op('dve', lambda: nc.vector.tensor_scalar(out=DTX[:], in0=xh[:], scalar1=dth[:, 0:1], scalar2=None, op0=ALU.mult),
                     [kx, kd_], ['s_dtx'])
                o3 = OP[:].rearrange("q (p n) -> q p n", n=64)
                s3 = ST[:].rearrange("q (p n) -> q p n", n=64)
                S.op('dve', lambda: tt(out=o3, in0=DTX[:].unsqueeze(2).to_broadcast([128, 64, 64]),
                                       in1=bh[:].unsqueeze(1).to_broadcast([128, 64, 64]), op=ALU.mult),
                     ['s_dtx', kb], ['s_op'])
                S.op('dve', lambda: nc.vector.scalar_tensor_tensor(out=ST[:], in0=ST[:], scalar=DEC[:, 0:1], in1=OP[:],
                                                                   op0=ALU.mult, op1=ALU.add), [kst, 's_dec', 's_op'], [kst])
                for g in range(2):
                    S.dma('sp', so_v[g, half * 8:(half + 1) * 8], ST[g * 64:(g + 1) * 64, :], reads=[kst])
                S.op('dve', lambda: tt(out=o3, in0=s3, in1=ch[:].unsqueeze(1).to_broadcast([128, 64, 64]), op=ALU.mult),
                     [kst, kc_], ['s_op'])
                S.dma('sp', ST[:], self.st_hg[:, half * 4096:(half + 1) * 4096], writes=[kst])
                S.op('dve', lambda: nc.vector.tensor_reduce(out=Yh[:], in_=o3, axis=AX.X, op=ALU.add), ['s_op'], ['s_yh'])
                S.op('dve', lambda: nc.vector.scalar_tensor_tensor(out=Yh[:], in0=xh[:], scalar=V3[:, 2:3], in1=Yh[:],
                                                                   op0=ALU.mult, op1=ALU.add), [kx, 's_v3', 's_yh'], ['s_yh'])
                for g in range(2):
                    S.dma('sp', y_v[g, half * 8:(half + 1) * 8], Yh[g * 64:(g + 1) * 64, :], reads=['s_yh'], writes=[('sc', 'y')])
            S.op('dve', lambda: tt(out=L0[:], in0=L0[:], in1=L1[:], op=ALU.subtract), ['g_l0', 'g_l1'], ['g_l0'])
            S.op('act', lambda: nc.scalar.activation(out=L0[:], in_=L0[:], func=AF.Sigmoid), ['g_l0'], ['g_l0'])
            S.op('dve', lambda: nc.vector.tensor_scalar(out=L1[:], in0=L0[:], scalar1=-1.0, scalar2=1.0, op0=ALU.mult, op1=ALU.add),
                 ['g_l0', 'g_l1'], ['g_l1'])
            S.op('act', lambda: nc.scalar.activation(out=Q[:], in_=Q[:], func=AF.Silu), ['g_q'], ['g_q'])
            S.op('act', lambda: nc.scalar.activation(out=G[:], in_=G[:], func=AF.Silu), ['g_g'], ['g_g'])
            S.op('act', lambda: nc.scalar.activation(out=Fg[:], in_=Fg[:], func=AF.Sigmoid), ['g_f'], ['g_f'])
            S.op('dve', lambda: nc.vector.tensor_scalar(out=KK[:], in0=Fg[:], scalar1=-1.0, scalar2=1.0, op0=ALU.mult, op1=ALU.add),
                 ['g_f'], ['g_kk'])
            S.op('dve', lambda: tt(out=KK[:], in0=KK[:], in1=L1[:], op=ALU.mult), ['g_kk', 'g_l1'], ['g_kk'])
            S.op('dve', lambda: tt(out=Fg[:], in0=Fg[:], in1=L1[:], op=ALU.mult), ['g_f', 'g_l1'], ['g_f'])
            S.op('dve', lambda: tt(out=Fg[:], in0=Fg[:], in1=L0[:], op=ALU.add), ['g_f', 'g_l0'], ['g_f'])
            for part in range(4):
                ks = slice(part * 32, (part + 1) * 32)
                cols = slice(part * 4096, (part + 1) * 4096)
                SP = STB[part % 2]
                ksp = ('stb', part % 2)
                sp3 = SP[:].rearrange("q (k v) -> q k v", v=128)
                op3 = OP[:].rearrange("q (k v) -> q k v", v=128)
                S.op('dve', lambda: tt(out=sp3, in0=sp3, in1=Fg[:, ks].unsqueeze(2).to_broadcast([128, 32, 128]), op=ALU.mult),
                     [ksp, 'g_f'], [ksp])
                S.op('pool', lambda: nc.gpsimd.tensor_tensor(out=op3, in0=KK[:, ks].unsqueeze(2).to_broadcast([128, 32, 128]),
                                                             in1=Iv[:].unsqueeze(1).to_broadcast([128, 32, 128]), op=ALU.mult),
                     ['g_kk', 'g_i'], ['s_op'])
                S.op('dve', lambda: tt(out=SP[:], in0=SP[:], in1=OP[:], op=ALU.add), [ksp, 's_op'], [ksp])
                S.dma('sp', self.s_hg[:, cols], SP[:], reads=[ksp])
                S.op('dve', lambda: tt(out=op3, in0=sp3, in1=Q[:, ks].unsqueeze(2).to_broadcast([128, 32, 128]), op=ALU.mult),
                     [ksp, 'g_q'], ['s_op'])
                if part + 2 < 4:
                    S.dma('sp', SP[:], self.st_hg[:, (part + 2) * 4096:(part + 3) * 4096], writes=[ksp])
                dst = O if part == 0 else O2
                S.op('dve', lambda: nc.vector.tensor_reduce(out=dst[:], in_=OP[:].rearrange("q (k v) -> q v k", v=128),
                                                            axis=AX.X, op=ALU.add), ['s_op'], ['g_o' if part == 0 else 'g_o2'])
                if part:
                    S.op('dve', lambda: tt(out=O[:], in0=O[:], in1=O2[:], op=ALU.add), ['g_o', 'g_o2'], ['g_o'])
            S.op('act', lambda: nc.scalar.activation(out=O2[:], in_=O[:], func=AF.Square, accum_out=SS[:, 0:1]), ['g_o'], ['g_o2', 'g_ss'])
            S.op('act', lambda: nc.scalar.activation(out=SS[:], in_=SS[:], func=AF.Sqrt, bias=self.epsc[:, 0:1], scale=1.0 / 128.0),
                 ['g_ss', 'epsc'], ['g_ss'])
            S.op('dve', lambda: nc.vector.reciprocal(out=SS[:], in_=SS[:]), ['g_ss'], ['g_ss'])
            S.op('dve', lambda: nc.vector.scalar_tensor_tensor(out=O[:], in0=O[:], scalar=SS[:, 0:1], in1=NWr[:],
                                                               op0=ALU.mult, op1=ALU.mult), ['g_o', 'g_ss', 'g_nw'], ['g_o'])
            S.op('dve', lambda: tt(out=O[:], in0=O[:], in1=G[:], op=ALU.mult), ['g_o', 'g_g'], ['g_o'])
            S.dma('sp', SC['o'].rearrange("b (h k) -> (b h) k", k=128), O[:], reads=['g_o'], writes=[('sc', 'o')])
            S.barrier()
        with ExitStack() as es:
            MS = self.sb(es, "m_ms", [NS, 2 * D])
            Z = self.sb(es, "m_z", [NS, D])
            NWt = self.sb(es, "m_nw", [NS, D])
            SQt = self.sb(es, "m_sq", [NS, D])
            SS = self.sb(es, "m_ss", [NS, 1])
            S.dma('sp', MS[:, 0:D], SC['y'][:, :], reads=[('sc', 'y')], writes=['m_ms'])
            S.dma('sp', MS[:, D:2 * D], SC['o'][:, :], reads=[('sc', 'o')], writes=['m_ms'])
            S.dma('sp', Z[:], SC['z'][:, :], reads=[('sc', 'z')], writes=['m_z'])
            S.dma('sp', NWt[:], self.snw_row[0:1, :].partition_broadcast(NS), writes=['m_nw'])
            S.op('act', lambda: nc.scalar.activation(out=Z[:], in_=Z[:], func=AF.Silu), ['m_z'], ['m_z'])
            S.op('dve', lambda: tt(out=MS[:, 0:D], in0=MS[:, 0:D], in1=Z[:], op=ALU.mult), ['m_ms', 'm_z'], ['m_ms'])
            S.op('act', lambda: nc.scalar.activation(out=SQt[:], in_=MS[:, 0:D], func=AF.Square, accum_out=SS[:, 0:1]),
                 ['m_ms'], ['m_sq', 'm_ss'])
            S.op('act', lambda: nc.scalar.activation(out=SS[:], in_=SS[:], func=AF.Sqrt, bias=self.epsc[0:NS, 0:1], scale=1.0 / D),
                 ['m_ss', 'epsc'], ['m_ss'])
            S.op('dve', lambda: nc.vector.reciprocal(out=SS[:], in_=SS[:]), ['m_ss'], ['m_ss'])
            S.op('dve', lambda: nc.vector.scalar_tensor_tensor(out=MS[:, 0:D], in0=MS[:, 0:D], scalar=SS[:, 0:1], in1=NWt[:],
                                                               op0=ALU.mult, op1=ALU.mult), ['m_ms', 'm_ss', 'm_nw'], ['m_ms'])
            for half in range(2):
                b = self.bank()
                for j in range(8):
                    c = half * 8 + j
                    self.transpose(self.PB[b][:, j * NS:(j + 1) * NS], [('pb', b)], MS[:, c * 128:(c + 1) * 128],
                                   self.ident[0:NS, 0:NS], ['m_ms', 'ident'])
                self.copy('dve', MIX[:, half * 8:(half + 1) * 8, SEG:SEG + NS],
                          self.PB[b][:, 0:8 * NS].rearrange("p (c t) -> p c t", t=NS), [('pb', b)],
                          tkeys('mix', range(half * 8, half * 8 + 8), 2))
            S.barrier()

    def mixer_ab(self, seg):
        nc, S = self.nc, self.S
        ncol = seg_cols(seg)
        with ExitStack() as es0:
            MIX = self.sb(es0, "mix", [128, 16, ncol], BF16)
            HN = self.sb(es0, "hn", [128, 8, ncol], BF16)
            with ExitStack() as es2:
                self.rmsnorm_fm(es2, seg, self.XR, 'xr', self.NW[:, 0, 0, :], HN, 'hn')
                S.barrier()
            self.ssd_phase(seg, HN, MIX)
            self.hgrn_phase(seg, HN, MIX)
            self.zgate_phase(seg, HN, MIX)
            if seg == 0:
                self.ab_sample_phase(MIX)
            with ExitStack() as es:
                Mo = self.sb(es, "mo", [128, 8, ncol])
                ws_o = wstream(self, [(self.ab_out_w[:, ob * 128:(ob + 1) * 128], 16) for ob in range(8)])
                for ob in range(8):
                    wv, wk = next(ws_o)
                    for ti, (c0, n) in enumerate(ttiles(seg)):
                        b = self.bank()
                        self.mm(self.PB[b][:, 0:n], [('pb', b)], [(wv[:, kc, :], MIX[:, kc, c0:c0 + n]) for kc in range(16)],
                                [wk] + tkeys('mix', range(16), ti))
                        self.copy(self.evac_eng(), Mo[:, ob, c0:c0 + n], self.PB[b][:, 0:n], [('pb', b)], [('mo', ob, ti)])
                with ExitStack() as es2:
                    self.rmsnorm_fm(es2, seg, Mo, 'mo', self.NW[:, 1, 0, :], self.XR, 'xr', residual=True)
                    S.barrier()
                S.barrier()
            S.barrier()


L8 = 8
NJ = SEG // L8
TWO_PI = 2.0 * np.pi


class ProgC(ProgAB3):
    def s5_trig(self, arg, arg_keys, sn, cs, np_, U, KI):
        nc, S = self.nc, self.S
        for t, key, shift in ((sn, 'trg_s', 0.0), (cs, 'trg_c', 0.5 * np.pi)):
            S.op('dve', lambda: nc.vector.tensor_scalar(out=U, in0=arg, scalar1=float(shift), scalar2=float(1.0 / TWO_PI),
                                                        op0=ALU.add, op1=ALU.mult), arg_keys, ['trg_u'])
            S.op('dve', lambda: nc.vector.tensor_copy(out=KI, in_=U), ['trg_u'], ['trg_k'])
            S.op('dve', lambda: nc.vector.tensor_copy(out=U, in_=KI), ['trg_k'], ['trg_u'])
            S.op('dve', lambda: nc.vector.scalar_tensor_tensor(out=t, in0=U, scalar=float(-TWO_PI), in1=arg,
                                                               op0=ALU.mult, op1=ALU.add), ['trg_u'] + arg_keys, [key])
            S.op('dve', lambda: nc.vector.tensor_scalar(out=t, in0=t, scalar1=float(shift), scalar2=float(np.pi),
                                                        op0=ALU.add, op1=ALU.min), [key], [key])
            S.op('dve', lambda: nc.vector.tensor_scalar(out=t, in0=t, scalar1=float(-np.pi), scalar2=None, op0=ALU.max), [key], [key])
            S.op('act', lambda: nc.scalar.activation(out=t, in_=t, func=AF.Sin), [key], [key])
        return 'trg_s', 'trg_c'

    def setup_s5(self):
        nc, S, es = self.nc, self.S, self.es
        tt = nc.vector.tensor_tensor
        def scr(name, shape, dt):
            return nc.dram_tensor(name, list(shape), dt, kind="Internal").ap()
        self.SC_WA = scr("sc_wa", [128, 8 * L8 * 2 * 128], BF16)
        self.SC_WV = scr("sc_wv", [128, 8 * L8 * 2 * 128], BF16)
        self.SC_CW = scr("sc_cw", [128, 32 * L8 * 2 * 32], BF16)
        self.SC_CT = scr("sc_ct", [128, 32 * NJ], F32)
        self.SC_ST = scr("sc_st", [128, 32 * NJ], F32)
        self.SC_RH = scr("sc_rh", [128, 32 * NJ], F32)
        self.SC['bur'] = scr("sc_bur", [NS, 4096], F32)
        self.SC['bui'] = scr("sc_bui", [NS, 4096], F32)
        lamA = self.inp("s5_lamA", [2, 128, 512])
        lsA = self.inp("s5_lsA", [128, 8])
        bA = self.inp("s5_bA", [2, 128, 512])
        lamC = self.inp("s5_lamC", [2, 128, 32])
        lsC = self.inp("s5_lsC", [128, 32])
        cC = self.inp("s5_cC", [2, 128, 512])
        self.s5_lam_row = self.inp("s5_lam_row", [2, 4096])
        self.s5_ls_row = self.inp("s5_ls_row", [1, 4096])
        self.glu_w = self.inp("s5_glu_w", [D, 2 * D])
        dsk = self.inp("s5_d_fm", [128, 8])
        cm = self.inp("c_s5masks", [128, 8])
        iota = self.inp("c_iota", [128, NJ])
        rst8 = self.inp("c_rst8", [128, 512])
        rstj = self.inp("c_rstj", [128, NJ])
        self.st_s5 = [self.inp("st_s5_re", [NS, 4096]), self.inp("st_s5_im", [NS, 4096])]
        self.s_s5 = [self.outp("s_s5_re", [NS, 4096]), self.outp("s_s5_im", [NS, 4096])]
        self.p_s5 = [self.outp("p_s5_re", [64, 64]), self.outp("p_s5_im", [64, 64])]
        self.DSK = self.sb(es, "DSK", [128, 8])
        self.S5M = self.sb(es, "S5M", [128, 8])
        self.RST8 = self.sb(es, "RST8", [128, 512])
        self.NPI = self.sb(es, "NPI", [128, 1])
        self.XPV = [self.sb(es, "XPV%d" % i, [128, 32]) for i in range(2)]
        self.AINV = [self.sb(es, "AINV%d" % i, [128, 32]) for i in range(2)]
        self.RHO8 = self.sb(es, "RHO8", [128, 32])
        S.dma('sp', self.DSK[:], dsk[:, :], writes=['DSK'])
        S.dma('sp', self.S5M[:], cm[:, :], writes=['S5M'])
        S.dma('sp', self.RST8[:], rst8[:, :], writes=['RST8'])
        S.op('dve', lambda: nc.vector.memset(self.NPI[:], -float(np.pi)), [], ['NPI'])
        for i in range(2):
            S.op('dve', lambda: nc.vector.memset(self.XPV[i][:], 0.0), [], [('XPV', i)])
        with ExitStack() as es1, nc.allow_non_contiguous_dma(reason="one-time S5 parameter re-layout"):
            A3 = [128, 8, 64]
            LRa = self.sb(es1, "a_lr", A3)
            LIa = self.sb(es1, "a_li", A3)
            LSa = self.sb(es1, "a_ls", [128, 8])
            BTr = self.sb(es1, "a_btr", A3)
            BTi = self.sb(es1, "a_bti", A3)
            for t_, r in ((LRa, 0), (LIa, 1)):
                S.dma('sp', t_[:].rearrange("p f n -> p (f n)"), lamA[r], writes=['a_l'])
            S.dma('sp', LSa[:], lsA[:, :], writes=['a_l'])
            for t_, r in ((BTr, 0), (BTi, 1)):
                S.dma('sp', t_[:].rearrange("p f n -> p (f n)"), bA[r], writes=['a_bt'])
            S.op('act', lambda: nc.scalar.activation(out=LSa[:], in_=LSa[:], func=AF.Exp), ['a_l'], ['a_l'])
            stp = LSa[:].unsqueeze(2).to_broadcast(A3)
            LAMr = self.sb(es1, "a_lamr", A3)
            LAMi = self.sb(es1, "a_lami", A3)
            self.copy('dve', LAMr[:], LRa[:], ['a_l'], ['a_lam'])
            self.copy('dve', LAMi[:], LIa[:], ['a_l'], ['a_lam'])
            S.op('dve', lambda: tt(out=LRa[:], in0=LRa[:], in1=stp, op=ALU.mult), ['a_l', 'a_lam'], ['a_l'])
            S.op('dve', lambda: tt(out=LIa[:], in0=LIa[:], in1=stp, op=ALU.mult), ['a_l'], [('arg', 'a1'), 'a_l'])
            SN = self.sb(es1, "a_sn", A3)
            CSn = self.sb(es1, "a_cs", A3)
            TU = self.sb(es1, "a_tu", A3)
            TK = self.sb(es1, "a_tk", A3, mybir.dt.int32)
            ks1, kc1 = self.s5_trig(LIa[:], ['a_l'], SN[:], CSn[:], 128, TU[:], TK[:])
            s1, c1 = SN, CSn
            MAG = self.sb(es1, "a_mag", A3)
            S.op('act', lambda: nc.scalar.activation(out=MAG[:], in_=LRa[:], func=AF.Exp), ['a_l'], ['a_mag'])
            ABr = self.sb(es1, "a_abr", A3)
            ABi = self.sb(es1, "a_abi", A3)
            S.op('dve', lambda: tt(out=ABr[:], in0=MAG[:], in1=c1[:], op=ALU.mult), ['a_mag', kc1], ['a_ab'])
            S.op('dve', lambda: tt(out=ABi[:], in0=MAG[:], in1=s1[:], op=ALU.mult), ['a_mag', ks1], ['a_ab'])
            DEN = self.sb(es1, "a_den", A3)
            T1 = self.sb(es1, "a_t1", A3)
            T2 = self.sb(es1, "a_t2", A3)
            CFr = self.sb(es1, "a_cfr", A3)
            CFi = self.sb(es1, "a_cfi", A3)
            S.op('dve', lambda: tt(out=DEN[:], in0=LAMr[:], in1=LAMr[:], op=ALU.mult), ['a_lam'], ['a_den'])
            S.op('dve', lambda: tt(out=T1[:], in0=LAMi[:], in1=LAMi[:], op=ALU.mult), ['a_lam'], ['a_t1'])
            S.op('dve', lambda: tt(out=DEN[:], in0=DEN[:], in1=T1[:], op=ALU.add), ['a_den', 'a_t1'], ['a_den'])
            S.op('dve', lambda: nc.vector.reciprocal(out=DEN[:], in_=DEN[:]), ['a_den'], ['a_den'])
            S.op('dve', lambda: nc.vector.tensor_scalar(out=T2[:], in0=ABr[:], scalar1=-1.0, scalar2=None, op0=ALU.add), ['a_ab'], ['a_t2'])
            S.op('dve', lambda: tt(out=CFr[:], in0=T2[:], in1=LAMr[:], op=ALU.mult), ['a_t2', 'a_lam'], ['a_cfr'])
            S.op('dve', lambda: tt(out=T1[:], in0=ABi[:], in1=LAMi[:], op=ALU.mult), ['a_ab', 'a_lam', 'a_den'], ['a_t1'])
            S.op('dve', lambda: tt(out=CFr[:], in0=CFr[:], in1=T1[:], op=ALU.add), ['a_cfr', 'a_t1'], ['a_cfr'])
            S.op('dve', lambda: tt(out=CFr[:], in0=CFr[:], in1=DEN[:], op=ALU.mult), ['a_cfr', 'a_den'], ['a_cfr'])
            S.op('dve', lambda: tt(out=CFi[:], in0=ABi[:], in1=LAMr[:], op=ALU.mult), ['a_ab', 'a_lam'], ['a_cfi'])
            S.op('dve', lambda: tt(out=T1[:], in0=T2[:], in1=LAMi[:], op=ALU.mult), ['a_t2', 'a_lam', 'a_cfr'], ['a_t1'])
            S.op('dve', lambda: tt(out=CFi[:], in0=CFi[:], in1=T1[:], op=ALU.subtract), ['a_cfi', 'a_t1'], ['a_cfi'])
            S.op('dve', lambda: tt(out=CFi[:], in0=CFi[:], in1=DEN[:], op=ALU.mult), ['a_cfi', 'a_den'], ['a_cfi'])
            BBr = self.sb(es1, "a_bbr", A3)
            BBi = self.sb(es1, "a_bbi", A3)
            def cmul(outr, outi, ar, ai, br, bi, keys_in, key_out, neg_ai=False):
                S.op('dve', lambda: tt(out=outr, in0=ar, in1=br, op=ALU.mult), keys_in + ['a_t1', 'a_t2'], [key_out + 'r'])
                S.op('dve', lambda: tt(out=T1[:], in0=ai, in1=bi, op=ALU.mult), keys_in, ['a_t1'])
                S.op('dve', lambda: tt(out=outr, in0=outr, in1=T1[:], op=ALU.subtract), [key_out + 'r', 'a_t1'], [key_out + 'r'])
                S.op('dve', lambda: tt(out=outi, in0=ar, in1=bi, op=ALU.mult), keys_in, [key_out + 'i'])
                S.op('dve', lambda: tt(out=T2[:], in0=ai, in1=br, op=ALU.mult), keys_in, ['a_t2'])
                S.op('dve', lambda: tt(out=outi, in0=outi, in1=T2[:], op=ALU.add), [key_out + 'i', 'a_t2'], [key_out + 'i'])
            cmul(BBr[:], BBi[:], CFr[:], CFi[:], BTr[:], BTi[:], ['a_cfr', 'a_cfi', 'a_bt'], 'a_bb')
            WT = self.sb(es1, "a_wt", [128, 8, L8, 2, 128], BF16)
            ARG = self.sb(es1, "a_arg", A3)
            PM = self.sb(es1, "a_pm", A3)
            Pr = self.sb(es1, "a_pr", A3)
            Pi = self.sb(es1, "a_pi", A3)
            Wr = self.sb(es1, "a_wr", A3)
            Wi = self.sb(es1, "a_wi", A3)
            Qr = self.sb(es1, "a_qr", A3)
            Qi = self.sb(es1, "a_qi", A3)
            S.op('act', lambda: nc.scalar.activation(out=PM[:], in_=LRa[:], func=AF.Exp, scale=-1.0), ['a_l'], ['a_pm'])
            S.op('dve', lambda: tt(out=Qr[:], in0=PM[:], in1=c1[:], op=ALU.mult), ['a_pm', kc1, 'a_ab'], ['a_q'])
            S.op('dve', lambda: tt(out=Qi[:], in0=PM[:], in1=s1[:], op=ALU.mult), ['a_pm', ks1, 'a_ab'], ['a_q'])
            S.op('dve', lambda: nc.vector.tensor_scalar(out=Qi[:], in0=Qi[:], scalar1=-1.0, scalar2=None, op0=ALU.mult), ['a_q'], ['a_q'])
            S.op('dve', lambda: nc.vector.tensor_scalar(out=ARG[:], in0=LIa[:], scalar1=float(L8), scalar2=None, op0=ALU.mult),
                 ['a_l'], ['a_arg'])
            ksn, kcs = self.s5_trig(ARG[:], ['a_arg', 'a_q'], SN[:], CSn[:], 128, TU[:], TK[:])
            S.op('act', lambda: nc.scalar.activation(out=PM[:], in_=LRa[:], func=AF.Exp, scale=float(L8)), ['a_l', 'a_q'], ['a_pm'])
            S.op('dve', lambda: tt(out=Pr[:], in0=PM[:], in1=CSn[:], op=ALU.mult), ['a_pm', kcs], ['a_p'])
            S.op('dve', lambda: tt(out=Pi[:], in0=PM[:], in1=SN[:], op=ALU.mult), ['a_pm', ksn], ['a_p'])
            Wr2 = self.sb(es1, "a_wr2", A3)
            Wi2 = self.sb(es1, "a_wi2", A3)
            WW = [(Wr, Wi, 'a_w'), (Wr2, Wi2, 'a_x')]
            for wset, dst in ((0, self.SC_WA), (1, self.SC_WV)):
                for l in range(L8):
                    wr_, wi_, kk = WW[l % 2]
                    if l == 0 and wset == 0:
                        self.copy('dve', wr_[:], BBr[:], ['a_bbr'], [kk + 'r'])
                        self.copy('dve', wi_[:], BBi[:], ['a_bbi'], [kk + 'i'])
                    elif l == 0:
                        cmul(wr_[:], wi_[:], Pr[:], Pi[:], BBr[:], BBi[:], ['a_p', 'a_bbr', 'a_bbi'], kk)
                    else:
                        pr_, pi_, pk = WW[(l - 1) % 2]
                        cmul(wr_[:], wi_[:], pr_[:], pi_[:], Qr[:], Qi[:], [pk + 'r', pk + 'i', 'a_q'], kk)
                    for ri, wsrc, kw in ((0, wr_, kk + 'r'), (1, wi_, kk + 'i')):
                        for g2 in range(2):
                            S.op('act', lambda: nc.scalar.activation(out=WT[:, :, l, ri, g2 * 64:(g2 + 1) * 64], in_=wsrc[:],
                                                                     func=AF.Identity, scale=self.S5M[:, g2:g2 + 1]),
                                 [kw, 'S5M'], ['a_wt'])
                S.dma('sp', dst[:, :], WT[:].rearrange("p f l r m -> p (f l r m)"), reads=['a_wt'], writes=[('scw', wset)])
            S.barrier()
        with ExitStack() as es1, nc.allow_non_contiguous_dma(reason="one-time S5 parameter re-layout"):
            C2 = [128, 32]
            LRc = self.sb(es1, "c_lr", C2)
            LIc = self.sb(es1, "c_li", C2)
            LSc = self.sb(es1, "c_ls", C2)
            CTr = self.sb(es1, "c_ctr", [128, 32, 16])
            CTi = self.sb(es1, "c_cti", [128, 32, 16])
            for t_, r in ((LRc, 0), (LIc, 1)):
                S.dma('sp', t_[:], lamC[r], writes=['c_l'])
            S.dma('sp', LSc[:], lsC[:, :], writes=['c_l'])
            for t_, r in ((CTr, 0), (CTi, 1)):
                S.dma('sp', t_[:].rearrange("p a c -> p (a c)"), cC[r], writes=['c_ct'])
            S.op('act', lambda: nc.scalar.activation(out=LSc[:], in_=LSc[:], func=AF.Exp), ['c_l'], ['c_l'])
            S.op('dve', lambda: tt(out=LRc[:], in0=LRc[:], in1=LSc[:], op=ALU.mult), ['c_l'], ['c_l'])
            S.op('dve', lambda: tt(out=LIc[:], in0=LIc[:], in1=LSc[:], op=ALU.mult), ['c_l'], ['c_l'])
            ARG = self.sb(es1, "c_arg", C2)
            PM = self.sb(es1, "c_pm", C2)
            Ar = self.sb(es1, "c_ar", C2)
            Ai = self.sb(es1, "c_ai", C2)
            SNc = self.sb(es1, "c_sn", C2)
            CSc = self.sb(es1, "c_cs", C2)
            TUc = self.sb(es1, "c_tu", C2)
            TKc = self.sb(es1, "c_tk", C2, mybir.dt.int32)
            def apow(m, tag):
                S.op('dve', lambda: nc.vector.tensor_scalar(out=ARG[:], in0=LIc[:], scalar1=float(m), scalar2=None, op0=ALU.mult),
                     ['c_l', 'c_a'], ['c_arg'])
                ksn, kcs = self.s5_trig(ARG[:], ['c_arg'], SNc[:], CSc[:], 128, TUc[:], TKc[:])
                sn, cs = SNc, CSc
                S.op('act', lambda: nc.scalar.activation(out=PM[:], in_=LRc[:], func=AF.Exp, scale=float(m)), ['c_l', 'c_a'], ['c_pm'])
                S.op('dve', lambda: tt(out=Ar[:], in0=PM[:], in1=cs[:], op=ALU.mult), ['c_pm', kcs], ['c_a'])
                S.op('dve', lambda: tt(out=Ai[:], in0=PM[:], in1=sn[:], op=ALU.mult), ['c_pm', ksn], ['c_a'])
            CW = self.sb(es1, "c_cw", [128, 32, L8, 2, 32], BF16)
            R1 = self.sb(es1, "c_r1", [128, 32, 16])
            R2 = self.sb(es1, "c_r2", [128, 32, 16])
            for l in range(L8):
                apow(l, 'c%d' % l)
                arb = Ar[:].unsqueeze(2).to_broadcast([128, 32, 16])
                aib = Ai[:].unsqueeze(2).to_broadcast([128, 32, 16])
                S.op('dve', lambda: tt(out=R1[:], in0=CTr[:], in1=arb, op=ALU.mult), ['c_ct', 'c_a'], ['c_r1'])
                S.op('dve', lambda: tt(out=R2[:], in0=CTi[:], in1=aib, op=ALU.mult), ['c_ct', 'c_a'], ['c_r2'])
                S.op('dve', lambda: tt(out=R1[:], in0=R1[:], in1=R2[:], op=ALU.subtract), ['c_r1', 'c_r2'], ['c_r1'])
                for g2 in range(2):
                    S.op('act', lambda: nc.scalar.activation(out=CW[:, :, l, 0, g2 * 16:(g2 + 1) * 16], in_=R1[:], func=AF.Identity,
                                                             scale=self.S5M[:, 6 + g2:7 + g2]), ['c_r1', 'S5M'], ['c_cw'])
                S.op('dve', lambda: tt(out=R1[:], in0=CTr[:], in1=aib, op=ALU.mult), ['c_ct', 'c_a', 'c_r1'], ['c_r1'])
                S.op('dve', lambda: tt(out=R2[:], in0=CTi[:], in1=arb, op=ALU.mult), ['c_ct', 'c_a', 'c_r2'], ['c_r2'])
                S.op('dve', lambda: tt(out=R1[:], in0=R1[:], in1=R2[:], op=ALU.add), ['c_r1', 'c_r2'], ['c_r1'])
                S.op('dve', lambda: nc.vector.tensor_scalar(out=R1[:], in0=R1[:], scalar1=-1.0, scalar2=None, op0=ALU.mult), ['c_r1'], ['c_r1'])
                for g2 in range(2):
                    S.op('act', lambda: nc.scalar.activation(out=CW[:, :, l, 1, g2 * 16:(g2 + 1) * 16], in_=R1[:], func=AF.Identity,
                                                             scale=self.S5M[:, 6 + g2:7 + g2]), ['c_r1', 'S5M'], ['c_cw'])
            S.dma('sp', self.SC_CW[:, :], CW[:].rearrange("p a l r c -> p (a l r c)"), reads=['c_cw'], writes=['sccw'])
            apow(-1, 'cm1')
            self.copy('dve', self.AINV[0][:], Ar[:], ['c_a'], ['AINV'])
            self.copy('dve', self.AINV[1][:], Ai[:], ['c_a'], ['AINV'])
            S.op('act', lambda: nc.scalar.activation(out=self.RHO8[:], in_=LRc[:], func=AF.Exp, scale=float(L8)), ['c_l'], ['RHO8'])
            IO = self.sb(es1, "c_io", [128, NJ])
            RJ = self.sb(es1, "c_rj", [128, NJ])
            S.dma('sp', IO[:], iota[:, :], writes=['c_io'])
            S.dma('sp', RJ[:], rstj[:, :], writes=['c_rj'])
            ANG = self.sb(es1, "c_ang", [128, 32, NJ])
            PH = self.sb(es1, "c_ph", C2)
            S.op('dve', lambda: nc.vector.tensor_scalar(out=TUc[:], in0=LIc[:], scalar1=float(L8 / TWO_PI), scalar2=None, op0=ALU.mult),
                 ['c_l', 'trg_u'], ['trg_u'])
            S.op('dve', lambda: nc.vector.tensor_copy(out=TKc[:], in_=TUc[:]), ['trg_u', 'trg_k'], ['trg_k'])
            S.op('dve', lambda: nc.vector.tensor_copy(out=TUc[:], in_=TKc[:]), ['trg_k'], ['trg_u'])
            S.op('dve', lambda: nc.vector.tensor_scalar(out=PH[:], in0=LIc[:], scalar1=float(L8), scalar2=None, op0=ALU.mult), ['c_l'], ['c_ph'])
            S.op('dve', lambda: nc.vector.scalar_tensor_tensor(out=PH[:], in0=TUc[:], scalar=float(-TWO_PI), in1=PH[:],
                                                               op0=ALU.mult, op1=ALU.add), ['trg_u', 'c_ph'], ['c_ph'])
            S.op('dve', lambda: tt(out=ANG[:], in0=IO[:].unsqueeze(1).to_broadcast([128, 32, NJ]),
                                   in1=PH[:].unsqueeze(2).to_broadcast([128, 32, NJ]), op=ALU.mult), ['c_io', 'c_ph'], ['c_ang'])
            a2 = ANG[:].rearrange("p a j -> p (a j)")
            TUj = self.sb(es1, "c_tuj", [128, 32 * NJ])
            TKj = self.sb(es1, "c_tkj", [128, 32 * NJ], mybir.dt.int32)
            SNj = self.sb(es1, "c_snj", [128, 32 * NJ])
            CSj = self.sb(es1, "c_csj", [128, 32 * NJ])
            ksn, kcs = self.s5_trig(a2, ['c_ang'], SNj[:], CSj[:], 128, TUj[:], TKj[:])
            S.dma('sp', self.SC_CT[:, :], CSj[:], reads=[kcs], writes=['scct'])
            S.dma('sp', self.SC_ST[:, :], SNj[:], reads=[ksn], writes=['scct'])
            S.op('dve', lambda: tt(out=ANG[:], in0=RJ[:].unsqueeze(1).to_broadcast([128, 32, NJ]),
                                   in1=self.RHO8[:].unsqueeze(2).to_broadcast([128, 32, NJ]), op=ALU.mult),
                 ['c_rj', 'RHO8', kcs, ksn], ['c_ang'])
            S.dma('sp', self.SC_RH[:, :], a2, reads=['c_ang'], writes=['scct'])
            S.barrier()


class ProgC2(ProgC):
    def mixer_c(self, seg):
        nc, S = self.nc, self.S
        tt = nc.vector.tensor_tensor
        ncol = seg_cols(seg)
        NG = 8
        with ExitStack() as es0:
            HN = self.sb(es0, "c_hn", [128, 8, ncol], BF16)
            GY = self.sb(es0, "c_gy", [128, 8, ncol], BF16)
            with ExitStack() as es2:
                self.rmsnorm_fm(es2, seg, self.XR, 'xr', self.NW[:, 0, 1, :], HN, 'hn')
                S.barrier()
            with ExitStack() as esT:
                TT = [self.sb(esT, "c_tt%d" % i, [128, 32, NJ]) for i in range(2)]
                self.s5_scan_phases(seg, HN, GY, TT)
                S.barrier()
            self.nrot = 7
            if seg == 0:
                self.s5_sample_phase(HN, GY)
            if seg == NSEG - 1:
                self.s5_final_state()
            self.s5_glu(seg, GY)
            S.barrier()

    def s5_scan_phases(self, seg, HN, GY, TT):
        nc, S = self.nc, self.S
        tt = nc.vector.tensor_tensor
        NG = 8
        if True:
            if self.dbg.get('s5_stop') == 1:
                S.barrier()
                return
            with ExitStack() as es:
                WPQ = [self.sb(es, "c_wpq%d" % i, [128, L8, 2, 128], BF16) for i in range(4)]
                for i in range(4):
                    S.op('pool', lambda: nc.gpsimd.memset(WPQ[i][:].rearrange("p l r m -> p (l r m)"), 0.0), [], [('wp', i)])
                G3 = [128, NG, NJ]
                CT = self.sb(es, "c_ct", G3)
                ST = self.sb(es, "c_st", G3)
                RH = self.sb(es, "c_rh", G3)
                U = [self.sb(es, "c_u%d" % i, G3) for i in range(2)]
                Z = [self.sb(es, "c_z%d" % i, G3) for i in range(2)]
                T1 = self.sb(es, "c_t1", G3)
                T2 = self.sb(es, "c_t2", G3)
                SD = self.sb(es, "c_sd", [128, NG])
                def pass1(fc):
                        for p4 in range(4):
                            p = fc * 4 + p4
                            wp = WPQ[p4]
                            rows = slice(32 * p4, 32 * p4 + 32)
                            S.dma('sp', wp[rows, :, :, :].rearrange("p l r m -> p (l r m)"), self.SC_WV[rows, fc * 2048:(fc + 1) * 2048],
                                  reads=[('scw', 1)], writes=[('wp', p4)])
                            if self.dbg.get('s5_stop') == 21:
                                continue
                            b = self.bank()
                            for ri in range(2):
                                self.mm(self.PB[b][:, ri * NJ:(ri + 1) * NJ], [('pb', b)],
                                        [(wp[:, l, ri, :] if self.dbg.get('s5_stop') != 23 else self.identb[:], HN[:, fc, l:SEG:L8] if self.dbg.get('s5_stop') not in (22, 23) else HN[:, fc, l * 128:(l + 1) * 128]) for l in range(L8)],
                                        [('wp', p4)] + tkeys('hn', [fc], 0) + tkeys('hn', [fc], 1))
                            for ri in range(2):
                                if self.dbg.get('s5_stop') == 24:
                                    continue
                                if self.dbg.get('s5_stop') == 25:
                                    self.copy('dve', WF[0][:, 0, 0, :], self.PB[b][:, ri * NJ:(ri + 1) * NJ], [('pb', b)], [('wf', 0)])
                                    continue
                                self.copy('act', TT[ri][:, p, :], self.PB[b][:, ri * NJ:(ri + 1) * NJ], [('pb', b)], [('tt', ri, p // NG)])
                def coarse(pg):
                        pp = slice(pg * NG, (pg + 1) * NG)
                        cols = slice(pg * NG * NJ, (pg + 1) * NG * NJ)
                        for t_, d_, k_ in ((CT, self.SC_CT, 'cct'), (ST, self.SC_ST, 'cst'), (RH, self.SC_RH, 'crh')):
                            S.dma('sp', t_[:].rearrange("p a j -> p (a j)"), d_[:, cols], reads=['scct'], writes=[k_])
                        Vr, Vi = TT[0][:, pp, :], TT[1][:, pp, :]
                        kv = [('tt', 0, pg), ('tt', 1, pg)]
                        S.op('dve', lambda: tt(out=U[0][:], in0=CT[:], in1=Vr, op=ALU.mult), ['cct'] + kv, ['cu0'])
                        S.op('pool', lambda: nc.gpsimd.tensor_tensor(out=T1[:], in0=ST[:], in1=Vi, op=ALU.mult), ['cst'] + kv, ['ct1'])
                        S.op('dve', lambda: tt(out=U[0][:], in0=U[0][:], in1=T1[:], op=ALU.add), ['cu0', 'ct1'], ['cu0'])
                        S.op('dve', lambda: tt(out=U[1][:], in0=CT[:], in1=Vi, op=ALU.mult), ['cct'] + kv, ['cu1'])
                        S.op('pool', lambda: nc.gpsimd.tensor_tensor(out=T2[:], in0=ST[:], in1=Vr, op=ALU.mult), ['cst'] + kv, ['ct2'])
                        S.op('dve', lambda: tt(out=U[1][:], in0=U[1][:], in1=T2[:], op=ALU.subtract), ['cu1', 'ct2'], ['cu1'])
                        for ri in range(2):
                            S.op('dve', lambda: tt(out=SD[:], in0=self.RHO8[:, pp], in1=self.XPV[ri][:, pp], op=ALU.mult),
                                 ['RHO8', ('XPV', ri)], ['csd'])
                            S.op('dve', lambda: tt(out=U[ri][:, :, 0], in0=U[ri][:, :, 0], in1=SD[:], op=ALU.add), ['cu%d' % ri, 'csd'], ['cu%d' % ri])
                        for ri in range(2):
                            S.op('dve', lambda: nc.vector.tensor_tensor_scan(
                                out=Z[ri][:].rearrange("p a j -> p (a j)"), data0=RH[:].rearrange("p a j -> p (a j)"),
                                data1=U[ri][:].rearrange("p a j -> p (a j)"), initial=0.0, op0=ALU.mult, op1=ALU.add),
                                ['crh', 'cu%d' % ri], ['cz%d' % ri])
                        S.op('dve', lambda: tt(out=U[0][:], in0=CT[:], in1=Z[0][:], op=ALU.mult), ['cct', 'cz0'], ['cu0'])
                        S.op('pool', lambda: nc.gpsimd.tensor_tensor(out=T1[:], in0=ST[:], in1=Z[1][:], op=ALU.mult), ['cst', 'cz1'], ['ct1'])
                        S.op('dve', lambda: tt(out=U[0][:], in0=U[0][:], in1=T1[:], op=ALU.subtract), ['cu0', 'ct1'], ['cu0'])
                        S.op('dve', lambda: tt(out=U[1][:], in0=CT[:], in1=Z[1][:], op=ALU.mult), ['cct', 'cz1'], ['cu1'])
                        S.op('pool', lambda: nc.gpsimd.tensor_tensor(out=T2[:], in0=ST[:], in1=Z[0][:], op=ALU.mult), ['cst', 'cz0'], ['ct2'])
                        S.op('dve', lambda: tt(out=U[1][:], in0=U[1][:], in1=T2[:], op=ALU.add), ['cu1', 'ct2'], ['cu1'])
                        for ri in range(2):
                            self.copy('pool', TT[ri][:, pp, 1:NJ], U[ri][:, :, 0:NJ - 1], ['cu%d' % ri], [('tt', ri, pg)])
                            self.copy('dve', TT[ri][:, pp, 0], self.XPV[ri][:, pp], [('XPV', ri)], [('tt', ri, pg)])
                            self.copy('dve', self.XPV[ri][:, pp], U[ri][:, :, NJ - 1], ['cu%d' % ri], [('XPV', ri)])
                npg = 32 // NG
                fpg = 8 // npg
                for fc in range(fpg):
                    pass1(fc)
                for pg in range(npg):
                    if pg + 1 < npg:
                        for fc in range((pg + 1) * fpg, (pg + 2) * fpg):
                            pass1(fc)
                    coarse(pg)
                S.barrier()
            with ExitStack() as es:
                WPQ = [self.sb(es, "c_wpq%d" % i, [128, L8, 2, 128], BF16) for i in range(4)]
                for i in range(4):
                    S.op('pool', lambda: nc.gpsimd.memset(WPQ[i][:].rearrange("p l r m -> p (l r m)"), 0.0), [], [('wp', i)])
                CWF = [self.sb(es, "c_cwf%d" % i, [128, 4, L8, 2, 32], BF16) for i in range(2)]
                ZB = [[self.sb(es, "c_zb%d_%d" % (i, ri), [128, SEG], BF16) for ri in range(2)] for i in range(2)]
                YTS = self.sb(es, "c_yts", [128, L8, 128])
                TY = self.sb(es, "c_ty", [128, SEG])
                for fc in range(8):
                    cwf = CWF[fc % 2]
                    S.dma('sp', cwf[:].rearrange("p a l r c -> p (a l r c)"), self.SC_CW[:, fc * 2048:(fc + 1) * 2048],
                          reads=['sccw'], writes=[('cwf', fc % 2)])
                    self.nrot = 6
                    by = [6, 7]
                    for p4 in range(4):
                        p = fc * 4 + p4
                        wp = WPQ[p4]
                        zb = ZB[p % 2]
                        rows = slice(32 * p4, 32 * p4 + 32)
                        S.dma('sp', wp[rows, :, :, :].rearrange("p l r m -> p (l r m)"), self.SC_WA[rows, fc * 2048:(fc + 1) * 2048],
                              reads=[('scw', 0)], writes=[('wp', p4)])
                        for ri in range(2):
                            for half in range(2):
                                b = self.bank()
                                S.deps('pe', [('wp', p4)] + tkeys('hn', [fc], half), [('pb', b)])
                                ins = None
                                for l in range(L8):
                                    ins = nc.tensor.matmul(self.PB[b][:, l:512:L8], lhsT=wp[:, l, ri, :],
                                                           rhs=HN[:, fc, half * 512 + l:(half + 1) * 512:L8], start=True, stop=True)
                                    S.n_inst += 1
                                S.group_end('pe', ins, [('wp', p4)] + tkeys('hn', [fc], half), [('pb', b)])
                                S.op('dve', lambda: tt(out=self.PB[b][:, 0:512:L8], in0=self.PB[b][:, 0:512:L8],
                                                       in1=TT[ri][:, p, half * 64:(half + 1) * 64], op=ALU.add),
                                     [('pb', b), ('tt', ri, p // NG)], [('pb', b)])
                                S.op('dve', lambda: nc.vector.tensor_tensor_scan(
                                    out=zb[ri][:, half * 512:(half + 1) * 512], data0=self.RST8[:], data1=self.PB[b][:, :],
                                    initial=0.0, op0=ALU.mult, op1=ALU.add), [('pb', b), 'RST8'], [('zb', p % 2, ri)])
                        if seg == 0:
                            b = self.bank()
                            for ri in range(2):
                                self.mm(self.PB[b][0:NS, ri * 128:(ri + 1) * 128], [('pb', b)],
                                        [(HN[:, fc, SEG:SEG + NS], wp[:, 0, ri, :])], [('wp', p4)] + tkeys('hn', [fc], 2))
                            for ri, nm in ((0, 'bur'), (1, 'bui')):
                                i = self.stg_rr
                                self.stg_rr = (self.stg_rr + 1) % len(self.stg)
                                self.copy('dve', self.stg[i][:, :], self.PB[b][0:NS, ri * 128:(ri + 1) * 128], [('pb', b)], [('stg', i)])
                                S.dma('sp', self.SC[nm][:, p * 128:(p + 1) * 128], self.stg[i][:, :], reads=[('stg', i)], writes=[('sc', nm)])
                        if self.dbg.get('s5_stop') == 4:
                            continue
                        for l in range(L8):
                            bb = by[l // 4]
                            c0 = (l % 4) * 128 + p4 * 32
                            self.mm(self.PB[bb][:, c0:c0 + 32], [('pb', bb)],
                                    [(zb[0][:, l:SEG:L8], cwf[:, p4, l, 0, :]), (zb[1][:, l:SEG:L8], cwf[:, p4, l, 1, :])],
                                    [('zb', p % 2, 0), ('zb', p % 2, 1), ('cwf', fc % 2)])
                    if self.dbg.get('s5_stop') == 4:
                        continue
                    for hh in range(2):
                        self.copy('act', YTS[:, hh * 4:(hh + 1) * 4, :], self.PB[by[hh]][:, :].rearrange("p (l f) -> p l f", f=128),
                                  [('pb', by[hh])], ['yts'])
                    for hh in range(2):
                        b = self.bank()
                        for li in range(4):
                            self.transpose(self.PB[b][:, li * 128:(li + 1) * 128], [('pb', b)], YTS[:, hh * 4 + li, :], self.ident[:],
                                           ['yts', 'ident'])
                        S.op('dve', lambda: nc.vector.scalar_tensor_tensor(
                            out=TY[:].rearrange("p (j l) -> p j l", l=L8)[:, :, hh * 4:(hh + 1) * 4],
                            in0=HN[:, fc, 0:SEG].rearrange("p (j l) -> p j l", l=L8)[:, :, hh * 4:(hh + 1) * 4],
                            scalar=self.DSK[:, fc:fc + 1],
                            in1=self.PB[b][:, :].rearrange("p (l j) -> p j l", l=4), op0=ALU.mult, op1=ALU.add),
                            [('pb', b), 'DSK'] + tkeys('hn', [fc], 0) + tkeys('hn', [fc], 1), ['ty'])
                    if seg == 0 and fc == 0:
                        self.dump("ty", TY[:], ['ty'])
                    S.op('act', lambda: nc.scalar.activation(out=GY[:, fc, 0:SEG], in_=TY[:], func=AF.Gelu_apprx_tanh), ['ty'],
                         [('gy', fc, 0), ('gy', fc, 1)])
                S.barrier()

    def s5_glu(self, seg, GY):
        nc, S = self.nc, self.S
        tt = nc.vector.tensor_tensor
        ncol = seg_cols(seg)
        if True:
            with ExitStack() as es:
                Mo = self.sb(es, "c_mo", [128, 8, ncol])
                SGt = [self.sb(es, "c_sg%d" % i, [128, 512]) for i in range(2)]
                ws_g = wstream(self, [(self.glu_w[:, vg * D + ob * 128:vg * D + (ob + 1) * 128], 8)
                                      for ob in range(8) for vg in range(2)], depth=2)
                for ob in range(8):
                    wv, wkv = next(ws_g)
                    wg, wkg = next(ws_g)
                    for ti, (c0, n) in enumerate(ttiles(seg)):
                        bv, bg = self.bank(), self.bank()
                        self.mm(self.PB[bv][:, 0:n], [('pb', bv)], [(wv[:, kc, :], GY[:, kc, c0:c0 + n]) for kc in range(8)],
                                [wkv] + tkeys('gy', range(8), ti))
                        self.mm(self.PB[bg][:, 0:n], [('pb', bg)], [(wg[:, kc, :], GY[:, kc, c0:c0 + n]) for kc in range(8)],
                                [wkg] + tkeys('gy', range(8), ti))
                        sg = SGt[ti % 2]
                        S.op('act', lambda: nc.scalar.activation(out=sg[:, 0:n], in_=self.PB[bg][:, 0:n], func=AF.Sigmoid),
                             [('pb', bg)], [('csg', ti % 2)])
                        S.op('dve', lambda: tt(out=Mo[:, ob, c0:c0 + n], in0=self.PB[bv][:, 0:n], in1=sg[:, 0:n], op=ALU.mult),
                             [('pb', bv), ('csg', ti % 2)], [('mo', ob, ti)])
                with ExitStack() as es2:
                    self.rmsnorm_fm(es2, seg, Mo, 'mo', self.NW[:, 1, 1, :], self.XR, 'xr', residual=True)
                    S.barrier()
                S.barrier()
            S.barrier()

    def s5_final_state(self):
        nc, S = self.nc, self.S
        tt = nc.vector.tensor_tensor
        with ExitStack() as es, nc.allow_non_contiguous_dma(reason="tiny state scatter"):
            R = [self.sb(es, "fs_r%d" % i, [128, 32]) for i in range(2)]
            T = self.sb(es, "fs_t", [128, 32])
            xr, xi = self.XPV
            ar, ai = self.AINV
            S.op('dve', lambda: tt(out=R[0][:], in0=xr[:], in1=ar[:], op=ALU.mult), [('XPV', 0), 'AINV'], ['fs0'])
            S.op('dve', lambda: tt(out=T[:], in0=xi[:], in1=ai[:], op=ALU.mult), [('XPV', 1), 'AINV'], ['fst'])
            S.op('dve', lambda: tt(out=R[0][:], in0=R[0][:], in1=T[:], op=ALU.subtract), ['fs0', 'fst'], ['fs0'])
            S.op('dve', lambda: tt(out=R[1][:], in0=xr[:], in1=ai[:], op=ALU.mult), [('XPV', 0), 'AINV'], ['fs1'])
            S.op('dve', lambda: tt(out=T[:], in0=xi[:], in1=ar[:], op=ALU.mult), [('XPV', 1), 'AINV', 'fs0'], ['fst'])
            S.op('dve', lambda: tt(out=R[1][:], in0=R[1][:], in1=T[:], op=ALU.add), ['fs1', 'fst'], ['fs1'])
            for ri in range(2):
                for g2 in range(2):
                    ps = slice(g2 * 64, (g2 + 1) * 64)
                    S.dma('sp', self.p_s5[ri].rearrange("(p g) n -> g n p", g=2)[g2], R[ri][ps, :], reads=['fs%d' % ri])
            S.barrier()

    def s5_sample_phase(self, HN, GY):
        nc, S = self.nc, self.S
        tt = nc.vector.tensor_tensor
        with ExitStack() as es:
            Q = [NS, 1024]
            LR = self.sb(es, "q_lr", Q)
            LI = self.sb(es, "q_li", Q)
            LS = self.sb(es, "q_ls", Q)
            SN = self.sb(es, "q_sn", Q)
            CS = self.sb(es, "q_cs", Q)
            TU = self.sb(es, "q_tu", Q)
            TK = self.sb(es, "q_tk", Q, mybir.dt.int32)
            MG = self.sb(es, "q_mg", Q)
            S0 = [self.sb(es, "q_s%d" % i, Q) for i in range(2)]
            BU = [self.sb(es, "q_bu%d" % i, Q) for i in range(2)]
            X = [self.sb(es, "q_x%d" % i, Q) for i in range(2)]
            T1 = self.sb(es, "q_t1", Q)
            XS = [self.sb(es, "q_xs%d" % i, [128, 32, NS], BF16) for i in range(2)]
            CW0 = self.sb(es, "q_cw0", [128, 32, 64], BF16)
            YT = self.sb(es, "q_yt", [NS, D])
            TYs = self.sb(es, "q_ty", [128, 8, NS])
            S.dma('sp', CW0[:], self.SC_CW.rearrange("p (a l x) -> p a l x", l=L8, x=64)[:, :, 0, :], reads=['sccw'], writes=['q_cw0'])
            for qd in range(4):
                cols = slice(qd * 1024, (qd + 1) * 1024)
                S.dma('sp', LR[:], self.s5_lam_row[0:1, cols].partition_broadcast(NS), writes=['q_lr'])
                S.dma('sp', LI[:], self.s5_lam_row[1:2, cols].partition_broadcast(NS), writes=['q_li'])
                S.dma('sp', LS[:], self.s5_ls_row[0:1, cols].partition_broadcast(NS), writes=['q_ls'])
                for ri in range(2):
                    S.dma('sp', S0[ri][:], self.st_s5[ri][:, cols], writes=[('q_s', ri)])
                    S.dma('sp', BU[ri][:], self.SC['bur' if ri == 0 else 'bui'][:, cols], reads=[('sc', 'bur' if ri == 0 else 'bui')],
                          writes=[('q_bu', ri)])
                S.op('act', lambda: nc.scalar.activation(out=LS[:], in_=LS[:], func=AF.Exp), ['q_ls'], ['q_ls'])
                S.op('dve', lambda: tt(out=LR[:], in0=LR[:], in1=LS[:], op=ALU.mult), ['q_lr', 'q_ls'], ['q_lr'])
                S.op('dve', lambda: tt(out=LI[:], in0=LI[:], in1=LS[:], op=ALU.mult), ['q_li', 'q_ls'], ['q_li'])
                ksn, kcs = self.s5_trig(LI[:], ['q_li'], SN[:], CS[:], NS, TU[:], TK[:])
                S.op('act', lambda: nc.scalar.activation(out=MG[:], in_=LR[:], func=AF.Exp), ['q_lr'], ['q_mg'])
                S.op('dve', lambda: tt(out=CS[:], in0=CS[:], in1=MG[:], op=ALU.mult), [kcs, 'q_mg'], [kcs])
                S.op('dve', lambda: tt(out=SN[:], in0=SN[:], in1=MG[:], op=ALU.mult), [ksn, 'q_mg'], [ksn])
                S.op('dve', lambda: tt(out=X[0][:], in0=CS[:], in1=S0[0][:], op=ALU.mult), [kcs, ('q_s', 0)], [('q_x', 0)])
                S.op('dve', lambda: tt(out=T1[:], in0=SN[:], in1=S0[1][:], op=ALU.mult), [ksn, ('q_s', 1)], ['q_t1'])
                S.op('dve', lambda: tt(out=X[0][:], in0=X[0][:], in1=T1[:], op=ALU.subtract), [('q_x', 0), 'q_t1'], [('q_x', 0)])
                S.op('dve', lambda: tt(out=X[0][:], in0=X[0][:], in1=BU[0][:], op=ALU.add), [('q_x', 0), ('q_bu', 0)], [('q_x', 0)])
                S.op('dve', lambda: tt(out=X[1][:], in0=CS[:], in1=S0[1][:], op=ALU.mult), [kcs, ('q_s', 1)], [('q_x', 1)])
                S.op('dve', lambda: tt(out=T1[:], in0=SN[:], in1=S0[0][:], op=ALU.mult), [ksn, ('q_s', 0), ('q_x', 0)], ['q_t1'])
                S.op('dve', lambda: tt(out=X[1][:], in0=X[1][:], in1=T1[:], op=ALU.add), [('q_x', 1), 'q_t1'], [('q_x', 1)])
                S.op('dve', lambda: tt(out=X[1][:], in0=X[1][:], in1=BU[1][:], op=ALU.add), [('q_x', 1), ('q_bu', 1)], [('q_x', 1)])
                for ri in range(2):
                    S.dma('sp', self.s_s5[ri][:, cols], X[ri][:], reads=[('q_x', ri)])
                    b = self.bank()
                    for j in range(8):
                        self.transpose(self.PB[b][:, j * NS:(j + 1) * NS], [('pb', b)], X[ri][:, j * 128:(j + 1) * 128],
                                       self.ident[0:NS, 0:NS], [('q_x', ri), 'ident'])
                    self.copy('dve', XS[ri][:, qd * 8:(qd + 1) * 8, :], self.PB[b][:, 0:8 * NS].rearrange("p (a t) -> p a t", t=NS),
                              [('pb', b)], [('q_xs', ri)])
            for hh in range(2):
                b = self.bank()
                for pj in range(16):
                    p = hh * 16 + pj
                    self.mm(self.PB[b][0:NS, pj * 32:(pj + 1) * 32], [('pb', b)],
                            [(XS[0][:, p, :], CW0[:, p, 0:32]), (XS[1][:, p, :], CW0[:, p, 32:64])],
                            [('q_xs', 0), ('q_xs', 1), 'q_cw0'])
                self.copy('dve', YT[:, hh * 512:(hh + 1) * 512], self.PB[b][0:NS, :], [('pb', b)], ['q_yt'])
            b = self.bank()
            for fc in range(8):
                self.transpose(self.PB[b][:, fc * NS:(fc + 1) * NS], [('pb', b)], YT[:, fc * 128:(fc + 1) * 128],
                               self.ident[0:NS, 0:NS], ['q_yt', 'ident'])
            for fc in range(8):
                S.op('dve', lambda: nc.vector.scalar_tensor_tensor(
                    out=TYs[:, fc, :], in0=HN[:, fc, SEG:SEG + NS], scalar=self.DSK[:, fc:fc + 1],
                    in1=self.PB[b][:, fc * NS:(fc + 1) * NS], op0=ALU.mult, op1=ALU.add),
                    [('pb', b), 'DSK', ('hn', fc, 2)], ['q_ty'])
            S.op('act', lambda: nc.scalar.activation(out=GY[:, :, SEG:SEG + NS], in_=TYs[:], func=AF.Gelu_apprx_tanh), ['q_ty'],
                 tkeys('gy', range(8), 2))
            S.barrier()


def kernel(**inputs):
    inp = {k: np.asarray(v) for k, v in inputs.items()}
    res = run_stage(inp, "full", cores=8)
    B = 8
    def cat(name, shape=None):
        a = [np.asarray(r[name], np.float32) for r in res]
        return a
    y_prompt = np.stack(cat("y_p"), 0)
    y_sample = np.concatenate(cat("y_s"), 0).reshape(B * NS, 1, D)
    p_ssm = np.stack(cat("p_ssm"), 0)[None]
    p_sconv = np.stack(cat("p_sconv"), 0)[None]
    p_hg = np.stack(cat("p_hg"), 0)[None]
    p_s5r = np.stack(cat("p_s5_re"), 0)[None]
    p_s5i = np.stack(cat("p_s5_im"), 0)[None]
    p_fconv = np.stack(cat("p_fconv"), 1)
    s_ssm = np.concatenate(cat("s_ssm"), 0).reshape(1, B * NS, 16, 64, 64)
    s_sconv = np.concatenate(cat("s_sconv"), 0)[None]
    s_hg = np.concatenate(cat("s_hg"), 0).reshape(1, B * NS, 8, 128, 128)
    s_s5r = np.concatenate(cat("s_s5_re"), 0).reshape(1, B * NS, 64, 64)
    s_s5i = np.concatenate(cat("s_s5_im"), 0).reshape(1, B * NS, 64, 64)
    s_fconv = np.concatenate(cat("s_fconv"), 1)
    return (y_prompt, y_sample, p_ssm, p_sconv, p_hg, p_s5r, p_s5i, p_fconv,
            s_ssm, s_sconv, s_hg, s_s5r, s_s5i, s_fconv)
```
